# Optimizing a Trainium2 kernel written in Bass

```python
import jax, jax.numpy as jnp
from jax import lax
import numpy as np

D_MODEL = 4096
BATCH = 1
SEQ = 8192
DEPTH = 1

MEM_LEN = 256

POOL_WINDOWS = (2, 4, 8, 16)
POOL_GROUPS = 4
POOL_WIDTH = D_MODEL
POOL_GROUP_DIM = POOL_WIDTH // POOL_GROUPS

RET_QK_DIM = 256
RET_HEADS = D_MODEL // RET_QK_DIM
RET_V_DIM = 2 * RET_QK_DIM
RET_QK_WIDTH = RET_HEADS * RET_QK_DIM
RET_V_WIDTH = RET_HEADS * RET_V_DIM
RET_CHUNK = 128

MEM_HEADS = 4
MEM_WIDTH = D_MODEL
MEM_HEAD_DIM = MEM_WIDTH // MEM_HEADS

ROPE_BASE = 10000.0
NORM_EPS = 1e-6

IN_SPLITS = (POOL_WIDTH, POOL_WIDTH,
             RET_QK_WIDTH, RET_QK_WIDTH, RET_V_WIDTH, RET_V_WIDTH,
             MEM_WIDTH, MEM_WIDTH,
             D_MODEL, D_MODEL, D_MODEL)
IN_WIDTH = sum(IN_SPLITS)
IN_OFFSETS = tuple(int(o) for o in np.cumsum(IN_SPLITS)[:-1])

kernel_name = "hybrid_pool_retention_memory_gated_block"


def rmsnorm(x, g):
    xf = x.astype(jnp.float32)
    y = xf * lax.rsqrt(jnp.mean(xf * xf, axis=-1, keepdims=True) + NORM_EPS)
    return (y * g.astype(jnp.float32)).astype(x.dtype)


def causal_multiscale_pool(u, w_group, scale):
    B, S, W = u.shape
    uf = u.astype(jnp.float32)
    c = jnp.concatenate([jnp.zeros((B, 1, W), jnp.float32), jnp.cumsum(uf, axis=1)], axis=1)
    c = c.reshape(B, S + 1, POOL_GROUPS, POOL_GROUP_DIM)
    t = jnp.arange(S)
    pooled = []
    for gi, w in enumerate(POOL_WINDOWS):
        cg = c[:, :, gi]
        lo = cg[:, jnp.maximum(t + 1 - w, 0)]
        cnt = jnp.minimum(t + 1, w).astype(jnp.float32)
        pooled.append((cg[:, 1:] - lo) / cnt[None, :, None])
    pooled = jnp.stack(pooled, axis=2)
    mixed = pooled - uf.reshape(B, S, POOL_GROUPS, POOL_GROUP_DIM)
    y = jnp.einsum('bsgd,gde->bsge', mixed, w_group.astype(jnp.float32))
    y = y.reshape(B, S, W) * scale.astype(jnp.float32)
    return y.astype(u.dtype)


def rope(x, pos):
    half = x.shape[-1] // 2
    inv = ROPE_BASE ** (-jnp.arange(half, dtype=jnp.float32) / half)
    ang = pos[:, None] * inv[None, :]
    cos = jnp.cos(ang)[None, :, None, :]
    sin = jnp.sin(ang)[None, :, None, :]
    xf = x.astype(jnp.float32)
    x1, x2 = xf[..., :half], xf[..., half:]
    return jnp.concatenate([x1 * cos - x2 * sin, x2 * cos + x1 * sin], axis=-1)


def chunkwise_retention(q, k, v):
    B, S, H, dk = q.shape
    dv = v.shape[-1]
    C = RET_CHUNK
    N = S // C
    lg = jnp.log1p(-(2.0 ** (-5.0 - jnp.arange(H, dtype=jnp.float32))))
    idx = jnp.arange(C, dtype=jnp.float32)
    diff = idx[:, None] - idx[None, :]
    intra = jnp.where(diff >= 0, jnp.exp(jnp.maximum(diff, 0.0)[None] * lg[:, None, None]), 0.0)
    q_decay = jnp.exp((idx + 1.0)[None, :] * lg[:, None])
    k_decay = jnp.exp((C - 1.0 - idx)[None, :] * lg[:, None])
    chunk_decay = jnp.exp(C * lg)

    def to_chunks(a):
        d = a.shape[-1]
        return a.astype(jnp.float32).reshape(B, N, C, H, d).transpose(1, 0, 3, 2, 4)

    def step(state, xs):
        qc, kc, vc = xs
        scores = jnp.einsum('bhid,bhjd->bhij', qc, kc) * intra[None]
        inner = jnp.einsum('bhij,bhje->bhie', scores, vc)
        cross = jnp.einsum('bhid,bhde->bhie', qc * q_decay[None, :, :, None], state)
        new_state = state * chunk_decay[None, :, None, None] + jnp.einsum(
            'bhjd,bhje->bhde', kc * k_decay[None, :, :, None], vc)
        return new_state, inner + cross

    s0 = jnp.zeros((B, H, dk, dv), jnp.float32)
    _, o = lax.scan(step, s0, (to_chunks(q), to_chunks(k), to_chunks(v)))
    return o.transpose(1, 0, 3, 2, 4).reshape(B, S, H, dv)


def memory_cross_attention(qm, km, vm):
    scores = jnp.einsum('bshd,bmhd->bhsm', qm.astype(jnp.float32), km.astype(jnp.float32))
    p = jax.nn.softmax(scores * (MEM_HEAD_DIM ** -0.5), axis=-1)
    return jnp.einsum('bhsm,bmhd->bshd', p.astype(vm.dtype), vm)


def setup_inputs(seed: int = 0) -> dict:
    key = jax.random.key(seed)
    ks = jax.random.split(key, 16)
    f32 = jnp.float32

    def nrm(k, shape, fan_in):
        return jax.random.normal(k, shape, f32) * (fan_in ** -0.5)

    return {
        "x": jax.random.normal(ks[0], (BATCH, SEQ, D_MODEL), f32),
        "mem": jax.random.normal(ks[1], (BATCH, MEM_LEN, D_MODEL), f32),
        "norm_in": 1.0 + 0.02 * jax.random.normal(ks[2], (DEPTH, D_MODEL), f32),
        "norm_mem": 1.0 + 0.02 * jax.random.normal(ks[3], (DEPTH, D_MODEL), f32),
        "w_in": nrm(ks[4], (DEPTH, D_MODEL, IN_WIDTH), D_MODEL),
        "w_pool_group": nrm(ks[5], (DEPTH, POOL_GROUPS, POOL_GROUP_DIM, POOL_GROUP_DIM), POOL_GROUP_DIM),
        "pool_scale": 1.0 + 0.02 * jax.random.normal(ks[6], (DEPTH, POOL_WIDTH), f32),
        "w_mem_k": nrm(ks[7], (DEPTH, D_MODEL, MEM_WIDTH), D_MODEL),
        "w_mem_v": nrm(ks[8], (DEPTH, D_MODEL, MEM_WIDTH), D_MODEL),
        "w_proj_pool": nrm(ks[9], (DEPTH, POOL_WIDTH, D_MODEL), POOL_WIDTH),
        "w_proj_ret": nrm(ks[10], (DEPTH, RET_V_WIDTH, D_MODEL), RET_V_WIDTH),
        "w_proj_mem": nrm(ks[11], (DEPTH, MEM_WIDTH, D_MODEL), MEM_WIDTH),
        "w_out": nrm(ks[12], (DEPTH, D_MODEL, D_MODEL), D_MODEL),
        "norm_f": 1.0 + 0.02 * jax.random.normal(ks[13], (D_MODEL,), f32),
    }


def reference(x, mem, norm_in, norm_mem, w_in, w_pool_group, pool_scale, w_mem_k, w_mem_v,
              w_proj_pool, w_proj_ret, w_proj_mem, w_out, norm_f):
    B, S, _ = x.shape
    M = mem.shape[1]
    pos = jnp.arange(S, dtype=jnp.float32)
    for l in range(DEPTH):
        h = rmsnorm(x, norm_in[l])
        z = h @ w_in[l]
        (u_pool, g_pool, q, k, v, g_ret, q_mem, g_mem,
         a_pool, a_ret, a_mem) = jnp.split(z, IN_OFFSETS, axis=-1)

        pool_out = causal_multiscale_pool(u_pool, w_pool_group[l], pool_scale[l]) * jax.nn.silu(g_pool)
        branch_pool = pool_out @ w_proj_pool[l]

        qh = rope(q.reshape(B, S, RET_HEADS, RET_QK_DIM), pos)
        kh = rope(k.reshape(B, S, RET_HEADS, RET_QK_DIM), pos) * (RET_QK_DIM ** -0.5)
        vh = v.reshape(B, S, RET_HEADS, RET_V_DIM)
        o = chunkwise_retention(qh, kh, vh)
        o = o * lax.rsqrt(jnp.mean(o * o, axis=-1, keepdims=True) + NORM_EPS)
        ret_out = o.reshape(B, S, RET_V_WIDTH).astype(x.dtype) * jax.nn.silu(g_ret)
        branch_ret = ret_out @ w_proj_ret[l]

        memn = rmsnorm(mem, norm_mem[l])
        km = (memn @ w_mem_k[l]).reshape(B, M, MEM_HEADS, MEM_HEAD_DIM)
        vm = (memn @ w_mem_v[l]).reshape(B, M, MEM_HEADS, MEM_HEAD_DIM)
        mo = memory_cross_attention(q_mem.reshape(B, S, MEM_HEADS, MEM_HEAD_DIM), km, vm)
        mem_out = mo.reshape(B, S, MEM_WIDTH) * jax.nn.silu(g_mem)
        branch_mem = mem_out @ w_proj_mem[l]

        merged = (jax.nn.sigmoid(a_pool) * branch_pool
                  + jax.nn.sigmoid(a_ret) * branch_ret
                  + jax.nn.sigmoid(a_mem) * branch_mem)
        x = x + merged @ w_out[l]
    return rmsnorm(x, norm_f)
```

```python
import numpy as np
import ml_dtypes
import concourse.bass as bass
import concourse.mybir as mybir
from concourse.bass_utils import run_bass_kernel_spmd

F32 = mybir.dt.float32
BF16 = mybir.dt.bfloat16
ALU = mybir.AluOpType
AF = mybir.ActivationFunctionType
AX = mybir.AxisListType
P = 128
TB = 512
NT = TB // P
WN = 256
EPS = 1e-6
MEM_LEN = 256
POOL_WINDOWS = (2, 4, 8, 16)


class Cfg:
    def __init__(self, D, S, ncores=8):
        self.D, self.S = D, S
        self.NCORES = ncores
        self.SC = S // ncores
        self.KC = D // P
        self.H = D // 256
        self.GB = D // 4 // P
        self.MB = D // 4 // P
        self.NB = self.SC // TB
        self.NPRE = (S - self.SC) // TB
        D_ = D
        self.OFF_U, self.OFF_G = 0, D_
        self.OFF_Q, self.OFF_K = 2 * D_, 3 * D_
        self.OFF_V, self.OFF_GR = 4 * D_, 6 * D_
        self.OFF_QM, self.OFF_GM = 8 * D_, 9 * D_
        self.OFF_A = 10 * D_
        self.INW = 13 * D_


class Tk:
    __slots__ = ("key", "sem", "val")

    def __init__(self, key, sem, val):
        self.key, self.sem, self.val = key, sem, val


class Buf:
    def __init__(self, name=""):
        self.name = name
        self.w = None
        self.r = {}


class Chan:
    def __init__(self, key, sem, unit=16):
        self.key, self.sem, self.count, self.unit = key, sem, 0, unit
        self.in_barrier = True


ENGS = ("pe", "act", "dve", "pool", "sp")


class _Proxy:
    def __getattr__(self, name):
        def f(*a, **k):
            return (name, a, k)
        return f


PROXY = _Proxy()


def _replay(c):
    return lambda e: getattr(e, c[0])(*c[1], **c[2])


class Rec:
    def __init__(self, esem):
        self.q = {e: [] for e in ENGS}
        self.cnt = {e: 0 for e in ENGS}
        self.waited = {e: {} for e in ENGS}
        self.esem = esem
        self.chans = []

    def chan(self, sem, unit=16):
        c = Chan("ch%d" % len(self.chans), sem, unit)
        self.chans.append(c)
        return c

    def _wait(self, eng, tk):
        if tk is None:
            return
        if eng == "pe" and tk.key == "pe":
            return
        w = self.waited[eng]
        if w.get(tk.key, 0) >= tk.val:
            return
        w[tk.key] = tk.val
        sem, val = tk.sem, tk.val
        self.q[eng].append(lambda e: e.wait_ge(sem, val))

    def _deps(self, eng, reads, writes):
        for b in reads:
            self._wait(eng, b.w)
        for b in writes:
            self._wait(eng, b.w)
            for tk in list(b.r.values()):
                self._wait(eng, tk)

    def _commit(self, tk, reads, writes):
        for b in reads:
            b.r[tk.key] = tk
        for b in writes:
            b.w = tk
            b.r = {}

    def op(self, eng, fns, reads=(), writes=()):
        if not isinstance(fns, (list, tuple)):
            fns = [fns]
        calls = [f(PROXY) for f in fns]
        self._deps(eng, reads, writes)
        self.cnt[eng] += 1
        sem = self.esem[eng]
        q = self.q[eng]
        for c in calls[:-1]:
            q.append(_replay(c))
        last = _replay(calls[-1])
        q.append(lambda e: last(e).then_inc(sem, 1))
        tk = Tk(eng, sem, self.cnt[eng])
        self._commit(tk, reads, writes)
        return tk

    def dma(self, qeng, ch, out_ap, in_ap, reads=(), writes=()):
        self._deps(qeng, reads, writes)
        if ch.count:
            self._wait(qeng, Tk(ch.key, ch.sem, 16 * ch.count))
        ch.count += 1
        sem = ch.sem
        self.q[qeng].append(lambda e: e.dma_start(out=out_ap, in_=in_ap).then_inc(sem, 16))
        tk = Tk(ch.key, sem, 16 * ch.count)
        self._commit(tk, reads, writes)
        return tk

    def coll(self, ch, in_ap, out_ap, ranks, reads=(), writes=()):
        self._deps("pool", reads, writes)
        if ch.count:
            self._wait("pool", Tk(ch.key, ch.sem, ch.count))
        ch.count += 1
        sem = ch.sem
        self.q["pool"].append(lambda e: e.collective_compute(
            "AllGather", ALU.bypass, replica_groups=[list(range(ranks))],
            ins=[in_ap], outs=[out_ap]).then_inc(sem, 1))
        tk = Tk(ch.key, sem, ch.count)
        self._commit(tk, reads, writes)
        return tk

    def barrier(self):
        for e in ENGS:
            for e2 in ("pe", "act", "dve", "pool"):
                if self.cnt[e2] and e2 != e:
                    self._wait(e, Tk(e2, self.esem[e2], self.cnt[e2]))
            for c in self.chans:
                if c.count and c.in_barrier:
                    self._wait(e, Tk(c.key, c.sem, c.unit * c.count))

    def final_wait(self, eng):
        for e2 in ("pe", "act", "dve", "pool"):
            if self.cnt[e2] and e2 != eng:
                self._wait(eng, Tk(e2, self.esem[e2], self.cnt[e2]))
        for c in self.chans:
            if c.count:
                self._wait(eng, Tk(c.key, c.sem, c.unit * c.count))


def build(cfg):
    D, S, KC, H, GB, MB, NB = cfg.D, cfg.SC, cfg.KC, cfg.H, cfg.GB, cfg.MB, cfg.NB
    NR = cfg.NCORES
    NPRE = cfg.NPRE
    nc = bass.Bass("TRN2", target_bir_lowering=False)

    def din(name, shape, dt=F32):
        return nc.dram_tensor(name, list(shape), dt, kind="ExternalInput").ap()

    def dscr(name, shape, dt=BF16):
        return nc.dram_tensor(name, list(shape), dt).ap()

    x_d = din("x", [S, D])
    mem_d = din("mem", [MEM_LEN, D])
    y_d = nc.dram_tensor("y", [S, D], F32, kind="ExternalOutput").ap()
    nin_d = din("norm_in_rep", [P, D])
    nmem_d = din("norm_mem_rep", [P, D])
    nf_d = din("norm_f_rep", [P, D])
    pscale_d = din("pscale", [P, KC])
    cos_d = din("cos_t", [P, S])
    sin_d = din("sin_t", [P, S])
    mask_d = din("maskT", [P, H * P])
    qdec_d = din("qdec", [P, H * P])
    kdec_d = din("kdec", [P, H])
    cdec_d = din("cdec", [P, H])
    rc_d = din("rc", [P, 64])
    ident_d = din("ident", [P, P], BF16)
    xp_d = din("x_prev", [max(NPRE, 1) * TB, D])
    cosp_d = din("cosp_t", [P, max(NPRE, 1) * TB])
    sinp_d = din("sinp_t", [P, max(NPRE, 1) * TB])

    wspecs = {
        "w_in": (cfg.INW // WN, KC, WN),
        "w_pg": (4 * GB, GB, P),
        "w_mk": (D // WN, KC, WN),
        "w_mv": (D // WN, KC, WN),
        "w_pp": (D // WN, KC, WN),
        "w_pr0": (D // WN, KC, WN),
        "w_pr1": (D // WN, KC, WN),
        "w_pm": (D // WN, KC, WN),
        "w_o": (D // WN, KC, WN),
    }
    wf, wb = {}, {}
    for nm, (ntl, kc, n) in wspecs.items():
        wf[nm] = din(nm, [ntl, P, kc * n])
        wb[nm] = [dscr("%s_b%d" % (nm, i), [min(16, ntl - i), P, kc * n]) for i in range(0, ntl, 16)]

    kmT_s = dscr("kmT_s", [P, KC, MEM_LEN])
    vm_s = dscr("vm_s", [2, P, D])
    S_s = dscr("S_s", [H, P, 2 * 512], F32)

    retT_s = dscr("retT_s", [P, 2 * KC, TB])
    poolT_s = dscr("poolT_s", [P, KC, TB])
    memT_s = dscr("memT_s", [P, KC, TB])
    sig_s = dscr("sig_s", [P, 3 * KC, TB])

    from contextlib import ExitStack
    es = ExitStack()
    AW = 53200
    arena = es.enter_context(nc.sbuf_tensor("arena", [P, AW], F32))
    NBK = 6
    banks = [es.enter_context(nc.psum_tensor("pb%d" % i, [P, 512], F32)) for i in range(NBK)]
    ptrs = [(es.enter_context(nc.psum_tensor("ptr%d" % i, [P, 1024], BF16)), Buf("ptr%d" % i)) for i in range(2)]
    ptr_i = [0]
    sem_names = ["s_pe", "s_act", "s_dve", "s_pool"]
    sems = [es.enter_context(nc.semaphore(n)) for n in sem_names]
    rec = Rec({"pe": sems[0], "act": sems[1], "dve": sems[2], "pool": sems[3]})

    def newchan(name):
        return rec.chan(es.enter_context(nc.semaphore(name)))

    class Carve:
        def __init__(self, base):
            self.off = base

        def f32(self, n):
            a = arena[:, self.off:self.off + n]
            self.off += n
            return a

        def bf16(self, n):
            w = (n + 1) // 2
            a = arena[:, self.off:self.off + w].bitcast(BF16)
            self.off += w
            return a

    NSPLIT = 2
    cv = Carve(0)
    maskT = cv.f32(H * P)
    qdec = cv.f32(H * P)
    kdec = cv.f32(H)
    cdec = cv.f32(H)
    rc = cv.f32(64)
    pscale = cv.f32(KC)
    ident = cv.bf16(P)
    consts_b = Buf("consts")
    wts = [(cv.bf16(KC * WN), Buf("wt%d" % i), [newchan("c_wt%d_%d" % (i, j)) for j in range(NSPLIT)]) for i in range(3)]
    wgs = [(cv.bf16(GB * P), Buf("wg%d" % i), newchan("c_wg%d" % i)) for i in range(4)]
    sgts = [(cv.bf16(TB), Buf("sgt%d" % i), newchan("c_sg%d" % i)) for i in range(4)]
    tmpB = cv.f32(TB)
    tmpB_b = Buf("tmpB")
    small = cv.f32(32)
    tails = [(cv.bf16(KC * 16).rearrange("p (k t) -> p k t", k=KC), Buf("tail%d" % i)) for i in range(2)]
    small_b = [Buf("sm%d" % i) for i in range(8)]
    UNION = cv.off

    cA = Carve(UNION)
    hT = cA.bf16(KC * TB).rearrange("p (k t) -> p k t", k=KC)
    hT_b = Buf("hT")
    WREG = cA.off
    cP = Carve(WREG)
    css = [(cP.f32(TB), cP.f32(TB), Buf("cs%d" % i)) for i in range(2)]
    t1 = cP.f32(TB); t1_b = Buf("t1")
    t2 = cP.f32(TB); t2_b = Buf("t2")
    kT = cP.bf16(2 * TB).rearrange("p (h t) -> p h t", h=2); kT_b = Buf("kT")
    kd = cP.bf16(NT * 256).rearrange("p (c d) -> p c d", c=NT); kd_b = Buf("kd")
    vv = cP.bf16(NT * 512).rearrange("p (c e) -> p c e", c=NT); vv_b = Buf("v")
    vT = cP.bf16(4 * TB).rearrange("p (b t) -> p b t", b=4); vT_b = Buf("vT")
    Sts = [(cP.f32(1024).rearrange("p (h e) -> p h e", h=2), Buf("S%d" % i)) for i in range(2)]
    NREG = cP.off
    cN = Carve(NREG)
    xt = cN.f32(D); xt_b = Buf("xt")
    hb = cN.bf16(D); hb_b = Buf("hb")
    vecrep = cN.f32(D); vecrep_b = Buf("vecrep")
    XW0 = cN.off
    memnT = cN.bf16(KC * MEM_LEN).rearrange("p (k t) -> p k t", k=KC); memnT_b = Buf("memnT")
    stg = cN.bf16(WN); stg_b = Buf("stg")
    cX = Carve(XW0)
    n_extra = 0
    while cX.off + KC * WN // 2 <= AW and n_extra < 3:
        wts.append((cX.bf16(KC * WN), Buf("wtx%d" % n_extra), [newchan("c_wtx%d_%d" % (n_extra, j)) for j in range(NSPLIT)]))
        n_extra += 1
    wt_depth = [3]
    cM = Carve(NREG)
    qT = cM.bf16(2 * TB).rearrange("p (h t) -> p h t", h=2); qT_b = Buf("qT")
    qdT = cM.bf16(2 * TB).rearrange("p (h t) -> p h t", h=2); qdT_b = Buf("qdT")
    sgT = cM.bf16(4 * TB).rearrange("p (b t) -> p b t", b=4); sgT_b = Buf("sgT")
    Sb = cM.bf16(1024).rearrange("p (h e) -> p h e", h=2); Sb_b = Buf("Sb")
    PT = cM.bf16(P); PT_b = Buf("PT")
    retn = cM.bf16(512); retn_b = Buf("retn")
    retT = cM.bf16(4 * TB).rearrange("p (b t) -> p b t", b=4); retT_b = Buf("retT")
    junk = cM.bf16(512); junk_b = Buf("junk")

    UW = 16 + TB
    sg2 = cM.bf16(2 * TB).rearrange("p (j t) -> p j t", j=2); sg2_b = Buf("sg2")
    po = cM.bf16(2 * TB).rearrange("p (j t) -> p j t", j=2); po_b = Buf("po")
    cPl = Carve(cM.off)
    ubufs = [(cPl.f32(UW), Buf("u%d" % i)) for i in range(3)]
    mixT = cPl.bf16(GB * TB).rearrange("p (k t) -> p k t", k=GB); mixT_b = Buf("mixT")
    t16 = cPl.f32(16); t16_b = Buf("t16")
    cMm = Carve(cM.off)
    kmTh = cMm.bf16(MB * MEM_LEN).rearrange("p (k m) -> p k m", k=MB); kmTh_b = Buf("kmTh")
    vmh = cMm.bf16(2 * MB * P).rearrange("p (m e) -> p m e", m=2); vmh_b = Buf("vmh")
    qmT = cMm.bf16(MB * TB).rearrange("p (k t) -> p k t", k=MB); qmT_b = Buf("qmT")
    pex = cMm.f32(MEM_LEN); pex_b = Buf("pex")
    pn = cMm.bf16(MEM_LEN); pn_b = Buf("pn")
    pT = cMm.bf16(2 * TB).rearrange("p (m t) -> p m t", m=2); pT_b = Buf("pT")
    cM.off = max(cPl.off, cMm.off)
    own_extra = 1 if (n_extra >= 1 and cM.off <= XW0) else 0
    cB = Carve(UNION)
    HB = KC // 2
    acc = cB.f32(HB * TB).rearrange("p (b t) -> p b t", b=HB); acc_b = Buf("acc")
    pin = cB.bf16(KC * TB).rearrange("p (k t) -> p k t", k=KC); pin_b = Buf("pin")
    mT = cB.bf16(KC * TB).rearrange("p (k t) -> p k t", k=KC); mT_b = Buf("mT")
    cF = Carve(UNION)
    xo4 = cF.f32(NT * D).rearrange("p (t d) -> p t d", t=NT)
    xo_b = [Buf("xo%d" % i) for i in range(NT)]
    assert cF.off <= UNION + (HB * TB + KC * TB // 2), "final stage must not overlap mT"
    cF = Carve(cB.off)
    hb2 = cF.bf16(D); hb2_b = Buf("hb2")
    vecrep2 = cF.f32(D); vecrep2_b = Buf("vecrep2")
    assert max(cA.off, cP.off, cN.off, cM.off, cB.off, cF.off) <= AW, (cA.off, cP.off, cN.off, cM.off, cB.off, cF.off)

    bank_b = [Buf("bank%d" % i) for i in range(NBK)]
    bank_i = [0]

    def nb():
        i = bank_i[0] % NBK
        bank_i[0] += 1
        return banks[i], bank_b[i]

    ch_ld = [newchan("c_ld%d" % i) for i in range(4)]
    ch_st = [newchan("c_st%d" % i) for i in range(4)]
    ch_cast = [newchan("c_cast%d" % i) for i in range(4)]
    ch_sst = [newchan("c_sst%d" % i) for i in range(2)]
    ch_xq = newchan("c_xq")
    ld_i, st_i = [0], [0]

    def load(out_ap, in_ap, reads=(), writes=()):
        c = ch_ld[ld_i[0] % 4]; ld_i[0] += 1
        return rec.dma("sp", c, out_ap, in_ap, reads, writes)

    def store(out_ap, in_ap, reads=(), writes=()):
        c = ch_st[st_i[0] % 4]; st_i[0] += 1
        return rec.dma("pool", c, out_ap, in_ap, reads, writes)

    wb_b = {nm: [Buf("%s_%d" % (nm, t)) for t in range(wspecs[nm][0])] for nm in wspecs}
    for c_ in ch_cast:
        c_.in_barrier = False
    kmT_sb, vm_sb, S_sb = Buf("kmT_s"), Buf("vm_s"), [Buf("S_s%d" % h) for h in range(H)]

    retT_sb, poolT_sb, memT_sb, sig_sb = Buf("retT_s"), Buf("poolT_s"), Buf("memT_s"), Buf("sig_s")
    y_b = Buf("y")

    for (ap, src) in ((maskT, mask_d), (qdec, qdec_d), (kdec, kdec_d), (cdec, cdec_d), (rc, rc_d),
                      (pscale, pscale_d), (ident, ident_d)):
        load(ap, src, writes=[consts_b])
    ci = 0
    NCAST = 3
    order = []
    for h in range(H):
        order.append(("w_in", (cfg.OFF_K + h * 256) // WN))
        order += [("w_in", (cfg.OFF_V + h * 512) // WN + vt) for vt in range(2)]
    seen = set(order)
    for nm in ("w_mk", "w_mv", "w_pg", "w_in", "w_pp", "w_pr0", "w_pr1", "w_pm", "w_o"):
        for tl in range(wspecs[nm][0]):
            if (nm, tl) not in seen:
                order.append((nm, tl))
    for nm, tl in order:
        row = wspecs[nm][1] * wspecs[nm][2]
        cw = min(row, 4096)
        for c0 in range(0, row, cw):
            rec.dma("pool", ch_cast[ci % NCAST], wb[nm][tl // 16][tl % 16][:, c0:c0 + cw], wf[nm][tl][:, c0:c0 + cw],
                    writes=[wb_b[nm][tl]])
            ci += 1

    wt_i = [0]

    def _load_tile(nm, tidx, ap, b, chs):
        row = wspecs[nm][1] * wspecs[nm][2]
        src = wb[nm][tidx // 16][tidx % 16]
        hw = row // NSPLIT if row >= 2048 else row
        for i, c0 in enumerate(range(0, row, hw)):
            rec.dma("sp", chs[i % len(chs)], ap[:, c0:c0 + hw], src[:, c0:c0 + hw], reads=[wb_b[nm][tidx]], writes=[b])

    def load_wt(nm, tidx):
        ap, b, chs = wts[wt_i[0] % wt_depth[0]]
        wt_i[0] += 1
        _load_tile(nm, tidx, ap, b, chs)
        return ap.rearrange("p (k n) -> p k n", n=WN), b

    wg_i = [0]

    def load_wg(tidx):
        ap, b, ch = wgs[wg_i[0] % 4]
        wg_i[0] += 1
        src = wb["w_pg"][tidx // 16][tidx % 16]
        rec.dma("act", ch, ap, src, reads=[wb_b["w_pg"][tidx]], writes=[b])
        return ap.rearrange("p (k n) -> p k n", n=P), b

    def mm_group(out_ap, out_b, pairs, reads):
        n = len(pairs)
        fns = []
        for i, (l, r) in enumerate(pairs):
            fns.append((lambda l=l, r=r, i=i: (lambda e: e.matmul(out_ap, l, r, start=(i == 0), stop=(i == n - 1))))())
        return rec.op("pe", fns, reads=reads, writes=[out_b])

    def proj_fm(wt, wt_b, j, rhs3, rhs_b, ncols, kcn=None):
        kcn = kcn or KC
        ps, psb = nb()
        o = ps[:, 0:ncols]
        mm_group(o, psb, [(wt[:, kc, j * P:(j + 1) * P], rhs3[:, kc, 0:ncols]) for kc in range(kcn)],
                 reads=[wt_b, rhs_b])
        return o, psb

    def transposes(srcs, src_b):
        ptr_t, ptr_b = ptrs[ptr_i[0] % 2]
        ptr_i[0] += 1
        fns = []
        for i, s_ in enumerate(srcs):
            fns.append((lambda s_=s_, i=i: (lambda e: e.transpose(ptr_t[:, i * P:(i + 1) * P], s_, ident)))())
        rec.op("pe", fns, reads=[src_b, consts_b], writes=[ptr_b])
        return ptr_t, ptr_b

    def rstd_from_ssq(ssq_ap, ssq_b, n, out_ap, out_b):
        rec.op("dve", lambda e: e.tensor_scalar(out_ap, ssq_ap, 1.0 / n, EPS, ALU.mult, ALU.add),
               reads=[ssq_b], writes=[out_b])
        rec.op("act", lambda e: e.activation(out_ap, out_ap, AF.Sqrt), reads=[out_b], writes=[out_b])
        rec.op("dve", lambda e: e.reciprocal(out_ap, out_ap), reads=[out_b], writes=[out_b])

    alt = [0]

    def evac_copy(out_ap, out_b, in_ap, in_b):
        alt[0] += 1
        if alt[0] % 2:
            rec.op("act", lambda e: e.copy(out_ap, in_ap), reads=[in_b], writes=[out_b])
        else:
            rec.op("dve", lambda e: e.tensor_copy(out_ap, in_ap), reads=[in_b], writes=[out_b])

    def norm_tile_to_T(src_rows, dstT, dstT_b, tt, vrep, vrep_b, xt_, xt_b_, hb_, hb_b_, act_q=False):
        if act_q:
            rec.dma("act", ch_xq, xt_, src_rows, writes=[xt_b_])
        else:
            load(xt_, src_rows, writes=[xt_b_])
        ssq, ssq_b = small[:, 0:1], small_b[0]
        rs, rs_b = small[:, 1:2], small_b[1]
        rec.op("act", lambda e: e.activation(hb_, xt_, AF.Square, accum_out=ssq),
               reads=[xt_b_], writes=[hb_b_, ssq_b])
        rstd_from_ssq(ssq, ssq_b, D, rs, rs_b)
        rec.op("dve", lambda e: e.scalar_tensor_tensor(hb_, xt_, rs, vrep, ALU.mult, ALU.mult),
               reads=[xt_b_, rs_b, vrep_b], writes=[hb_b_])
        for k0 in range(0, KC, 4):
            ptr_t, ptr_b = transposes([hb_[:, (k0 + i) * P:(k0 + i + 1) * P] for i in range(4)], hb_b_)
            evac_copy(dstT[:, k0:k0 + 4, tt * P:(tt + 1) * P],
                      dstT_b, ptr_t[:, 0:4 * P].rearrange("p (k t) -> p k t", k=4), ptr_b)

    rec.barrier()

    msc = float((MB * P) ** -0.5)


    cs_cur = [None]
    cs_i = [0]

    def norm_stage(blk, xsrc=None, csrc=None, ssrc=None, prefix=False):
        xsrc = x_d if xsrc is None else xsrc
        csrc = cos_d if csrc is None else csrc
        ssrc = sin_d if ssrc is None else ssrc
        t0 = blk * TB
        if not prefix:
            load(vecrep, nin_d, writes=[vecrep_b])
        for tt in range(NT):
            norm_tile_to_T(xsrc[t0 + tt * P:t0 + (tt + 1) * P, :], hT, hT_b, tt, vecrep, vecrep_b, xt, xt_b, hb, hb_b,
                           act_q=prefix)
        if not prefix:
            rec.barrier()
        cosb_, sinb_, csb_ = css[cs_i[0] % 2]
        cs_i[0] += 1
        load(cosb_, csrc[:, t0:t0 + TB], writes=[csb_])
        load(sinb_, ssrc[:, t0:t0 + TB], writes=[csb_])
        cs_cur[0] = (cosb_, sinb_, csb_)

    def rope_pair(tile_idx, dst, dst_b):
        cosb, sinb, cs_b = cs_cur[0]
        wt, wt_b = load_wt("w_in", tile_idx)
        A, Ab = proj_fm(wt, wt_b, 0, hT, hT_b, TB)
        Bq, Bb = proj_fm(wt, wt_b, 1, hT, hT_b, TB)
        rec.op("dve", lambda e: e.tensor_tensor(t1, A, cosb, ALU.mult), reads=[Ab, cs_b], writes=[t1_b])
        rec.op("dve", lambda e: e.tensor_tensor(t2, Bq, sinb, ALU.mult), reads=[Bb, cs_b], writes=[t2_b])
        rec.op("dve", lambda e: e.tensor_tensor(dst[:, 0, :], t1, t2, ALU.subtract),
               reads=[t1_b, t2_b], writes=[dst_b])
        rec.op("dve", lambda e: e.tensor_tensor(t1, Bq, cosb, ALU.mult), reads=[Bb, cs_b], writes=[t1_b])
        rec.op("dve", lambda e: e.tensor_tensor(t2, A, sinb, ALU.mult), reads=[Ab, cs_b], writes=[t2_b])
        rec.op("dve", lambda e: e.tensor_tensor(dst[:, 1, :], t1, t2, ALU.add),
               reads=[t1_b, t2_b], writes=[dst_b])

    kT_f = kT.rearrange("p h t -> p (h t)")
    kd_f = kd.rearrange("p c d -> p (c d)")
    vv_f = vv.rearrange("p c e -> p (c e)")

    def state_update(h, c):
        St, St_b = Sts[h % 2]
        for hf in range(2):
            p2, p2b = nb()
            o2 = p2[:, 0:512]
            mm_group(o2, p2b, [(kd[:, c, hf * P:(hf + 1) * P], vv[:, c, :])], reads=[kd_b, vv_b])
            rec.op("dve", lambda e: e.scalar_tensor_tensor(
                St[:, hf, :], St[:, hf, :], cdec[:, h:h + 1], o2, ALU.mult, ALU.add),
                reads=[St_b, p2b, consts_b], writes=[St_b])

    def kv_compute(h):
        rope_pair((cfg.OFF_K + h * 256) // WN, kT, kT_b)
        for c in range(NT):
            ptr_t, ptr_b = transposes([kT[:, hf, c * P:(c + 1) * P] for hf in range(2)], kT_b)
            rec.op("dve", lambda e: e.tensor_scalar(kd[:, c, :], ptr_t[:, 0:256], kdec[:, h:h + 1], None, ALU.mult),
                   reads=[ptr_b, consts_b], writes=[kd_b])
        for vt in range(2):
            wt, wt_b = load_wt("w_in", (cfg.OFF_V + h * 512) // WN + vt)
            for j in range(2):
                o, ob = proj_fm(wt, wt_b, j, hT, hT_b, TB)
                evac_copy(vT[:, vt * 2 + j, :], vT_b, o, ob)
        for c in range(NT):
            ptr_t, ptr_b = transposes([vT[:, eb, c * P:(c + 1) * P] for eb in range(4)], vT_b)
            evac_copy(vv[:, c, :], vv_b, ptr_t[:, 0:512], ptr_b)

    def kv_prefix(pj, h):
        kv_compute(h)
        St, St_b = Sts[h % 2]
        Sflat = St.rearrange("p h e -> p (h e)")
        if pj == 0:
            rec.op("dve", lambda e: e.memset(Sflat, 0.0), writes=[St_b])
        else:
            load(Sflat, S_s[h], reads=[S_sb[h]], writes=[St_b])
        for c in range(NT):
            state_update(h, c)
        rec.dma("act", ch_sst[h % 2], S_s[h], Sflat, reads=[St_b], writes=[S_sb[h]])

    def ret_pass2(blk, h):
        St, St_b = Sts[h % 2]
        Sflat = St.rearrange("p h e -> p (h e)")
        kv_compute(h)
        rope_pair((cfg.OFF_Q + h * 256) // WN, qT, qT_b)
        qdv = qdec[:, h * P:(h + 1) * P].unsqueeze(1).to_broadcast([P, NT, P])
        for hf in range(2):
            rec.op("dve", lambda e: e.tensor_tensor(
                qdT[:, hf, :].rearrange("p (c i) -> p c i", c=NT),
                qT[:, hf, :].rearrange("p (c i) -> p c i", c=NT), qdv, ALU.mult),
                reads=[qT_b, consts_b], writes=[qdT_b])
        for gt in range(2):
            wt, wt_b = load_wt("w_in", (cfg.OFF_GR + h * 512) // WN + gt)
            for j in range(2):
                o, ob = proj_fm(wt, wt_b, j, hT, hT_b, TB)
                rec.op("act", lambda e: e.activation(sgT[:, gt * 2 + j, :], o, AF.Silu), reads=[ob], writes=[sgT_b])
        if NPRE == 0 and blk == 0:
            rec.op("pool", lambda e: e.memset(Sflat, 0.0), writes=[St_b])
        else:
            load(Sflat, S_s[h], reads=[S_sb[h]], writes=[St_b])
        for c in range(NT):
            cs = slice(c * P, (c + 1) * P)
            ps, psb = nb()
            sc = ps[:, 0:P]
            mm_group(sc, psb, [(kT[:, hf, cs], qT[:, hf, cs]) for hf in range(2)], reads=[kT_b, qT_b])
            rec.op("dve", lambda e: e.tensor_tensor(PT, sc, maskT[:, h * P:(h + 1) * P], ALU.mult),
                   reads=[psb, consts_b], writes=[PT_b])
            rec.op("act", lambda e: e.copy(Sb.rearrange("p h e -> p (h e)"), Sflat), reads=[St_b], writes=[Sb_b])
            po_, pob = nb()
            o = po_[:, 0:512]
            mm_group(o, pob, [(PT, vv[:, c, :])] + [(qdT[:, hf, cs], Sb[:, hf, :]) for hf in range(2)],
                     reads=[PT_b, vv_b, qdT_b, Sb_b])
            ssq, ssq_b = small[:, 2:3], small_b[2]
            rs, rs_b = small[:, 3:4], small_b[3]
            rec.op("act", lambda e: e.activation(junk, o, AF.Square, accum_out=ssq),
                   reads=[pob], writes=[junk_b, ssq_b])
            rstd_from_ssq(ssq, ssq_b, 512, rs, rs_b)
            rec.op("dve", lambda e: e.tensor_scalar(retn, o, rs, None, ALU.mult), reads=[pob, rs_b], writes=[retn_b])
            ptr_t, ptr_b = transposes([retn[:, eb * P:(eb + 1) * P] for eb in range(4)], retn_b)
            rec.op("dve", lambda e: e.tensor_tensor(
                retT[:, :, cs], ptr_t[:, 0:512].rearrange("p (b i) -> p b i", b=4), sgT[:, :, cs], ALU.mult),
                reads=[ptr_b, sgT_b], writes=[retT_b])
            if not (blk == NB - 1 and c == NT - 1):
                state_update(h, c)
        if blk < NB - 1:
            store(S_s[h], Sflat, reads=[St_b], writes=[S_sb[h]])
        store(retT_s[:, h * 4:(h + 1) * 4, :], retT, reads=[retT_b], writes=[retT_sb])

    if NPRE == 0:
        rec.op("dve", lambda e: e.memset(tails[0][0], 0.0), writes=[tails[0][1]])
    if NPRE:
        load(vecrep, nin_d, writes=[vecrep_b])
        wt_depth[0] = 3 + n_extra
    for pj in range(NPRE):
        norm_stage(pj, xp_d, cosp_d, sinp_d, prefix=True)
        for h in range(H):
            kv_prefix(pj, h)
        if pj == NPRE - 1:
            rec.op("dve", lambda e: e.tensor_copy(tails[0][0], hT[:, :, TB - 16:TB]), reads=[hT_b], writes=[tails[0][1]])
    rec.barrier()
    wt_depth[0] = 3

    load(vecrep, nmem_d, writes=[vecrep_b])
    for mt in range(2):
        norm_tile_to_T(mem_d[mt * P:(mt + 1) * P, :], memnT, memnT_b, mt, vecrep, vecrep_b, xt, xt_b, hb, hb_b)
    for kt in range(D // WN):
        wt, wt_b = load_wt("w_mk", kt)
        for j in range(2):
            o, ob = proj_fm(wt, wt_b, j, memnT, memnT_b, MEM_LEN)
            evac_copy(stg, stg_b, o, ob)
            store(kmT_s[:, kt * 2 + j, :], stg, reads=[stg_b], writes=[kmT_sb])
    for vt in range(D // WN):
        wt, wt_b = load_wt("w_mv", vt)
        for mt in range(2):
            ps, psb = nb()
            o = ps[:, 0:WN]
            mm_group(o, psb, [(memnT[:, kc, mt * P:(mt + 1) * P], wt[:, kc, :]) for kc in range(KC)],
                     reads=[wt_b, memnT_b])
            evac_copy(stg, stg_b, o, psb)
            store(vm_s[mt][:, vt * WN:(vt + 1) * WN], stg, reads=[stg_b], writes=[vm_sb])
    rec.barrier()

    wt_depth[0] = 3 + own_extra
    for blk in range(NB):
        t0 = blk * TB
        hTt, hTt_b = tails[blk % 2]
        norm_stage(blk)

        for g in range(4):
            w = POOL_WINDOWS[g]
            for ut in range(GB * P // WN):
                wt, wt_b = load_wt("w_in", (cfg.OFF_U + g * GB * P) // WN + ut)
                for j in range(2):
                    ub_i = ut * 2 + j
                    o, ob = proj_fm(wt, wt_b, j, hT, hT_b, TB)
                    oh, ohb = proj_fm(wt, wt_b, j, hTt, hTt_b, 16)
                    ua, ua_b = ubufs[0]
                    rec.op("act", (lambda o=o: lambda e: e.copy(ua[:, 16:UW], o))(), reads=[ob], writes=[ua_b])
                    rec.op("act", (lambda oh=oh: lambda e: e.copy(ua[:, 0:16], oh))(), reads=[ohb], writes=[ua_b])
                    cur, cur_b = ua, ua_b
                    off, step, pi_ = 0, 1, 1
                    while step < w:
                        nx, nx_b = ubufs[pi_]
                        lo = off + step
                        rec.op("dve", (lambda nx=nx, cur=cur, lo=lo, step=step: lambda e: e.tensor_tensor(
                            nx[:, lo:UW], cur[:, lo:UW], cur[:, lo - step:UW - step], ALU.add))(),
                            reads=[cur_b], writes=[nx_b])
                        cur, cur_b = nx, nx_b
                        off = lo
                        step *= 2
                        pi_ = 3 - pi_
                    rec.op("dve", (lambda cur=cur, ub_i=ub_i: lambda e: e.scalar_tensor_tensor(
                        mixT[:, ub_i, :], cur[:, 16:UW], 1.0 / w, ua[:, 16:UW], ALU.mult, ALU.subtract))(),
                        reads=[cur_b, ua_b], writes=[mixT_b])
                    if blk == 0:
                        rec.op("dve", (lambda cur=cur: lambda e: e.tensor_tensor(
                            t16, cur[:, 16:32], rc[:, g * 16:(g + 1) * 16], ALU.mult))(),
                            reads=[cur_b, consts_b], writes=[t16_b])
                        rec.op("dve", (lambda ub_i=ub_i: lambda e: e.tensor_tensor(
                            mixT[:, ub_i, 0:16], t16, ua[:, 16:32], ALU.subtract))(),
                            reads=[t16_b, ua_b], writes=[mixT_b])
            for ot in range(GB * P // WN):
                wt, wt_b = load_wt("w_in", (cfg.OFF_G + g * GB * P) // WN + ot)
                for j in range(2):
                    o, ob = proj_fm(wt, wt_b, j, hT, hT_b, TB)
                    rec.op("act", (lambda o=o, j=j: lambda e: e.activation(sg2[:, j, :], o, AF.Silu))(),
                           reads=[ob], writes=[sg2_b])
                for j in range(2):
                    oblk = g * GB + ot * 2 + j
                    wg, wg_b = load_wg(oblk)
                    ps, psb = nb()
                    o = ps[:, 0:TB]
                    mm_group(o, psb, [(wg[:, kc, :], mixT[:, kc, :]) for kc in range(GB)], reads=[wg_b, mixT_b])
                    rec.op("dve", (lambda o=o, j=j, oblk=oblk: lambda e: e.scalar_tensor_tensor(
                        po[:, j, :], o, pscale[:, oblk:oblk + 1], sg2[:, j, :], ALU.mult, ALU.mult))(),
                        reads=[psb, sg2_b, consts_b], writes=[po_b])
                ob0 = g * GB + ot * 2
                store(poolT_s[:, ob0:ob0 + 2, :], po, reads=[po_b], writes=[poolT_sb])

        rec.barrier()
        for hm in range(4):
            load(kmTh, kmT_s[:, hm * MB:(hm + 1) * MB, :], reads=[kmT_sb], writes=[kmTh_b])
            load(vmh, vm_s[:, :, hm * MB * P:(hm + 1) * MB * P].rearrange("m p e -> p m e"),
                 reads=[vm_sb], writes=[vmh_b])
            for qt in range(MB * P // WN):
                wt, wt_b = load_wt("w_in", (cfg.OFF_QM + hm * MB * P) // WN + qt)
                for j in range(2):
                    o, ob = proj_fm(wt, wt_b, j, hT, hT_b, TB)
                    evac_copy(qmT[:, qt * 2 + j, :], qmT_b, o, ob)
            for c in range(NT):
                cs = slice(c * P, (c + 1) * P)
                ps, psb = nb()
                sc = ps[:, 0:MEM_LEN]
                mm_group(sc, psb, [(qmT[:, kc, cs], kmTh[:, kc, :]) for kc in range(MB)], reads=[qmT_b, kmTh_b])
                mx, mx_b = small[:, 4:5], small_b[4]
                sm, sm_b = small[:, 5:6], small_b[5]
                rec.op("dve", (lambda sc=sc: lambda e: e.reduce_max(mx, sc, AX.X))(), reads=[psb], writes=[mx_b])
                rec.op("dve", lambda e: e.tensor_scalar(mx, mx, -msc, None, ALU.mult), reads=[mx_b], writes=[mx_b])
                rec.op("act", (lambda sc=sc: lambda e: e.activation(pex, sc, AF.Exp, bias=mx, scale=msc, accum_out=sm))(),
                       reads=[psb, mx_b], writes=[pex_b, sm_b])
                rec.op("dve", lambda e: e.reciprocal(sm, sm), reads=[sm_b], writes=[sm_b])
                rec.op("dve", lambda e: e.tensor_scalar(pn, pex, sm, None, ALU.mult), reads=[pex_b, sm_b], writes=[pn_b])
                ptr_t, ptr_b = transposes([pn[:, m * P:(m + 1) * P] for m in range(2)], pn_b)
                rec.op("dve", (lambda cs=cs: lambda e: e.tensor_copy(
                    pT[:, :, cs], ptr_t[:, 0:256].rearrange("p (m i) -> p m i", m=2)))(),
                    reads=[ptr_b], writes=[pT_b])
            for dt_ in range(MB * P // WN):
                wt, wt_b = load_wt("w_in", (cfg.OFF_GM + hm * MB * P) // WN + dt_)
                for j in range(2):
                    o, ob = proj_fm(wt, wt_b, j, hT, hT_b, TB)
                    rec.op("act", (lambda o=o, j=j: lambda e: e.activation(sg2[:, j, :], o, AF.Silu))(),
                           reads=[ob], writes=[sg2_b])
                for j in range(2):
                    eb = dt_ * 2 + j
                    ps, psb = nb()
                    o = ps[:, 0:TB]
                    mm_group(o, psb, [(vmh[:, m, eb * P:(eb + 1) * P], pT[:, m, :]) for m in range(2)],
                             reads=[vmh_b, pT_b])
                    rec.op("dve", (lambda o=o, j=j: lambda e: e.tensor_tensor(po[:, j, :], o, sg2[:, j, :], ALU.mult))(),
                           reads=[psb, sg2_b], writes=[po_b])
                ob0 = hm * MB + dt_ * 2
                store(memT_s[:, ob0:ob0 + 2, :], po, reads=[po_b], writes=[memT_sb])

        for a in range(3):
            for t in range(D // WN):
                wt, wt_b = load_wt("w_in", (cfg.OFF_A + a * D) // WN + t)
                for j in range(2):
                    o, ob = proj_fm(wt, wt_b, j, hT, hT_b, TB)
                    rec.op("act", (lambda o=o, j=j: lambda e: e.activation(po[:, j, :], o, AF.Sigmoid))(),
                           reads=[ob], writes=[po_b])
                ob0 = a * KC + t * 2
                store(sig_s[:, ob0:ob0 + 2, :], po, reads=[po_b], writes=[sig_sb])
        for h in range(H):
            ret_pass2(blk, h)
        nxt, nxt_b = tails[(blk + 1) % 2]
        rec.op("pool", lambda e: e.tensor_copy(nxt, hT[:, :, TB - 16:TB]), reads=[hT_b], writes=[nxt_b])
        rec.barrier()

        pieces = [(poolT_s, poolT_sb, 0, "w_pp", 0), (retT_s, retT_sb, 0, "w_pr0", 1),
                  (retT_s, retT_sb, KC, "w_pr1", 1), (memT_s, memT_sb, 0, "w_pm", 2)]
        sg_i = 0
        for dh in range(2):
            for pi, (src, src_b, boff, wn, a) in enumerate(pieces):
                for q4 in range(4):
                    k0_, k1_ = q4 * KC // 4, (q4 + 1) * KC // 4
                    load(pin[:, k0_:k1_, :], src[:, boff + k0_:boff + k1_, :],
                         reads=[src_b], writes=[pin_b])
                for t in range(HB // 2):
                    tg = dh * (HB // 2) + t
                    wt, wt_b = load_wt(wn, tg)
                    for j in range(2):
                        ob_ = tg * 2 + j
                        lb = t * 2 + j
                        o, ob = proj_fm(wt, wt_b, j, pin, pin_b, TB)
                        sgt, sgt_b, sch = sgts[sg_i % 4]
                        sg_i += 1
                        rec.dma("act", sch, sgt, sig_s[:, a * KC + ob_, :], reads=[sig_sb], writes=[sgt_b])
                        if pi == 0:
                            rec.op("dve", (lambda o=o, lb=lb, sgt=sgt: lambda e: e.tensor_tensor(
                                acc[:, lb, :], o, sgt, ALU.mult))(), reads=[ob, sgt_b], writes=[acc_b])
                        else:
                            rec.op("dve", (lambda o=o, sgt=sgt: lambda e: e.tensor_tensor(tmpB, o, sgt, ALU.mult))(),
                                   reads=[ob, sgt_b], writes=[tmpB_b])
                            rec.op("pool", (lambda lb=lb: lambda e: e.tensor_tensor(
                                acc[:, lb, :], acc[:, lb, :], tmpB, ALU.add))(), reads=[tmpB_b, acc_b], writes=[acc_b])
            rec.op("act", (lambda dh=dh: lambda e: e.copy(
                mT[:, dh * HB:(dh + 1) * HB, :].rearrange("p b t -> p (b t)"),
                acc.rearrange("p b t -> p (b t)")))(), reads=[acc_b], writes=[mT_b])
        rec.barrier()

        wt_depth[0] = 3
        load(vecrep2, nf_d, writes=[vecrep2_b])
        for tt in range(NT):
            load(xo4[:, tt, :], x_d[t0 + tt * P:t0 + (tt + 1) * P, :], writes=[xo_b[tt]])
        for t in range(D // WN):
            wt, wt_b = load_wt("w_o", t)
            for tt in range(NT):
                ps, psb = nb()
                o = ps[:, 0:WN]
                mm_group(o, psb, [(mT[:, kc, tt * P:(tt + 1) * P], wt[:, kc, :]) for kc in range(KC)],
                         reads=[wt_b, mT_b])
                rec.op("dve", lambda e: e.tensor_tensor(
                    xo4[:, tt, t * WN:(t + 1) * WN], xo4[:, tt, t * WN:(t + 1) * WN], o, ALU.add),
                    reads=[psb, xo_b[tt]], writes=[xo_b[tt]])
        for tt in range(NT):
            ssq, ssq_b = small[:, 6:7], small_b[6]
            rs, rs_b = small[:, 7:8], small_b[7]
            rec.op("act", lambda e: e.activation(hb2, xo4[:, tt, :], AF.Square, accum_out=ssq),
                   reads=[xo_b[tt]], writes=[hb2_b, ssq_b])
            rstd_from_ssq(ssq, ssq_b, D, rs, rs_b)
            rec.op("dve", lambda e: e.scalar_tensor_tensor(xo4[:, tt, :], xo4[:, tt, :], rs, vecrep2, ALU.mult, ALU.mult),
                   reads=[xo_b[tt], rs_b, vecrep2_b], writes=[xo_b[tt]])
            store(y_d[t0 + tt * P:t0 + (tt + 1) * P, :], xo4[:, tt, :], reads=[xo_b[tt]], writes=[y_b])
        rec.barrier()
        wt_depth[0] = 3 + own_extra

    rec.final_wait("sp")

    with nc.Block() as block:
        @block.tensor
        def _(e):
            for f in rec.q["pe"]:
                f(e)

        @block.scalar
        def _(e):
            for f in rec.q["act"]:
                f(e)

        @block.vector
        def _(e):
            for f in rec.q["dve"]:
                f(e)

        @block.gpsimd
        def _(e):
            for f in rec.q["pool"]:
                f(e)

        @block.sync
        def _(e):
            for f in rec.q["sp"]:
                f(e)
    es.close()
    return nc


def tile_major(w, n):
    K, C = w.shape
    return np.ascontiguousarray(
        w.reshape(K // P, P, C // n, n).transpose(2, 1, 0, 3)).reshape(C // n, P, (K // P) * n)


def const_tables(cfg, core):
    S, H, SC = cfg.S, cfg.H, cfg.SC
    half = 128
    inv = 10000.0 ** (-np.arange(half, dtype=np.float64) / half)
    pos = np.arange(core * SC, (core + 1) * SC, dtype=np.float64)
    ang = inv[:, None] * pos[None, :]
    lg = np.log1p(-(2.0 ** (-5.0 - np.arange(H, dtype=np.float64))))
    i = np.arange(P, dtype=np.float64)
    diff = i[None, :] - i[:, None]
    maskT = np.where(diff[None] >= 0, np.exp(np.maximum(diff, 0)[None] * lg[:, None, None]), 0.0) / 16.0
    maskT = maskT.transpose(1, 0, 2).reshape(P, H * P)
    qdec = np.exp((i + 1.0)[None, :] * lg[:, None])
    qdec = np.broadcast_to(qdec.reshape(1, H * P), (P, H * P))
    kdec = (np.exp((P - 1.0 - i)[:, None] * lg[None, :]) / 16.0)
    cdec = np.broadcast_to(np.exp(P * lg)[None, :], (P, H))
    rc = np.zeros((P, 64))
    for g, w in enumerate(POOL_WINDOWS):
        cnt = np.minimum(np.arange(16) + 1, w) if core == 0 else np.full(16, w)
        rc[:, g * 16:(g + 1) * 16] = 1.0 / cnt[None, :]
    npre = max(cfg.NPRE, 1) * TB
    ppos = np.arange(core * SC - npre, core * SC, dtype=np.float64)
    pang = inv[:, None] * ppos[None, :]
    f = lambda a: np.ascontiguousarray(a, dtype=np.float32)
    return dict(cos_t=f(np.cos(ang)), sin_t=f(np.sin(ang)), maskT=f(maskT), qdec=f(qdec), kdec=f(kdec),
                cdec=f(cdec), rc=f(rc), cosp_t=f(np.cos(pang)), sinp_t=f(np.sin(pang)), ident=np.eye(P, dtype=np.float32).astype(ml_dtypes.bfloat16))


def make_inputs(cfg, x, mem, norm_in, norm_mem, w_in, w_pool_group, pool_scale, w_mem_k, w_mem_v,
                w_proj_pool, w_proj_ret, w_proj_mem, w_out, norm_f):
    D, NR, SC = cfg.D, cfg.NCORES, cfg.SC
    rep = lambda v: np.ascontiguousarray(np.broadcast_to(np.asarray(v, np.float32).reshape(1, D), (P, D)))
    common = {}
    common["mem"] = np.ascontiguousarray(np.asarray(mem, np.float32).reshape(MEM_LEN, D))
    common["norm_in_rep"] = rep(norm_in)
    common["norm_mem_rep"] = rep(norm_mem)
    common["norm_f_rep"] = rep(norm_f)
    common["pscale"] = np.ascontiguousarray(np.asarray(pool_scale, np.float32).reshape(cfg.KC, P).T)
    wt = {}
    wt["w_in"] = tile_major(np.asarray(w_in, np.float32).reshape(D, cfg.INW), WN)
    wpg = np.asarray(w_pool_group, np.float32).reshape(4, D // 4, D // 4)
    wt["w_pg"] = np.concatenate([tile_major(wpg[g], P) for g in range(4)], 0)
    wt["w_mk"] = tile_major(np.asarray(w_mem_k, np.float32).reshape(D, D), WN)
    wt["w_mv"] = tile_major(np.asarray(w_mem_v, np.float32).reshape(D, D), WN)
    wt["w_pp"] = tile_major(np.asarray(w_proj_pool, np.float32).reshape(D, D), WN)
    wpr = np.asarray(w_proj_ret, np.float32).reshape(2 * D, D)
    wt["w_pr0"] = tile_major(wpr[:D], WN)
    wt["w_pr1"] = tile_major(wpr[D:], WN)
    wt["w_pm"] = tile_major(np.asarray(w_proj_mem, np.float32).reshape(D, D), WN)
    wt["w_o"] = tile_major(np.asarray(w_out, np.float32).reshape(D, D), WN)
    xf = np.asarray(x, np.float32).reshape(cfg.S, D)
    maps = []
    for c in range(NR):
        m = dict(common)
        m.update(const_tables(cfg, c))
        m["x"] = np.ascontiguousarray(xf[c * SC:(c + 1) * SC])
        npre = max(cfg.NPRE, 1) * TB
        xp = np.zeros((npre, D), np.float32)
        if c > 0:
            n = min(npre, c * SC)
            xp[npre - n:] = xf[c * SC - n:c * SC]
        m["x_prev"] = xp
        m.update(wt)
        maps.append(m)
    return maps


def run(cfg, **inputs):
    nc = build(cfg)
    maps = make_inputs(cfg, **inputs)
    res = run_bass_kernel_spmd(nc, maps, core_ids=list(range(cfg.NCORES)))
    y = np.concatenate([res.results[c]["y"] for c in range(cfg.NCORES)], axis=0)
    return y.reshape(1, cfg.S, cfg.D).astype(np.float32)


def kernel(**inputs):
    cfg = Cfg(4096, 8192)
    return run(cfg, **inputs)
```

```python
import numpy as np
import ml_dtypes
import concourse.bass as bass
import concourse.mybir as mybir
from concourse.bass_utils import run_bass_kernel_spmd

F32 = mybir.dt.float32
BF16 = mybir.dt.bfloat16
ALU = mybir.AluOpType
AF = mybir.ActivationFunctionType
AX = mybir.AxisListType
P = 128
TB = 512
NT = TB // P
WN = 256
EPS = 1e-6
MEM_LEN = 256
POOL_WINDOWS = (2, 4, 8, 16)


class Cfg:
    def __init__(self, D, S, ncores=8):
        self.D, self.S = D, S
        self.NCORES = ncores
        self.SC = S // ncores
        self.KC = D // P
        self.H = D // 256
        self.GB = D // 4 // P
        self.MB = D // 4 // P
        self.NB = self.SC // TB
        self.NPRE = (S - self.SC) // TB
        D_ = D
        self.OFF_U, self.OFF_G = 0, D_
        self.OFF_Q, self.OFF_K = 2 * D_, 3 * D_
        self.OFF_V, self.OFF_GR = 4 * D_, 6 * D_
        self.OFF_QM, self.OFF_GM = 8 * D_, 9 * D_
        self.OFF_A = 10 * D_
        self.INW = 13 * D_


class Tk:
    __slots__ = ("key", "sem", "val")

    def __init__(self, key, sem, val):
        self.key, self.sem, self.val = key, sem, val


class Buf:
    def __init__(self, name=""):
        self.name = name
        self.w = None
        self.r = {}


class Chan:
    def __init__(self, key, sem, unit=16):
        self.key, self.sem, self.count, self.unit = key, sem, 0, unit
        self.in_barrier = True


ENGS = ("pe", "act", "dve", "pool", "sp")


class _Proxy:
    def __getattr__(self, name):
        def f(*a, **k):
            return (name, a, k)
        return f


PROXY = _Proxy()


def _replay(c):
    return lambda e: getattr(e, c[0])(*c[1], **c[2])


class Rec:
    def __init__(self, esem):
        self.q = {e: [] for e in ENGS}
        self.cnt = {e: 0 for e in ENGS}
        self.waited = {e: {} for e in ENGS}
        self.esem = esem
        self.chans = []

    def chan(self, sem, unit=16):
        c = Chan("ch%d" % len(self.chans), sem, unit)
        self.chans.append(c)
        return c

    def _wait(self, eng, tk):
        if tk is None:
            return
        if eng == "pe" and tk.key == "pe":
            return
        w = self.waited[eng]
        if w.get(tk.key, 0) >= tk.val:
            return
        w[tk.key] = tk.val
        sem, val = tk.sem, tk.val
        self.q[eng].append(lambda e: e.wait_ge(sem, val))

    def _deps(self, eng, reads, writes):
        for b in reads:
            self._wait(eng, b.w)
        for b in writes:
            self._wait(eng, b.w)
            for tk in list(b.r.values()):
                self._wait(eng, tk)

    def _commit(self, tk, reads, writes):
        for b in reads:
            b.r[tk.key] = tk
        for b in writes:
            b.w = tk
            b.r = {}

    def op(self, eng, fns, reads=(), writes=()):
        if not isinstance(fns, (list, tuple)):
            fns = [fns]
        calls = [f(PROXY) for f in fns]
        self._deps(eng, reads, writes)
        self.cnt[eng] += 1
        sem = self.esem[eng]
        q = self.q[eng]
        for c in calls[:-1]:
            q.append(_replay(c))
        last = _replay(calls[-1])
        q.append(lambda e: last(e).then_inc(sem, 1))
        tk = Tk(eng, sem, self.cnt[eng])
        self._commit(tk, reads, writes)
        return tk

    def dma(self, qeng, ch, out_ap, in_ap, reads=(), writes=()):
        self._deps(qeng, reads, writes)
        if ch.count:
            self._wait(qeng, Tk(ch.key, ch.sem, 16 * ch.count))
        ch.count += 1
        sem = ch.sem
        self.q[qeng].append(lambda e: e.dma_start(out=out_ap, in_=in_ap).then_inc(sem, 16))
        tk = Tk(ch.key, sem, 16 * ch.count)
        self._commit(tk, reads, writes)
        return tk

    def coll(self, ch, in_ap, out_ap, ranks, reads=(), writes=()):
        self._deps("pool", reads, writes)
        if ch.count:
            self._wait("pool", Tk(ch.key, ch.sem, ch.count))
        ch.count += 1
        sem = ch.sem
        self.q["pool"].append(lambda e: e.collective_compute(
            "AllGather", ALU.bypass, replica_groups=[list(range(ranks))],
            ins=[in_ap], outs=[out_ap]).then_inc(sem, 1))
        tk = Tk(ch.key, sem, ch.count)
        self._commit(tk, reads, writes)
        return tk

    def barrier(self):
        for e in ENGS:
            for e2 in ("pe", "act", "dve", "pool"):
                if self.cnt[e2] and e2 != e:
                    self._wait(e, Tk(e2, self.esem[e2], self.cnt[e2]))
            for c in self.chans:
                if c.count and c.in_barrier:
                    self._wait(e, Tk(c.key, c.sem, c.unit * c.count))

    def final_wait(self, eng):
        for e2 in ("pe", "act", "dve", "pool"):
            if self.cnt[e2] and e2 != eng:
                self._wait(eng, Tk(e2, self.esem[e2], self.cnt[e2]))
        for c in self.chans:
            if c.count:
                self._wait(eng, Tk(c.key, c.sem, c.unit * c.count))


def build(cfg):
    D, S, KC, H, GB, MB, NB = cfg.D, cfg.SC, cfg.KC, cfg.H, cfg.GB, cfg.MB, cfg.NB
    NR = cfg.NCORES
    NPRE = cfg.NPRE
    nc = bass.Bass("TRN2", target_bir_lowering=False)

    def din(name, shape, dt=F32):
        return nc.dram_tensor(name, list(shape), dt, kind="ExternalInput").ap()

    def dscr(name, shape, dt=BF16):
        return nc.dram_tensor(name, list(shape), dt).ap()

    x_d = din("x", [S, D])
    mem_d = din("mem", [MEM_LEN, D])
    y_d = nc.dram_tensor("y", [S, D], F32, kind="ExternalOutput").ap()
    nin_d = din("norm_in_rep", [P, D])
    nmem_d = din("norm_mem_rep", [P, D])
    nf_d = din("norm_f_rep", [P, D])
    pscale_d = din("pscale", [P, KC])
    cos_d = din("cos_t", [P, S])
    sin_d = din("sin_t", [P, S])
    mask_d = din("maskT", [P, H * P])
    qdec_d = din("qdec", [P, H * P])
    kdec_d = din("kdec", [P, H])
    cdec_d = din("cdec", [P, H])
    rc_d = din("rc", [P, 64])
    ident_d = din("ident", [P, P], BF16)
    xp_d = din("x_prev", [max(NPRE, 1) * TB, D])
    cosp_d = din("cosp_t", [P, max(NPRE, 1) * TB])
    sinp_d = din("sinp_t", [P, max(NPRE, 1) * TB])

    wspecs = {
        "w_in": (cfg.INW // WN, KC, WN),
        "w_pg": (4 * GB, GB, P),
        "w_mk": (D // WN, KC, WN),
        "w_mv": (D // WN, KC, WN),
        "w_pp": (D // WN, KC, WN),
        "w_pr0": (D // WN, KC, WN),
        "w_pr1": (D // WN, KC, WN),
        "w_pm": (D // WN, KC, WN),
        "w_o": (D // WN, KC, WN),
    }
    wf, wb = {}, {}
    for nm, (ntl, kc, n) in wspecs.items():
        wf[nm] = din(nm, [ntl, P, kc * n])
        wb[nm] = [dscr("%s_b%d" % (nm, i), [min(16, ntl - i), P, kc * n]) for i in range(0, ntl, 16)]

    kmT_s = dscr("kmT_s", [P, KC, MEM_LEN])
    vm_s = dscr("vm_s", [2, P, D])
    S_s = dscr("S_s", [H, P, 2 * 512], F32)

    retT_s = dscr("retT_s", [P, 2 * KC, TB])
    poolT_s = dscr("poolT_s", [P, KC, TB])
    memT_s = dscr("memT_s", [P, KC, TB])
    sig_s = dscr("sig_s", [P, 3 * KC, TB])

    from contextlib import ExitStack
    es = ExitStack()
    AW = 53200
    arena = es.enter_context(nc.sbuf_tensor("arena", [P, AW], F32))
    NBK = 6
    banks = [es.enter_context(nc.psum_tensor("pb%d" % i, [P, 512], F32)) for i in range(NBK)]
    ptrs = [(es.enter_context(nc.psum_tensor("ptr%d" % i, [P, 1024], BF16)), Buf("ptr%d" % i)) for i in range(2)]
    ptr_i = [0]
    sem_names = ["s_pe", "s_act", "s_dve", "s_pool"]
    sems = [es.enter_context(nc.semaphore(n)) for n in sem_names]
    rec = Rec({"pe": sems[0], "act": sems[1], "dve": sems[2], "pool": sems[3]})

    def newchan(name):
        return rec.chan(es.enter_context(nc.semaphore(name)))

    class Carve:
        def __init__(self, base):
            self.off = base

        def f32(self, n):
            a = arena[:, self.off:self.off + n]
            self.off += n
            return a

        def bf16(self, n):
            w = (n + 1) // 2
            a = arena[:, self.off:self.off + w].bitcast(BF16)
            self.off += w
            return a

    NSPLIT = 2
    cv = Carve(0)
    maskT = cv.f32(H * P)
    qdec = cv.f32(H * P)
    kdec = cv.f32(H)
    cdec = cv.f32(H)
    rc = cv.f32(64)
    pscale = cv.f32(KC)
    ident = cv.bf16(P)
    consts_b = Buf("consts")
    wts = [(cv.bf16(KC * WN), Buf("wt%d" % i), [newchan("c_wt%d_%d" % (i, j)) for j in range(NSPLIT)]) for i in range(3)]
    wgs = [(cv.bf16(GB * P), Buf("wg%d" % i), newchan("c_wg%d" % i)) for i in range(4)]
    sgts = [(cv.bf16(TB), Buf("sgt%d" % i), newchan("c_sg%d" % i)) for i in range(4)]
    tmpB = cv.f32(TB)
    tmpB_b = Buf("tmpB")
    small = cv.f32(32)
    tails = [(cv.bf16(KC * 16).rearrange("p (k t) -> p k t", k=KC), Buf("tail%d" % i)) for i in range(2)]
    small_b = [Buf("sm%d" % i) for i in range(8)]
    UNION = cv.off

    cA = Carve(UNION)
    hT = cA.bf16(KC * TB).rearrange("p (k t) -> p k t", k=KC)
    hT_b = Buf("hT")
    WREG = cA.off
    cP = Carve(WREG)
    css = [(cP.f32(TB), cP.f32(TB), Buf("cs%d" % i)) for i in range(2)]
    t1 = cP.f32(TB); t1_b = Buf("t1")
    t2 = cP.f32(TB); t2_b = Buf("t2")
    kT = cP.bf16(2 * TB).rearrange("p (h t) -> p h t", h=2); kT_b = Buf("kT")
    kd = cP.bf16(NT * 256).rearrange("p (c d) -> p c d", c=NT); kd_b = Buf("kd")
    vv = cP.bf16(NT * 512).rearrange("p (c e) -> p c e", c=NT); vv_b = Buf("v")
    vT = cP.bf16(4 * TB).rearrange("p (b t) -> p b t", b=4); vT_b = Buf("vT")
    Sts = [(cP.f32(1024).rearrange("p (h e) -> p h e", h=2), Buf("S%d" % i)) for i in range(2)]
    NREG = cP.off
    cN = Carve(NREG)
    xt = cN.f32(D); xt_b = Buf("xt")
    hb = cN.bf16(D); hb_b = Buf("hb")
    vecrep = cN.f32(D); vecrep_b = Buf("vecrep")
    XW0 = cN.off
    memnT = cN.bf16(KC * MEM_LEN).rearrange("p (k t) -> p k t", k=KC); memnT_b = Buf("memnT")
    stg = cN.bf16(WN); stg_b = Buf("stg")
    cX = Carve(XW0)
    n_extra = 0
    while cX.off + KC * WN // 2 <= AW and n_extra < 3:
        wts.append((cX.bf16(KC * WN), Buf("wtx%d" % n_extra), [newchan("c_wtx%d_%d" % (n_extra, j)) for j in range(NSPLIT)]))
        n_extra += 1
    wt_depth = [3]
    cM = Carve(NREG)
    qT = cM.bf16(2 * TB).rearrange("p (h t) -> p h t", h=2); qT_b = Buf("qT")
    qdT = cM.bf16(2 * TB).rearrange("p (h t) -> p h t", h=2); qdT_b = Buf("qdT")
    sgT = cM.bf16(4 * TB).rearrange("p (b t) -> p b t", b=4); sgT_b = Buf("sgT")
    Sb = cM.bf16(1024).rearrange("p (h e) -> p h e", h=2); Sb_b = Buf("Sb")
    PT = cM.bf16(P); PT_b = Buf("PT")
    retn = cM.bf16(512); retn_b = Buf("retn")
    retT = cM.bf16(4 * TB).rearrange("p (b t) -> p b t", b=4); retT_b = Buf("retT")
    junk = cM.bf16(512); junk_b = Buf("junk")

    UW = 16 + TB
    sg2 = cM.bf16(2 * TB).rearrange("p (j t) -> p j t", j=2); sg2_b = Buf("sg2")
    po = cM.bf16(2 * TB).rearrange("p (j t) -> p j t", j=2); po_b = Buf("po")
    cPl = Carve(cM.off)
    ubufs = [(cPl.f32(UW), Buf("u%d" % i)) for i in range(3)]
    mixT = cPl.bf16(GB * TB).rearrange("p (k t) -> p k t", k=GB); mixT_b = Buf("mixT")
    t16 = cPl.f32(16); t16_b = Buf("t16")
    cMm = Carve(cM.off)
    kmTh = cMm.bf16(MB * MEM_LEN).rearrange("p (k m) -> p k m", k=MB); kmTh_b = Buf("kmTh")
    vmh = cMm.bf16(2 * MB * P).rearrange("p (m e) -> p m e", m=2); vmh_b = Buf("vmh")
    qmT = cMm.bf16(MB * TB).rearrange("p (k t) -> p k t", k=MB); qmT_b = Buf("qmT")
    pex = cMm.f32(MEM_LEN); pex_b = Buf("pex")
    pn = cMm.bf16(MEM_LEN); pn_b = Buf("pn")
    pT = cMm.bf16(2 * TB).rearrange("p (m t) -> p m t", m=2); pT_b = Buf("pT")
    cM.off = max(cPl.off, cMm.off)
    own_extra = 1 if (n_extra >= 1 and cM.off <= XW0) else 0
    cB = Carve(UNION)
    HB = KC // 2
    acc = cB.f32(HB * TB).rearrange("p (b t) -> p b t", b=HB); acc_b = Buf("acc")
    pin = cB.bf16(KC * TB).rearrange("p (k t) -> p k t", k=KC); pin_b = Buf("pin")
    mT = cB.bf16(KC * TB).rearrange("p (k t) -> p k t", k=KC); mT_b = Buf("mT")
    cF = Carve(UNION)
    xo4 = cF.f32(NT * D).rearrange("p (t d) -> p t d", t=NT)
    xo_b = [Buf("xo%d" % i) for i in range(NT)]
    assert cF.off <= UNION + (HB * TB + KC * TB // 2), "final stage must not overlap mT"
    cF = Carve(cB.off)
    hb2 = cF.bf16(D); hb2_b = Buf("hb2")
    vecrep2 = cF.f32(D); vecrep2_b = Buf("vecrep2")
    assert max(cA.off, cP.off, cN.off, cM.off, cB.off, cF.off) <= AW, (cA.off, cP.off, cN.off, cM.off, cB.off, cF.off)

    bank_b = [Buf("bank%d" % i) for i in range(NBK)]
    bank_i = [0]

    def nb():
        i = bank_i[0] % NBK
        bank_i[0] += 1
        return banks[i], bank_b[i]

    ch_ld = [newchan("c_ld%d" % i) for i in range(4)]
    ch_st = [newchan("c_st%d" % i) for i in range(4)]
    ch_cast = [newchan("c_cast%d" % i) for i in range(4)]
    ch_sst = [newchan("c_sst%d" % i) for i in range(2)]
    ch_xq = newchan("c_xq")
    ld_i, st_i = [0], [0]

    def load(out_ap, in_ap, reads=(), writes=()):
        c = ch_ld[ld_i[0] % 4]; ld_i[0] += 1
        return rec.dma("sp", c, out_ap, in_ap, reads, writes)

    def store(out_ap, in_ap, reads=(), writes=()):
        c = ch_st[st_i[0] % 4]; st_i[0] += 1
        return rec.dma("pool", c, out_ap, in_ap, reads, writes)

    wb_b = {nm: [Buf("%s_%d" % (nm, t)) for t in range(wspecs[nm][0])] for nm in wspecs}
    for c_ in ch_cast:
        c_.in_barrier = False
    kmT_sb, vm_sb, S_sb = Buf("kmT_s"), Buf("vm_s"), [Buf("S_s%d" % h) for h in range(H)]

    retT_sb, poolT_sb, memT_sb, sig_sb = Buf("retT_s"), Buf("poolT_s"), Buf("memT_s"), Buf("sig_s")
    y_b = Buf("y")

    for (ap, src) in ((maskT, mask_d), (qdec, qdec_d), (kdec, kdec_d), (cdec, cdec_d), (rc, rc_d),
                      (pscale, pscale_d), (ident, ident_d)):
        load(ap, src, writes=[consts_b])
    ci = 0
    NCAST = 2
    order = []
    for h in range(H):
        order.append(("w_in", (cfg.OFF_K + h * 256) // WN))
        order += [("w_in", (cfg.OFF_V + h * 512) // WN + vt) for vt in range(2)]
    seen = set(order)
    for nm in ("w_mk", "w_mv", "w_pg", "w_in", "w_pp", "w_pr0", "w_pr1", "w_pm", "w_o"):
        for tl in range(wspecs[nm][0]):
            if (nm, tl) not in seen:
                order.append((nm, tl))
    for nm, tl in order:
        row = wspecs[nm][1] * wspecs[nm][2]
        cw = min(row, 4096)
        for c0 in range(0, row, cw):
            rec.dma("pool", ch_cast[ci % NCAST], wb[nm][tl // 16][tl % 16][:, c0:c0 + cw], wf[nm][tl][:, c0:c0 + cw],
                    writes=[wb_b[nm][tl]])
            ci += 1

    wt_i = [0]

    def _load_tile(nm, tidx, ap, b, chs):
        row = wspecs[nm][1] * wspecs[nm][2]
        src = wb[nm][tidx // 16][tidx % 16]
        hw = row // NSPLIT if row >= 2048 else row
        for i, c0 in enumerate(range(0, row, hw)):
            rec.dma("sp", chs[i % len(chs)], ap[:, c0:c0 + hw], src[:, c0:c0 + hw], reads=[wb_b[nm][tidx]], writes=[b])

    def load_wt(nm, tidx):
        ap, b, chs = wts[wt_i[0] % wt_depth[0]]
        wt_i[0] += 1
        _load_tile(nm, tidx, ap, b, chs)
        return ap.rearrange("p (k n) -> p k n", n=WN), b

    wg_i = [0]

    def load_wg(tidx):
        ap, b, ch = wgs[wg_i[0] % 4]
        wg_i[0] += 1
        src = wb["w_pg"][tidx // 16][tidx % 16]
        rec.dma("act", ch, ap, src, reads=[wb_b["w_pg"][tidx]], writes=[b])
        return ap.rearrange("p (k n) -> p k n", n=P), b

    def mm_group(out_ap, out_b, pairs, reads):
        n = len(pairs)
        fns = []
        for i, (l, r) in enumerate(pairs):
            fns.append((lambda l=l, r=r, i=i: (lambda e: e.matmul(out_ap, l, r, start=(i == 0), stop=(i == n - 1))))())
        return rec.op("pe", fns, reads=reads, writes=[out_b])

    def proj_fm(wt, wt_b, j, rhs3, rhs_b, ncols, kcn=None):
        kcn = kcn or KC
        ps, psb = nb()
        o = ps[:, 0:ncols]
        mm_group(o, psb, [(wt[:, kc, j * P:(j + 1) * P], rhs3[:, kc, 0:ncols]) for kc in range(kcn)],
                 reads=[wt_b, rhs_b])
        return o, psb

    def transposes(srcs, src_b):
        ptr_t, ptr_b = ptrs[ptr_i[0] % 2]
        ptr_i[0] += 1
        fns = []
        for i, s_ in enumerate(srcs):
            fns.append((lambda s_=s_, i=i: (lambda e: e.transpose(ptr_t[:, i * P:(i + 1) * P], s_, ident)))())
        rec.op("pe", fns, reads=[src_b, consts_b], writes=[ptr_b])
        return ptr_t, ptr_b

    def rstd_from_ssq(ssq_ap, ssq_b, n, out_ap, out_b):
        rec.op("dve", lambda e: e.tensor_scalar(out_ap, ssq_ap, 1.0 / n, EPS, ALU.mult, ALU.add),
               reads=[ssq_b], writes=[out_b])
        rec.op("act", lambda e: e.activation(out_ap, out_ap, AF.Sqrt), reads=[out_b], writes=[out_b])
        rec.op("dve", lambda e: e.reciprocal(out_ap, out_ap), reads=[out_b], writes=[out_b])

    alt = [0]

    def evac_copy(out_ap, out_b, in_ap, in_b):
        alt[0] += 1
        if alt[0] % 2:
            rec.op("act", lambda e: e.copy(out_ap, in_ap), reads=[in_b], writes=[out_b])
        else:
            rec.op("dve", lambda e: e.tensor_copy(out_ap, in_ap), reads=[in_b], writes=[out_b])

    def norm_tile_to_T(src_rows, dstT, dstT_b, tt, vrep, vrep_b, xt_, xt_b_, hb_, hb_b_, act_q=False):
        if act_q:
            rec.dma("act", ch_xq, xt_, src_rows, writes=[xt_b_])
        else:
            load(xt_, src_rows, writes=[xt_b_])
        ssq, ssq_b = small[:, 0:1], small_b[0]
        rs, rs_b = small[:, 1:2], small_b[1]
        rec.op("act", lambda e: e.activation(hb_, xt_, AF.Square, accum_out=ssq),
               reads=[xt_b_], writes=[hb_b_, ssq_b])
        rstd_from_ssq(ssq, ssq_b, D, rs, rs_b)
        rec.op("dve", lambda e: e.scalar_tensor_tensor(hb_, xt_, rs, vrep, ALU.mult, ALU.mult),
               reads=[xt_b_, rs_b, vrep_b], writes=[hb_b_])
        for k0 in range(0, KC, 4):
            ptr_t, ptr_b = transposes([hb_[:, (k0 + i) * P:(k0 + i + 1) * P] for i in range(4)], hb_b_)
            evac_copy(dstT[:, k0:k0 + 4, tt * P:(tt + 1) * P],
                      dstT_b, ptr_t[:, 0:4 * P].rearrange("p (k t) -> p k t", k=4), ptr_b)

    rec.barrier()

    msc = float((MB * P) ** -0.5)


    cs_cur = [None]
    cs_i = [0]

    def norm_stage(blk, xsrc=None, csrc=None, ssrc=None, prefix=False):
        xsrc = x_d if xsrc is None else xsrc
        csrc = cos_d if csrc is None else csrc
        ssrc = sin_d if ssrc is None else ssrc
        t0 = blk * TB
        if not prefix:
            load(vecrep, nin_d, writes=[vecrep_b])
        for tt in range(NT):
            norm_tile_to_T(xsrc[t0 + tt * P:t0 + (tt + 1) * P, :], hT, hT_b, tt, vecrep, vecrep_b, xt, xt_b, hb, hb_b,
                           act_q=prefix)
        if not prefix:
            rec.barrier()
        cosb_, sinb_, csb_ = css[cs_i[0] % 2]
        cs_i[0] += 1
        load(cosb_, csrc[:, t0:t0 + TB], writes=[csb_])
        load(sinb_, ssrc[:, t0:t0 + TB], writes=[csb_])
        cs_cur[0] = (cosb_, sinb_, csb_)

    def rope_pair(tile_idx, dst, dst_b):
        cosb, sinb, cs_b = cs_cur[0]
        wt, wt_b = load_wt("w_in", tile_idx)
        A, Ab = proj_fm(wt, wt_b, 0, hT, hT_b, TB)
        Bq, Bb = proj_fm(wt, wt_b, 1, hT, hT_b, TB)
        rec.op("dve", lambda e: e.tensor_tensor(t1, A, cosb, ALU.mult), reads=[Ab, cs_b], writes=[t1_b])
        rec.op("dve", lambda e: e.tensor_tensor(t2, Bq, sinb, ALU.mult), reads=[Bb, cs_b], writes=[t2_b])
        rec.op("dve", lambda e: e.tensor_tensor(dst[:, 0, :], t1, t2, ALU.subtract),
               reads=[t1_b, t2_b], writes=[dst_b])
        rec.op("dve", lambda e: e.tensor_tensor(t1, Bq, cosb, ALU.mult), reads=[Bb, cs_b], writes=[t1_b])
        rec.op("dve", lambda e: e.tensor_tensor(t2, A, sinb, ALU.mult), reads=[Ab, cs_b], writes=[t2_b])
        rec.op("dve", lambda e: e.tensor_tensor(dst[:, 1, :], t1, t2, ALU.add),
               reads=[t1_b, t2_b], writes=[dst_b])

    kT_f = kT.rearrange("p h t -> p (h t)")
    kd_f = kd.rearrange("p c d -> p (c d)")
    vv_f = vv.rearrange("p c e -> p (c e)")

    def state_update(h, c):
        St, St_b = Sts[h % 2]
        for hf in range(2):
            p2, p2b = nb()
            o2 = p2[:, 0:512]
            mm_group(o2, p2b, [(kd[:, c, hf * P:(hf + 1) * P], vv[:, c, :])], reads=[kd_b, vv_b])
            rec.op("dve", lambda e: e.scalar_tensor_tensor(
                St[:, hf, :], St[:, hf, :], cdec[:, h:h + 1], o2, ALU.mult, ALU.add),
                reads=[St_b, p2b, consts_b], writes=[St_b])

    def kv_compute(h):
        rope_pair((cfg.OFF_K + h * 256) // WN, kT, kT_b)
        for c in range(NT):
            ptr_t, ptr_b = transposes([kT[:, hf, c * P:(c + 1) * P] for hf in range(2)], kT_b)
            rec.op("dve", lambda e: e.tensor_scalar(kd[:, c, :], ptr_t[:, 0:256], kdec[:, h:h + 1], None, ALU.mult),
                   reads=[ptr_b, consts_b], writes=[kd_b])
        for vt in range(2):
            wt, wt_b = load_wt("w_in", (cfg.OFF_V + h * 512) // WN + vt)
            for j in range(2):
                o, ob = proj_fm(wt, wt_b, j, hT, hT_b, TB)
                evac_copy(vT[:, vt * 2 + j, :], vT_b, o, ob)
        for c in range(NT):
            ptr_t, ptr_b = transposes([vT[:, eb, c * P:(c + 1) * P] for eb in range(4)], vT_b)
            evac_copy(vv[:, c, :], vv_b, ptr_t[:, 0:512], ptr_b)

    def kv_prefix(pj, h):
        kv_compute(h)
        St, St_b = Sts[h % 2]
        Sflat = St.rearrange("p h e -> p (h e)")
        if pj == 0:
            rec.op("dve", lambda e: e.memset(Sflat, 0.0), writes=[St_b])
        else:
            load(Sflat, S_s[h], reads=[S_sb[h]], writes=[St_b])
        for c in range(NT):
            state_update(h, c)
        rec.dma("act", ch_sst[h % 2], S_s[h], Sflat, reads=[St_b], writes=[S_sb[h]])

    def ret_pass2(blk, h):
        St, St_b = Sts[h % 2]
        Sflat = St.rearrange("p h e -> p (h e)")
        kv_compute(h)
        rope_pair((cfg.OFF_Q + h * 256) // WN, qT, qT_b)
        qdv = qdec[:, h * P:(h + 1) * P].unsqueeze(1).to_broadcast([P, NT, P])
        for hf in range(2):
            rec.op("dve", lambda e: e.tensor_tensor(
                qdT[:, hf, :].rearrange("p (c i) -> p c i", c=NT),
                qT[:, hf, :].rearrange("p (c i) -> p c i", c=NT), qdv, ALU.mult),
                reads=[qT_b, consts_b], writes=[qdT_b])
        for gt in range(2):
            wt, wt_b = load_wt("w_in", (cfg.OFF_GR + h * 512) // WN + gt)
            for j in range(2):
                o, ob = proj_fm(wt, wt_b, j, hT, hT_b, TB)
                rec.op("act", lambda e: e.activation(sgT[:, gt * 2 + j, :], o, AF.Silu), reads=[ob], writes=[sgT_b])
        if NPRE == 0 and blk == 0:
            rec.op("pool", lambda e: e.memset(Sflat, 0.0), writes=[St_b])
        else:
            load(Sflat, S_s[h], reads=[S_sb[h]], writes=[St_b])
        for c in range(NT):
            cs = slice(c * P, (c + 1) * P)
            ps, psb = nb()
            sc = ps[:, 0:P]
            mm_group(sc, psb, [(kT[:, hf, cs], qT[:, hf, cs]) for hf in range(2)], reads=[kT_b, qT_b])
            rec.op("dve", lambda e: e.tensor_tensor(PT, sc, maskT[:, h * P:(h + 1) * P], ALU.mult),
                   reads=[psb, consts_b], writes=[PT_b])
            rec.op("act", lambda e: e.copy(Sb.rearrange("p h e -> p (h e)"), Sflat), reads=[St_b], writes=[Sb_b])
            po_, pob = nb()
            o = po_[:, 0:512]
            mm_group(o, pob, [(PT, vv[:, c, :])] + [(qdT[:, hf, cs], Sb[:, hf, :]) for hf in range(2)],
                     reads=[PT_b, vv_b, qdT_b, Sb_b])
            ssq, ssq_b = small[:, 2:3], small_b[2]
            rs, rs_b = small[:, 3:4], small_b[3]
            rec.op("act", lambda e: e.activation(junk, o, AF.Square, accum_out=ssq),
                   reads=[pob], writes=[junk_b, ssq_b])
            rstd_from_ssq(ssq, ssq_b, 512, rs, rs_b)
            rec.op("dve", lambda e: e.tensor_scalar(retn, o, rs, None, ALU.mult), reads=[pob, rs_b], writes=[retn_b])
            ptr_t, ptr_b = transposes([retn[:, eb * P:(eb + 1) * P] for eb in range(4)], retn_b)
            rec.op("dve", lambda e: e.tensor_tensor(
                retT[:, :, cs], ptr_t[:, 0:512].rearrange("p (b i) -> p b i", b=4), sgT[:, :, cs], ALU.mult),
                reads=[ptr_b, sgT_b], writes=[retT_b])
            if not (blk == NB - 1 and c == NT - 1):
                state_update(h, c)
        if blk < NB - 1:
            store(S_s[h], Sflat, reads=[St_b], writes=[S_sb[h]])
        store(retT_s[:, h * 4:(h + 1) * 4, :], retT, reads=[retT_b], writes=[retT_sb])

    if NPRE == 0:
        rec.op("dve", lambda e: e.memset(tails[0][0], 0.0), writes=[tails[0][1]])
    if NPRE:
        load(vecrep, nin_d, writes=[vecrep_b])
        wt_depth[0] = 3 + n_extra
    for pj in range(NPRE):
        norm_stage(pj, xp_d, cosp_d, sinp_d, prefix=True)
        for h in range(H):
            kv_prefix(pj, h)
        if pj == NPRE - 1:
            rec.op("dve", lambda e: e.tensor_copy(tails[0][0], hT[:, :, TB - 16:TB]), reads=[hT_b], writes=[tails[0][1]])
    rec.barrier()
    wt_depth[0] = 3

    load(vecrep, nmem_d, writes=[vecrep_b])
    for mt in range(2):
        norm_tile_to_T(mem_d[mt * P:(mt + 1) * P, :], memnT, memnT_b, mt, vecrep, vecrep_b, xt, xt_b, hb, hb_b)
    for kt in range(D // WN):
        wt, wt_b = load_wt("w_mk", kt)
        for j in range(2):
            o, ob = proj_fm(wt, wt_b, j, memnT, memnT_b, MEM_LEN)
            evac_copy(stg, stg_b, o, ob)
            store(kmT_s[:, kt * 2 + j, :], stg, reads=[stg_b], writes=[kmT_sb])
    for vt in range(D // WN):
        wt, wt_b = load_wt("w_mv", vt)
        for mt in range(2):
            ps, psb = nb()
            o = ps[:, 0:WN]
            mm_group(o, psb, [(memnT[:, kc, mt * P:(mt + 1) * P], wt[:, kc, :]) for kc in range(KC)],
                     reads=[wt_b, memnT_b])
            evac_copy(stg, stg_b, o, psb)
            store(vm_s[mt][:, vt * WN:(vt + 1) * WN], stg, reads=[stg_b], writes=[vm_sb])
    rec.barrier()

    wt_depth[0] = 3 + own_extra
    for blk in range(NB):
        t0 = blk * TB
        hTt, hTt_b = tails[blk % 2]
        norm_stage(blk)

        for g in range(4):
            w = POOL_WINDOWS[g]
            for ut in range(GB * P // WN):
                wt, wt_b = load_wt("w_in", (cfg.OFF_U + g * GB * P) // WN + ut)
                for j in range(2):
                    ub_i = ut * 2 + j
                    o, ob = proj_fm(wt, wt_b, j, hT, hT_b, TB)
                    oh, ohb = proj_fm(wt, wt_b, j, hTt, hTt_b, 16)
                    ua, ua_b = ubufs[0]
                    rec.op("act", (lambda o=o: lambda e: e.copy(ua[:, 16:UW], o))(), reads=[ob], writes=[ua_b])
                    rec.op("act", (lambda oh=oh: lambda e: e.copy(ua[:, 0:16], oh))(), reads=[ohb], writes=[ua_b])
                    cur, cur_b = ua, ua_b
                    off, step, pi_ = 0, 1, 1
                    while step < w:
                        nx, nx_b = ubufs[pi_]
                        lo = off + step
                        rec.op("dve", (lambda nx=nx, cur=cur, lo=lo, step=step: lambda e: e.tensor_tensor(
                            nx[:, lo:UW], cur[:, lo:UW], cur[:, lo - step:UW - step], ALU.add))(),
                            reads=[cur_b], writes=[nx_b])
                        cur, cur_b = nx, nx_b
                        off = lo
                        step *= 2
                        pi_ = 3 - pi_
                    rec.op("dve", (lambda cur=cur, ub_i=ub_i: lambda e: e.scalar_tensor_tensor(
                        mixT[:, ub_i, :], cur[:, 16:UW], 1.0 / w, ua[:, 16:UW], ALU.mult, ALU.subtract))(),
                        reads=[cur_b, ua_b], writes=[mixT_b])
                    if blk == 0:
                        rec.op("dve", (lambda cur=cur: lambda e: e.tensor_tensor(
                            t16, cur[:, 16:32], rc[:, g * 16:(g + 1) * 16], ALU.mult))(),
                            reads=[cur_b, consts_b], writes=[t16_b])
                        rec.op("dve", (lambda ub_i=ub_i: lambda e: e.tensor_tensor(
                            mixT[:, ub_i, 0:16], t16, ua[:, 16:32], ALU.subtract))(),
                            reads=[t16_b, ua_b], writes=[mixT_b])
            for ot in range(GB * P // WN):
                wt, wt_b = load_wt("w_in", (cfg.OFF_G + g * GB * P) // WN + ot)
                for j in range(2):
                    o, ob = proj_fm(wt, wt_b, j, hT, hT_b, TB)
                    rec.op("act", (lambda o=o, j=j: lambda e: e.activation(sg2[:, j, :], o, AF.Silu))(),
                           reads=[ob], writes=[sg2_b])
                for j in range(2):
                    oblk = g * GB + ot * 2 + j
                    wg, wg_b = load_wg(oblk)
                    ps, psb = nb()
                    o = ps[:, 0:TB]
                    mm_group(o, psb, [(wg[:, kc, :], mixT[:, kc, :]) for kc in range(GB)], reads=[wg_b, mixT_b])
                    rec.op("dve", (lambda o=o, j=j, oblk=oblk: lambda e: e.scalar_tensor_tensor(
                        po[:, j, :], o, pscale[:, oblk:oblk + 1], sg2[:, j, :], ALU.mult, ALU.mult))(),
                        reads=[psb, sg2_b, consts_b], writes=[po_b])
                ob0 = g * GB + ot * 2
                store(poolT_s[:, ob0:ob0 + 2, :], po, reads=[po_b], writes=[poolT_sb])

        rec.barrier()
        for hm in range(4):
            load(kmTh, kmT_s[:, hm * MB:(hm + 1) * MB, :], reads=[kmT_sb], writes=[kmTh_b])
            load(vmh, vm_s[:, :, hm * MB * P:(hm + 1) * MB * P].rearrange("m p e -> p m e"),
                 reads=[vm_sb], writes=[vmh_b])
            for qt in range(MB * P // WN):
                wt, wt_b = load_wt("w_in", (cfg.OFF_QM + hm * MB * P) // WN + qt)
                for j in range(2):
                    o, ob = proj_fm(wt, wt_b, j, hT, hT_b, TB)
                    evac_copy(qmT[:, qt * 2 + j, :], qmT_b, o, ob)
            for c in range(NT):
                cs = slice(c * P, (c + 1) * P)
                ps, psb = nb()
                sc = ps[:, 0:MEM_LEN]
                mm_group(sc, psb, [(qmT[:, kc, cs], kmTh[:, kc, :]) for kc in range(MB)], reads=[qmT_b, kmTh_b])
                mx, mx_b = small[:, 4:5], small_b[4]
                sm, sm_b = small[:, 5:6], small_b[5]
                rec.op("dve", (lambda sc=sc: lambda e: e.reduce_max(mx, sc, AX.X))(), reads=[psb], writes=[mx_b])
                rec.op("dve", lambda e: e.tensor_scalar(mx, mx, -msc, None, ALU.mult), reads=[mx_b], writes=[mx_b])
                rec.op("act", (lambda sc=sc: lambda e: e.activation(pex, sc, AF.Exp, bias=mx, scale=msc, accum_out=sm))(),
                       reads=[psb, mx_b], writes=[pex_b, sm_b])
                rec.op("dve", lambda e: e.reciprocal(sm, sm), reads=[sm_b], writes=[sm_b])
                rec.op("dve", lambda e: e.tensor_scalar(pn, pex, sm, None, ALU.mult), reads=[pex_b, sm_b], writes=[pn_b])
                ptr_t, ptr_b = transposes([pn[:, m * P:(m + 1) * P] for m in range(2)], pn_b)
                rec.op("dve", (lambda cs=cs: lambda e: e.tensor_copy(
                    pT[:, :, cs], ptr_t[:, 0:256].rearrange("p (m i) -> p m i", m=2)))(),
                    reads=[ptr_b], writes=[pT_b])
            for dt_ in range(MB * P // WN):
                wt, wt_b = load_wt("w_in", (cfg.OFF_GM + hm * MB * P) // WN + dt_)
                for j in range(2):
                    o, ob = proj_fm(wt, wt_b, j, hT, hT_b, TB)
                    rec.op("act", (lambda o=o, j=j: lambda e: e.activation(sg2[:, j, :], o, AF.Silu))(),
                           reads=[ob], writes=[sg2_b])
                for j in range(2):
                    eb = dt_ * 2 + j
                    ps, psb = nb()
                    o = ps[:, 0:TB]
                    mm_group(o, psb, [(vmh[:, m, eb * P:(eb + 1) * P], pT[:, m, :]) for m in range(2)],
                             reads=[vmh_b, pT_b])
                    rec.op("dve", (lambda o=o, j=j: lambda e: e.tensor_tensor(po[:, j, :], o, sg2[:, j, :], ALU.mult))(),
                           reads=[psb, sg2_b], writes=[po_b])
                ob0 = hm * MB + dt_ * 2
                store(memT_s[:, ob0:ob0 + 2, :], po, reads=[po_b], writes=[memT_sb])

        for a in range(3):
            for t in range(D // WN):
                wt, wt_b = load_wt("w_in", (cfg.OFF_A + a * D) // WN + t)
                for j in range(2):
                    o, ob = proj_fm(wt, wt_b, j, hT, hT_b, TB)
                    rec.op("act", (lambda o=o, j=j: lambda e: e.activation(po[:, j, :], o, AF.Sigmoid))(),
                           reads=[ob], writes=[po_b])
                ob0 = a * KC + t * 2
                store(sig_s[:, ob0:ob0 + 2, :], po, reads=[po_b], writes=[sig_sb])
        for h in range(H):
            ret_pass2(blk, h)
        nxt, nxt_b = tails[(blk + 1) % 2]
        rec.op("pool", lambda e: e.tensor_copy(nxt, hT[:, :, TB - 16:TB]), reads=[hT_b], writes=[nxt_b])
        rec.barrier()

        pieces = [(poolT_s, poolT_sb, 0, "w_pp", 0), (retT_s, retT_sb, 0, "w_pr0", 1),
                  (retT_s, retT_sb, KC, "w_pr1", 1), (memT_s, memT_sb, 0, "w_pm", 2)]
        sg_i = 0
        for dh in range(2):
            for pi, (src, src_b, boff, wn, a) in enumerate(pieces):
                for q4 in range(4):
                    k0_, k1_ = q4 * KC // 4, (q4 + 1) * KC // 4
                    load(pin[:, k0_:k1_, :], src[:, boff + k0_:boff + k1_, :],
                         reads=[src_b], writes=[pin_b])
                for t in range(HB // 2):
                    tg = dh * (HB // 2) + t
                    wt, wt_b = load_wt(wn, tg)
                    for j in range(2):
                        ob_ = tg * 2 + j
                        lb = t * 2 + j
                        o, ob = proj_fm(wt, wt_b, j, pin, pin_b, TB)
                        sgt, sgt_b, sch = sgts[sg_i % 4]
                        sg_i += 1
                        rec.dma("act", sch, sgt, sig_s[:, a * KC + ob_, :], reads=[sig_sb], writes=[sgt_b])
                        if pi == 0:
                            rec.op("dve", (lambda o=o, lb=lb, sgt=sgt: lambda e: e.tensor_tensor(
                                acc[:, lb, :], o, sgt, ALU.mult))(), reads=[ob, sgt_b], writes=[acc_b])
                        else:
                            rec.op("dve", (lambda o=o, sgt=sgt: lambda e: e.tensor_tensor(tmpB, o, sgt, ALU.mult))(),
                                   reads=[ob, sgt_b], writes=[tmpB_b])
                            rec.op("pool", (lambda lb=lb: lambda e: e.tensor_tensor(
                                acc[:, lb, :], acc[:, lb, :], tmpB, ALU.add))(), reads=[tmpB_b, acc_b], writes=[acc_b])
            rec.op("act", (lambda dh=dh: lambda e: e.copy(
                mT[:, dh * HB:(dh + 1) * HB, :].rearrange("p b t -> p (b t)"),
                acc.rearrange("p b t -> p (b t)")))(), reads=[acc_b], writes=[mT_b])
        rec.barrier()

        wt_depth[0] = 3
        load(vecrep2, nf_d, writes=[vecrep2_b])
        for tt in range(NT):
            load(xo4[:, tt, :], x_d[t0 + tt * P:t0 + (tt + 1) * P, :], writes=[xo_b[tt]])
        for t in range(D // WN):
            wt, wt_b = load_wt("w_o", t)
            for tt in range(NT):
                ps, psb = nb()
                o = ps[:, 0:WN]
                mm_group(o, psb, [(mT[:, kc, tt * P:(tt + 1) * P], wt[:, kc, :]) for kc in range(KC)],
                         reads=[wt_b, mT_b])
                rec.op("dve", lambda e: e.tensor_tensor(
                    xo4[:, tt, t * WN:(t + 1) * WN], xo4[:, tt, t * WN:(t + 1) * WN], o, ALU.add),
                    reads=[psb, xo_b[tt]], writes=[xo_b[tt]])
        for tt in range(NT):
            ssq, ssq_b = small[:, 6:7], small_b[6]
            rs, rs_b = small[:, 7:8], small_b[7]
            rec.op("act", lambda e: e.activation(hb2, xo4[:, tt, :], AF.Square, accum_out=ssq),
                   reads=[xo_b[tt]], writes=[hb2_b, ssq_b])
            rstd_from_ssq(ssq, ssq_b, D, rs, rs_b)
            rec.op("dve", lambda e: e.scalar_tensor_tensor(xo4[:, tt, :], xo4[:, tt, :], rs, vecrep2, ALU.mult, ALU.mult),
                   reads=[xo_b[tt], rs_b, vecrep2_b], writes=[xo_b[tt]])
            store(y_d[t0 + tt * P:t0 + (tt + 1) * P, :], xo4[:, tt, :], reads=[xo_b[tt]], writes=[y_b])
        rec.barrier()
        wt_depth[0] = 3 + own_extra

    rec.final_wait("sp")

    with nc.Block() as block:
        @block.tensor
        def _(e):
            for f in rec.q["pe"]:
                f(e)

        @block.scalar
        def _(e):
            for f in rec.q["act"]:
                f(e)

        @block.vector
        def _(e):
            for f in rec.q["dve"]:
                f(e)

        @block.gpsimd
        def _(e):
            for f in rec.q["pool"]:
                f(e)

        @block.sync
        def _(e):
            for f in rec.q["sp"]:
                f(e)
    es.close()
    return nc


def tile_major(w, n):
    K, C = w.shape
    return np.ascontiguousarray(
        w.reshape(K // P, P, C // n, n).transpose(2, 1, 0, 3)).reshape(C // n, P, (K // P) * n)


def const_tables(cfg, core):
    S, H, SC = cfg.S, cfg.H, cfg.SC
    half = 128
    inv = 10000.0 ** (-np.arange(half, dtype=np.float64) / half)
    pos = np.arange(core * SC, (core + 1) * SC, dtype=np.float64)
    ang = inv[:, None] * pos[None, :]
    lg = np.log1p(-(2.0 ** (-5.0 - np.arange(H, dtype=np.float64))))
    i = np.arange(P, dtype=np.float64)
    diff = i[None, :] - i[:, None]
    maskT = np.where(diff[None] >= 0, np.exp(np.maximum(diff, 0)[None] * lg[:, None, None]), 0.0) / 16.0
    maskT = maskT.transpose(1, 0, 2).reshape(P, H * P)
    qdec = np.exp((i + 1.0)[None, :] * lg[:, None])
    qdec = np.broadcast_to(qdec.reshape(1, H * P), (P, H * P))
    kdec = (np.exp((P - 1.0 - i)[:, None] * lg[None, :]) / 16.0)
    cdec = np.broadcast_to(np.exp(P * lg)[None, :], (P, H))
    rc = np.zeros((P, 64))
    for g, w in enumerate(POOL_WINDOWS):
        cnt = np.minimum(np.arange(16) + 1, w) if core == 0 else np.full(16, w)
        rc[:, g * 16:(g + 1) * 16] = 1.0 / cnt[None, :]
    npre = max(cfg.NPRE, 1) * TB
    ppos = np.arange(core * SC - npre, core * SC, dtype=np.float64)
    pang = inv[:, None] * ppos[None, :]
    f = lambda a: np.ascontiguousarray(a, dtype=np.float32)
    return dict(cos_t=f(np.cos(ang)), sin_t=f(np.sin(ang)), maskT=f(maskT), qdec=f(qdec), kdec=f(kdec),
                cdec=f(cdec), rc=f(rc), cosp_t=f(np.cos(pang)), sinp_t=f(np.sin(pang)), ident=np.eye(P, dtype=np.float32).astype(ml_dtypes.bfloat16))


def make_inputs(cfg, x, mem, norm_in, norm_mem, w_in, w_pool_group, pool_scale, w_mem_k, w_mem_v,
                w_proj_pool, w_proj_ret, w_proj_mem, w_out, norm_f):
    D, NR, SC = cfg.D, cfg.NCORES, cfg.SC
    rep = lambda v: np.ascontiguousarray(np.broadcast_to(np.asarray(v, np.float32).reshape(1, D), (P, D)))
    common = {}
    common["mem"] = np.ascontiguousarray(np.asarray(mem, np.float32).reshape(MEM_LEN, D))
    common["norm_in_rep"] = rep(norm_in)
    common["norm_mem_rep"] = rep(norm_mem)
    common["norm_f_rep"] = rep(norm_f)
    common["pscale"] = np.ascontiguousarray(np.asarray(pool_scale, np.float32).reshape(cfg.KC, P).T)
    wt = {}
    wt["w_in"] = tile_major(np.asarray(w_in, np.float32).reshape(D, cfg.INW), WN)
    wpg = np.asarray(w_pool_group, np.float32).reshape(4, D // 4, D // 4)
    wt["w_pg"] = np.concatenate([tile_major(wpg[g], P) for g in range(4)], 0)
    wt["w_mk"] = tile_major(np.asarray(w_mem_k, np.float32).reshape(D, D), WN)
    wt["w_mv"] = tile_major(np.asarray(w_mem_v, np.float32).reshape(D, D), WN)
    wt["w_pp"] = tile_major(np.asarray(w_proj_pool, np.float32).reshape(D, D), WN)
    wpr = np.asarray(w_proj_ret, np.float32).reshape(2 * D, D)
    wt["w_pr0"] = tile_major(wpr[:D], WN)
    wt["w_pr1"] = tile_major(wpr[D:], WN)
    wt["w_pm"] = tile_major(np.asarray(w_proj_mem, np.float32).reshape(D, D), WN)
    wt["w_o"] = tile_major(np.asarray(w_out, np.float32).reshape(D, D), WN)
    xf = np.asarray(x, np.float32).reshape(cfg.S, D)
    maps = []
    for c in range(NR):
        m = dict(common)
        m.update(const_tables(cfg, c))
        m["x"] = np.ascontiguousarray(xf[c * SC:(c + 1) * SC])
        npre = max(cfg.NPRE, 1) * TB
        xp = np.zeros((npre, D), np.float32)
        if c > 0:
            n = min(npre, c * SC)
            xp[npre - n:] = xf[c * SC - n:c * SC]
        m["x_prev"] = xp
        m.update(wt)
        maps.append(m)
    return maps


def run(cfg, **inputs):
    nc = build(cfg)
    maps = make_inputs(cfg, **inputs)
    res = run_bass_kernel_spmd(nc, maps, core_ids=list(range(cfg.NCORES)))
    y = np.concatenate([res.results[c]["y"] for c in range(cfg.NCORES)], axis=0)
    return y.reshape(1, cfg.S, cfg.D).astype(np.float32)


def kernel(**inputs):
    cfg = Cfg(4096, 8192)
    return run(cfg, **inputs)
```

```python
import numpy as np
import ml_dtypes
import concourse.bass as bass
import concourse.mybir as mybir
from concourse.bass_utils import run_bass_kernel_spmd

F32 = mybir.dt.float32
BF16 = mybir.dt.bfloat16
ALU = mybir.AluOpType
AF = mybir.ActivationFunctionType
AX = mybir.AxisListType
P = 128
TB = 512
NT = TB // P
WN = 256
EPS = 1e-6
MEM_LEN = 256
POOL_WINDOWS = (2, 4, 8, 16)


class Cfg:
    def __init__(self, D, S, ncores=8):
        self.D, self.S = D, S
        self.NCORES = ncores
        self.SC = S // ncores
        self.KC = D // P
        self.H = D // 256
        self.GB = D // 4 // P
        self.MB = D // 4 // P
        self.NB = self.SC // TB
        self.NPRE = (S - self.SC) // TB
        D_ = D
        self.OFF_U, self.OFF_G = 0, D_
        self.OFF_Q, self.OFF_K = 2 * D_, 3 * D_
        self.OFF_V, self.OFF_GR = 4 * D_, 6 * D_
        self.OFF_QM, self.OFF_GM = 8 * D_, 9 * D_
        self.OFF_A = 10 * D_
        self.INW = 13 * D_


class Tk:
    __slots__ = ("key", "sem", "val")

    def __init__(self, key, sem, val):
        self.key, self.sem, self.val = key, sem, val


class Buf:
    def __init__(self, name=""):
        self.name = name
        self.w = None
        self.r = {}


class Chan:
    def __init__(self, key, sem, unit=16):
        self.key, self.sem, self.count, self.unit = key, sem, 0, unit
        self.in_barrier = True


ENGS = ("pe", "act", "dve", "pool", "sp")


class _Proxy:
    def __getattr__(self, name):
        def f(*a, **k):
            return (name, a, k)
        return f


PROXY = _Proxy()


def _replay(c):
    return lambda e: getattr(e, c[0])(*c[1], **c[2])


class Rec:
    def __init__(self, esem):
        self.q = {e: [] for e in ENGS}
        self.cnt = {e: 0 for e in ENGS}
        self.waited = {e: {} for e in ENGS}
        self.esem = esem
        self.chans = []

    def chan(self, sem, unit=16):
        c = Chan("ch%d" % len(self.chans), sem, unit)
        self.chans.append(c)
        return c

    def _wait(self, eng, tk):
        if tk is None:
            return
        if eng == "pe" and tk.key == "pe":
            return
        w = self.waited[eng]
        if w.get(tk.key, 0) >= tk.val:
            return
        w[tk.key] = tk.val
        sem, val = tk.sem, tk.val
        self.q[eng].append(lambda e: e.wait_ge(sem, val))

    def _deps(self, eng, reads, writes):
        for b in reads:
            self._wait(eng, b.w)
        for b in writes:
            self._wait(eng, b.w)
            for tk in list(b.r.values()):
                self._wait(eng, tk)

    def _commit(self, tk, reads, writes):
        for b in reads:
            b.r[tk.key] = tk
        for b in writes:
            b.w = tk
            b.r = {}

    def op(self, eng, fns, reads=(), writes=()):
        if not isinstance(fns, (list, tuple)):
            fns = [fns]
        calls = [f(PROXY) for f in fns]
        self._deps(eng, reads, writes)
        self.cnt[eng] += 1
        sem = self.esem[eng]
        q = self.q[eng]
        for c in calls[:-1]:
            q.append(_replay(c))
        last = _replay(calls[-1])
        q.append(lambda e: last(e).then_inc(sem, 1))
        tk = Tk(eng, sem, self.cnt[eng])
        self._commit(tk, reads, writes)
        return tk

    def dma(self, qeng, ch, out_ap, in_ap, reads=(), writes=()):
        self._deps(qeng, reads, writes)
        if ch.count:
            self._wait(qeng, Tk(ch.key, ch.sem, 16 * ch.count))
        ch.count += 1
        sem = ch.sem
        self.q[qeng].append(lambda e: e.dma_start(out=out_ap, in_=in_ap).then_inc(sem, 16))
        tk = Tk(ch.key, sem, 16 * ch.count)
        self._commit(tk, reads, writes)
        return tk

    def coll(self, ch, in_ap, out_ap, ranks, reads=(), writes=()):
        self._deps("pool", reads, writes)
        if ch.count:
            self._wait("pool", Tk(ch.key, ch.sem, ch.count))
        ch.count += 1
        sem = ch.sem
        self.q["pool"].append(lambda e: e.collective_compute(
            "AllGather", ALU.bypass, replica_groups=[list(range(ranks))],
            ins=[in_ap], outs=[out_ap]).then_inc(sem, 1))
        tk = Tk(ch.key, sem, ch.count)
        self._commit(tk, reads, writes)
        return tk

    def barrier(self):
        for e in ENGS:
            for e2 in ("pe", "act", "dve", "pool"):
                if self.cnt[e2] and e2 != e:
                    self._wait(e, Tk(e2, self.esem[e2], self.cnt[e2]))
            for c in self.chans:
                if c.count and c.in_barrier:
                    self._wait(e, Tk(c.key, c.sem, c.unit * c.count))

    def final_wait(self, eng):
        for e2 in ("pe", "act", "dve", "pool"):
            if self.cnt[e2] and e2 != eng:
                self._wait(eng, Tk(e2, self.esem[e2], self.cnt[e2]))
        for c in self.chans:
            if c.count:
                self._wait(eng, Tk(c.key, c.sem, c.unit * c.count))


def build(cfg):
    D, S, KC, H, GB, MB, NB = cfg.D, cfg.SC, cfg.KC, cfg.H, cfg.GB, cfg.MB, cfg.NB
    NR = cfg.NCORES
    NPRE = cfg.NPRE
    nc = bass.Bass("TRN2", target_bir_lowering=False)

    def din(name, shape, dt=F32):
        return nc.dram_tensor(name, list(shape), dt, kind="ExternalInput").ap()

    def dscr(name, shape, dt=BF16):
        return nc.dram_tensor(name, list(shape), dt).ap()

    x_d = din("x", [S, D])
    mem_d = din("mem", [MEM_LEN, D])
    y_d = nc.dram_tensor("y", [S, D], F32, kind="ExternalOutput").ap()
    nin_d = din("norm_in_rep", [P, D])
    nmem_d = din("norm_mem_rep", [P, D])
    nf_d = din("norm_f_rep", [P, D])
    pscale_d = din("pscale", [P, KC])
    cos_d = din("cos_t", [P, S])
    sin_d = din("sin_t", [P, S])
    mask_d = din("maskT", [P, H * P])
    qdec_d = din("qdec", [P, H * P])
    kdec_d = din("kdec", [P, H])
    cdec_d = din("cdec", [P, H])
    rc_d = din("rc", [P, 64])
    ident_d = din("ident", [P, P], BF16)
    xp_d = din("x_prev", [max(NPRE, 1) * TB, D])
    cosp_d = din("cosp_t", [P, max(NPRE, 1) * TB])
    sinp_d = din("sinp_t", [P, max(NPRE, 1) * TB])

    wspecs = {
        "w_in": (cfg.INW // WN, KC, WN),
        "w_pg": (4 * GB, GB, P),
        "w_mk": (D // WN, KC, WN),
        "w_mv": (D // WN, KC, WN),
        "w_pp": (D // WN, KC, WN),
        "w_pr0": (D // WN, KC, WN),
        "w_pr1": (D // WN, KC, WN),
        "w_pm": (D // WN, KC, WN),
        "w_o": (D // WN, KC, WN),
    }
    wf, wb = {}, {}
    for nm, (ntl, kc, n) in wspecs.items():
        wf[nm] = din(nm, [ntl, P, kc * n])
        wb[nm] = [dscr("%s_b%d" % (nm, i), [min(16, ntl - i), P, kc * n]) for i in range(0, ntl, 16)]

    kmT_s = dscr("kmT_s", [P, KC, MEM_LEN])
    vm_s = dscr("vm_s", [2, P, D])
    S_s = dscr("S_s", [H, P, 2 * 512], F32)

    retT_s = dscr("retT_s", [P, 2 * KC, TB])
    poolT_s = dscr("poolT_s", [P, KC, TB])
    memT_s = dscr("memT_s", [P, KC, TB])
    sig_s = dscr("sig_s", [P, 3 * KC, TB])

    from contextlib import ExitStack
    es = ExitStack()
    AW = 53200
    arena = es.enter_context(nc.sbuf_tensor("arena", [P, AW], F32))
    NBK = 6
    banks = [es.enter_context(nc.psum_tensor("pb%d" % i, [P, 512], F32)) for i in range(NBK)]
    ptrs = [(es.enter_context(nc.psum_tensor("ptr%d" % i, [P, 1024], BF16)), Buf("ptr%d" % i)) for i in range(2)]
    ptr_i = [0]
    sem_names = ["s_pe", "s_act", "s_dve", "s_pool"]
    sems = [es.enter_context(nc.semaphore(n)) for n in sem_names]
    rec = Rec({"pe": sems[0], "act": sems[1], "dve": sems[2], "pool": sems[3]})

    def newchan(name):
        return rec.chan(es.enter_context(nc.semaphore(name)))

    class Carve:
        def __init__(self, base):
            self.off = base

        def f32(self, n):
            a = arena[:, self.off:self.off + n]
            self.off += n
            return a

        def bf16(self, n):
            w = (n + 1) // 2
            a = arena[:, self.off:self.off + w].bitcast(BF16)
            self.off += w
            return a

    NSPLIT = 2
    cv = Carve(0)
    maskT = cv.f32(H * P)
    qdec = cv.f32(H * P)
    kdec = cv.f32(H)
    cdec = cv.f32(H)
    rc = cv.f32(64)
    pscale = cv.f32(KC)
    ident = cv.bf16(P)
    consts_b = Buf("consts")
    wts = [(cv.bf16(KC * WN), Buf("wt%d" % i), [newchan("c_wt%d_%d" % (i, j)) for j in range(NSPLIT)]) for i in range(3)]
    wgs = [(cv.bf16(GB * P), Buf("wg%d" % i), newchan("c_wg%d" % i)) for i in range(4)]
    sgts = [(cv.bf16(TB), Buf("sgt%d" % i), newchan("c_sg%d" % i)) for i in range(4)]
    tmpB = cv.f32(TB)
    tmpB_b = Buf("tmpB")
    small = cv.f32(32)
    tails = [(cv.bf16(KC * 16).rearrange("p (k t) -> p k t", k=KC), Buf("tail%d" % i)) for i in range(2)]
    small_b = [Buf("sm%d" % i) for i in range(8)]
    UNION = cv.off

    cA = Carve(UNION)
    hT = cA.bf16(KC * TB).rearrange("p (k t) -> p k t", k=KC)
    hT_b = Buf("hT")
    WREG = cA.off
    cP = Carve(WREG)
    css = [(cP.f32(TB), cP.f32(TB), Buf("cs%d" % i)) for i in range(2)]
    t1 = cP.f32(TB); t1_b = Buf("t1")
    t2 = cP.f32(TB); t2_b = Buf("t2")
    kTs = [(cP.bf16(2 * TB).rearrange("p (h t) -> p h t", h=2), Buf("kT%d" % i)) for i in range(2)]
    kT, kT_b = kTs[0]
    kd = cP.bf16(NT * 256).rearrange("p (c d) -> p c d", c=NT); kd_b = Buf("kd")
    vv = cP.bf16(NT * 512).rearrange("p (c e) -> p c e", c=NT); vv_b = Buf("v")
    vTs = [(cP.bf16(4 * TB).rearrange("p (b t) -> p b t", b=4), Buf("vT%d" % i)) for i in range(2)]
    vT, vT_b = vTs[0]
    Sts = [(cP.f32(1024).rearrange("p (h e) -> p h e", h=2), Buf("S%d" % i)) for i in range(2)]
    NREG = cP.off
    cN = Carve(NREG)
    xt = cN.f32(D); xt_b = Buf("xt")
    hb = cN.bf16(D); hb_b = Buf("hb")
    vecrep = cN.f32(D); vecrep_b = Buf("vecrep")
    XW0 = cN.off
    memnT = cN.bf16(KC * MEM_LEN).rearrange("p (k t) -> p k t", k=KC); memnT_b = Buf("memnT")
    stg = cN.bf16(WN); stg_b = Buf("stg")
    cX = Carve(XW0)
    n_extra = 0
    while cX.off + KC * WN // 2 <= AW and n_extra < 3:
        wts.append((cX.bf16(KC * WN), Buf("wtx%d" % n_extra), [newchan("c_wtx%d_%d" % (n_extra, j)) for j in range(NSPLIT)]))
        n_extra += 1
    wt_depth = [3]
    cM = Carve(NREG)
    qT = cM.bf16(2 * TB).rearrange("p (h t) -> p h t", h=2); qT_b = Buf("qT")
    qdT = cM.bf16(2 * TB).rearrange("p (h t) -> p h t", h=2); qdT_b = Buf("qdT")
    sgT = cM.bf16(4 * TB).rearrange("p (b t) -> p b t", b=4); sgT_b = Buf("sgT")
    Sb = cM.bf16(1024).rearrange("p (h e) -> p h e", h=2); Sb_b = Buf("Sb")
    PT = cM.bf16(P); PT_b = Buf("PT")
    retn = cM.bf16(512); retn_b = Buf("retn")
    retT = cM.bf16(4 * TB).rearrange("p (b t) -> p b t", b=4); retT_b = Buf("retT")
    junk = cM.bf16(512); junk_b = Buf("junk")

    UW = 16 + TB
    sg2 = cM.bf16(2 * TB).rearrange("p (j t) -> p j t", j=2); sg2_b = Buf("sg2")
    po = cM.bf16(2 * TB).rearrange("p (j t) -> p j t", j=2); po_b = Buf("po")
    cPl = Carve(cM.off)
    ubufs = [(cPl.f32(UW), Buf("u%d" % i)) for i in range(3)]
    mixT = cPl.bf16(GB * TB).rearrange("p (k t) -> p k t", k=GB); mixT_b = Buf("mixT")
    t16 = cPl.f32(16); t16_b = Buf("t16")
    cMm = Carve(cM.off)
    kmTh = cMm.bf16(MB * MEM_LEN).rearrange("p (k m) -> p k m", k=MB); kmTh_b = Buf("kmTh")
    vmh = cMm.bf16(2 * MB * P).rearrange("p (m e) -> p m e", m=2); vmh_b = Buf("vmh")
    qmT = cMm.bf16(MB * TB).rearrange("p (k t) -> p k t", k=MB); qmT_b = Buf("qmT")
    pex = cMm.f32(MEM_LEN); pex_b = Buf("pex")
    pn = cMm.bf16(MEM_LEN); pn_b = Buf("pn")
    pT = cMm.bf16(2 * TB).rearrange("p (m t) -> p m t", m=2); pT_b = Buf("pT")
    cM.off = max(cPl.off, cMm.off)
    own_extra = 1 if (n_extra >= 1 and cM.off <= XW0) else 0
    cB = Carve(UNION)
    HB = KC // 2
    acc = cB.f32(HB * TB).rearrange("p (b t) -> p b t", b=HB); acc_b = Buf("acc")
    pin = cB.bf16(KC * TB).rearrange("p (k t) -> p k t", k=KC); pin_b = Buf("pin")
    mT = cB.bf16(KC * TB).rearrange("p (k t) -> p k t", k=KC); mT_b = Buf("mT")
    cF = Carve(UNION)
    xo4 = cF.f32(NT * D).rearrange("p (t d) -> p t d", t=NT)
    xo_b = [Buf("xo%d" % i) for i in range(NT)]
    assert cF.off <= UNION + (HB * TB + KC * TB // 2), "final stage must not overlap mT"
    cF = Carve(cB.off)
    hb2 = cF.bf16(D); hb2_b = Buf("hb2")
    vecrep2 = cF.f32(D); vecrep2_b = Buf("vecrep2")
    assert max(cA.off, cP.off, cN.off, cM.off, cB.off, cF.off) <= AW, (cA.off, cP.off, cN.off, cM.off, cB.off, cF.off)

    bank_b = [Buf("bank%d" % i) for i in range(NBK)]
    bank_i = [0]

    def nb():
        i = bank_i[0] % NBK
        bank_i[0] += 1
        return banks[i], bank_b[i]

    ch_ld = [newchan("c_ld%d" % i) for i in range(4)]
    ch_st = [newchan("c_st%d" % i) for i in range(4)]
    ch_cast = [newchan("c_cast%d" % i) for i in range(4)]
    ch_sst = [newchan("c_sst%d" % i) for i in range(2)]
    ch_xq = newchan("c_xq")
    ld_i, st_i = [0], [0]

    def load(out_ap, in_ap, reads=(), writes=()):
        c = ch_ld[ld_i[0] % 4]; ld_i[0] += 1
        return rec.dma("sp", c, out_ap, in_ap, reads, writes)

    def store(out_ap, in_ap, reads=(), writes=()):
        c = ch_st[st_i[0] % 4]; st_i[0] += 1
        return rec.dma("pool", c, out_ap, in_ap, reads, writes)

    wb_b = {nm: [Buf("%s_%d" % (nm, t)) for t in range(wspecs[nm][0])] for nm in wspecs}
    for c_ in ch_cast:
        c_.in_barrier = False
    kmT_sb, vm_sb, S_sb = Buf("kmT_s"), Buf("vm_s"), [Buf("S_s%d" % h) for h in range(H)]

    retT_sb, poolT_sb, memT_sb, sig_sb = Buf("retT_s"), Buf("poolT_s"), Buf("memT_s"), Buf("sig_s")
    y_b = Buf("y")

    for (ap, src) in ((maskT, mask_d), (qdec, qdec_d), (kdec, kdec_d), (cdec, cdec_d), (rc, rc_d),
                      (pscale, pscale_d), (ident, ident_d)):
        load(ap, src, writes=[consts_b])
    ci = 0
    NCAST = 2
    order = []
    for h in range(H):
        order.append(("w_in", (cfg.OFF_K + h * 256) // WN))
        order += [("w_in", (cfg.OFF_V + h * 512) // WN + vt) for vt in range(2)]
    seen = set(order)
    for nm in ("w_mk", "w_mv", "w_pg", "w_in", "w_pp", "w_pr0", "w_pr1", "w_pm", "w_o"):
        for tl in range(wspecs[nm][0]):
            if (nm, tl) not in seen:
                order.append((nm, tl))
    for nm, tl in order:
        row = wspecs[nm][1] * wspecs[nm][2]
        cw = min(row, 4096)
        for c0 in range(0, row, cw):
            rec.dma("pool", ch_cast[ci % NCAST], wb[nm][tl // 16][tl % 16][:, c0:c0 + cw], wf[nm][tl][:, c0:c0 + cw],
                    writes=[wb_b[nm][tl]])
            ci += 1

    wt_i = [0]

    def _load_tile(nm, tidx, ap, b, chs):
        row = wspecs[nm][1] * wspecs[nm][2]
        src = wb[nm][tidx // 16][tidx % 16]
        hw = row // NSPLIT if row >= 2048 else row
        for i, c0 in enumerate(range(0, row, hw)):
            rec.dma("sp", chs[i % len(chs)], ap[:, c0:c0 + hw], src[:, c0:c0 + hw], reads=[wb_b[nm][tidx]], writes=[b])

    def load_wt(nm, tidx):
        ap, b, chs = wts[wt_i[0] % wt_depth[0]]
        wt_i[0] += 1
        _load_tile(nm, tidx, ap, b, chs)
        return ap.rearrange("p (k n) -> p k n", n=WN), b

    wg_i = [0]

    def load_wg(tidx):
        ap, b, ch = wgs[wg_i[0] % 4]
        wg_i[0] += 1
        src = wb["w_pg"][tidx // 16][tidx % 16]
        rec.dma("act", ch, ap, src, reads=[wb_b["w_pg"][tidx]], writes=[b])
        return ap.rearrange("p (k n) -> p k n", n=P), b

    def mm_group(out_ap, out_b, pairs, reads):
        n = len(pairs)
        fns = []
        for i, (l, r) in enumerate(pairs):
            fns.append((lambda l=l, r=r, i=i: (lambda e: e.matmul(out_ap, l, r, start=(i == 0), stop=(i == n - 1))))())
        return rec.op("pe", fns, reads=reads, writes=[out_b])

    def proj_fm(wt, wt_b, j, rhs3, rhs_b, ncols, kcn=None):
        kcn = kcn or KC
        ps, psb = nb()
        o = ps[:, 0:ncols]
        mm_group(o, psb, [(wt[:, kc, j * P:(j + 1) * P], rhs3[:, kc, 0:ncols]) for kc in range(kcn)],
                 reads=[wt_b, rhs_b])
        return o, psb

    def transposes(srcs, src_b):
        ptr_t, ptr_b = ptrs[ptr_i[0] % 2]
        ptr_i[0] += 1
        fns = []
        for i, s_ in enumerate(srcs):
            fns.append((lambda s_=s_, i=i: (lambda e: e.transpose(ptr_t[:, i * P:(i + 1) * P], s_, ident)))())
        rec.op("pe", fns, reads=[src_b, consts_b], writes=[ptr_b])
        return ptr_t, ptr_b

    def rstd_from_ssq(ssq_ap, ssq_b, n, out_ap, out_b):
        rec.op("dve", lambda e: e.tensor_scalar(out_ap, ssq_ap, 1.0 / n, EPS, ALU.mult, ALU.add),
               reads=[ssq_b], writes=[out_b])
        rec.op("act", lambda e: e.activation(out_ap, out_ap, AF.Sqrt), reads=[out_b], writes=[out_b])
        rec.op("dve", lambda e: e.reciprocal(out_ap, out_ap), reads=[out_b], writes=[out_b])

    alt = [0]

    def evac_copy(out_ap, out_b, in_ap, in_b):
        alt[0] += 1
        if alt[0] % 2:
            rec.op("act", lambda e: e.copy(out_ap, in_ap), reads=[in_b], writes=[out_b])
        else:
            rec.op("dve", lambda e: e.tensor_copy(out_ap, in_ap), reads=[in_b], writes=[out_b])

    def norm_tile_to_T(src_rows, dstT, dstT_b, tt, vrep, vrep_b, xt_, xt_b_, hb_, hb_b_, act_q=False):
        if act_q:
            rec.dma("act", ch_xq, xt_, src_rows, writes=[xt_b_])
        else:
            load(xt_, src_rows, writes=[xt_b_])
        ssq, ssq_b = small[:, 0:1], small_b[0]
        rs, rs_b = small[:, 1:2], small_b[1]
        rec.op("act", lambda e: e.activation(hb_, xt_, AF.Square, accum_out=ssq),
               reads=[xt_b_], writes=[hb_b_, ssq_b])
        rstd_from_ssq(ssq, ssq_b, D, rs, rs_b)
        rec.op("dve", lambda e: e.scalar_tensor_tensor(hb_, xt_, rs, vrep, ALU.mult, ALU.mult),
               reads=[xt_b_, rs_b, vrep_b], writes=[hb_b_])
        for k0 in range(0, KC, 4):
            ptr_t, ptr_b = transposes([hb_[:, (k0 + i) * P:(k0 + i + 1) * P] for i in range(4)], hb_b_)
            evac_copy(dstT[:, k0:k0 + 4, tt * P:(tt + 1) * P],
                      dstT_b, ptr_t[:, 0:4 * P].rearrange("p (k t) -> p k t", k=4), ptr_b)

    rec.barrier()

    msc = float((MB * P) ** -0.5)


    cs_cur = [None]
    cs_i = [0]

    def norm_stage(blk, xsrc=None, csrc=None, ssrc=None, prefix=False):
        xsrc = x_d if xsrc is None else xsrc
        csrc = cos_d if csrc is None else csrc
        ssrc = sin_d if ssrc is None else ssrc
        t0 = blk * TB
        if not prefix:
            load(vecrep, nin_d, writes=[vecrep_b])
        for tt in range(NT):
            norm_tile_to_T(xsrc[t0 + tt * P:t0 + (tt + 1) * P, :], hT, hT_b, tt, vecrep, vecrep_b, xt, xt_b, hb, hb_b,
                           act_q=prefix)
        if not prefix:
            rec.barrier()
        cosb_, sinb_, csb_ = css[cs_i[0] % 2]
        cs_i[0] += 1
        load(cosb_, csrc[:, t0:t0 + TB], writes=[csb_])
        load(sinb_, ssrc[:, t0:t0 + TB], writes=[csb_])
        cs_cur[0] = (cosb_, sinb_, csb_)

    def rope_pair(tile_idx, dst, dst_b):
        cosb, sinb, cs_b = cs_cur[0]
        wt, wt_b = load_wt("w_in", tile_idx)
        A, Ab = proj_fm(wt, wt_b, 0, hT, hT_b, TB)
        Bq, Bb = proj_fm(wt, wt_b, 1, hT, hT_b, TB)
        rec.op("dve", lambda e: e.tensor_tensor(t1, A, cosb, ALU.mult), reads=[Ab, cs_b], writes=[t1_b])
        rec.op("dve", lambda e: e.tensor_tensor(t2, Bq, sinb, ALU.mult), reads=[Bb, cs_b], writes=[t2_b])
        rec.op("dve", lambda e: e.tensor_tensor(dst[:, 0, :], t1, t2, ALU.subtract),
               reads=[t1_b, t2_b], writes=[dst_b])
        rec.op("dve", lambda e: e.tensor_tensor(t1, Bq, cosb, ALU.mult), reads=[Bb, cs_b], writes=[t1_b])
        rec.op("dve", lambda e: e.tensor_tensor(t2, A, sinb, ALU.mult), reads=[Ab, cs_b], writes=[t2_b])
        rec.op("dve", lambda e: e.tensor_tensor(dst[:, 1, :], t1, t2, ALU.add),
               reads=[t1_b, t2_b], writes=[dst_b])

    kT_f = kT.rearrange("p h t -> p (h t)")
    kd_f = kd.rearrange("p c d -> p (c d)")
    vv_f = vv.rearrange("p c e -> p (c e)")

    def state_update(h, c):
        St, St_b = Sts[h % 2]
        for hf in range(2):
            p2, p2b = nb()
            o2 = p2[:, 0:512]
            mm_group(o2, p2b, [(kd[:, c, hf * P:(hf + 1) * P], vv[:, c, :])], reads=[kd_b, vv_b])
            rec.op("dve", lambda e: e.scalar_tensor_tensor(
                St[:, hf, :], St[:, hf, :], cdec[:, h:h + 1], o2, ALU.mult, ALU.add),
                reads=[St_b, p2b, consts_b], writes=[St_b])

    def kv_proj(h, kTx, kTx_b, vTx, vTx_b):
        rope_pair((cfg.OFF_K + h * 256) // WN, kTx, kTx_b)
        for vt in range(2):
            wt, wt_b = load_wt("w_in", (cfg.OFF_V + h * 512) // WN + vt)
            for j in range(2):
                o, ob = proj_fm(wt, wt_b, j, hT, hT_b, TB)
                evac_copy(vTx[:, vt * 2 + j, :], vTx_b, o, ob)

    def kv_post(h, kTx, kTx_b, vTx, vTx_b):
        for c in range(NT):
            ptr_t, ptr_b = transposes([kTx[:, hf, c * P:(c + 1) * P] for hf in range(2)], kTx_b)
            rec.op("dve", lambda e: e.tensor_scalar(kd[:, c, :], ptr_t[:, 0:256], kdec[:, h:h + 1], None, ALU.mult),
                   reads=[ptr_b, consts_b], writes=[kd_b])
        for c in range(NT):
            ptr_t, ptr_b = transposes([vTx[:, eb, c * P:(c + 1) * P] for eb in range(4)], vTx_b)
            evac_copy(vv[:, c, :], vv_b, ptr_t[:, 0:512], ptr_b)

    def kv_compute(h):
        kv_proj(h, kT, kT_b, vT, vT_b)
        kv_post(h, kT, kT_b, vT, vT_b)

    def kv_prefix_post(pj, h):
        kTx, kTx_b = kTs[h % 2]
        vTx, vTx_b = vTs[h % 2]
        kv_post(h, kTx, kTx_b, vTx, vTx_b)
        St, St_b = Sts[h % 2]
        Sflat = St.rearrange("p h e -> p (h e)")
        if pj == 0:
            rec.op("dve", lambda e: e.memset(Sflat, 0.0), writes=[St_b])
        else:
            load(Sflat, S_s[h], reads=[S_sb[h]], writes=[St_b])
        for c in range(NT):
            state_update(h, c)
        rec.dma("act", ch_sst[h % 2], S_s[h], Sflat, reads=[St_b], writes=[S_sb[h]])

    def prefix_heads(pj):
        for h in range(H):
            kv_proj(h, kTs[h % 2][0], kTs[h % 2][1], vTs[h % 2][0], vTs[h % 2][1])
            if h >= 1:
                kv_prefix_post(pj, h - 1)
        kv_prefix_post(pj, H - 1)

    def ret_pass2(blk, h):
        St, St_b = Sts[h % 2]
        Sflat = St.rearrange("p h e -> p (h e)")
        kv_compute(h)
        rope_pair((cfg.OFF_Q + h * 256) // WN, qT, qT_b)
        qdv = qdec[:, h * P:(h + 1) * P].unsqueeze(1).to_broadcast([P, NT, P])
        for hf in range(2):
            rec.op("dve", lambda e: e.tensor_tensor(
                qdT[:, hf, :].rearrange("p (c i) -> p c i", c=NT),
                qT[:, hf, :].rearrange("p (c i) -> p c i", c=NT), qdv, ALU.mult),
                reads=[qT_b, consts_b], writes=[qdT_b])
        for gt in range(2):
            wt, wt_b = load_wt("w_in", (cfg.OFF_GR + h * 512) // WN + gt)
            for j in range(2):
                o, ob = proj_fm(wt, wt_b, j, hT, hT_b, TB)
                rec.op("act", lambda e: e.activation(sgT[:, gt * 2 + j, :], o, AF.Silu), reads=[ob], writes=[sgT_b])
        if NPRE == 0 and blk == 0:
            rec.op("pool", lambda e: e.memset(Sflat, 0.0), writes=[St_b])
        else:
            load(Sflat, S_s[h], reads=[S_sb[h]], writes=[St_b])
        for c in range(NT):
            cs = slice(c * P, (c + 1) * P)
            ps, psb = nb()
            sc = ps[:, 0:P]
            mm_group(sc, psb, [(kT[:, hf, cs], qT[:, hf, cs]) for hf in range(2)], reads=[kT_b, qT_b])
            rec.op("dve", lambda e: e.tensor_tensor(PT, sc, maskT[:, h * P:(h + 1) * P], ALU.mult),
                   reads=[psb, consts_b], writes=[PT_b])
            rec.op("act", lambda e: e.copy(Sb.rearrange("p h e -> p (h e)"), Sflat), reads=[St_b], writes=[Sb_b])
            po_, pob = nb()
            o = po_[:, 0:512]
            mm_group(o, pob, [(PT, vv[:, c, :])] + [(qdT[:, hf, cs], Sb[:, hf, :]) for hf in range(2)],
                     reads=[PT_b, vv_b, qdT_b, Sb_b])
            ssq, ssq_b = small[:, 2:3], small_b[2]
            rs, rs_b = small[:, 3:4], small_b[3]
            rec.op("act", lambda e: e.activation(junk, o, AF.Square, accum_out=ssq),
                   reads=[pob], writes=[junk_b, ssq_b])
            rstd_from_ssq(ssq, ssq_b, 512, rs, rs_b)
            rec.op("dve", lambda e: e.tensor_scalar(retn, o, rs, None, ALU.mult), reads=[pob, rs_b], writes=[retn_b])
            ptr_t, ptr_b = transposes([retn[:, eb * P:(eb + 1) * P] for eb in range(4)], retn_b)
            rec.op("dve", lambda e: e.tensor_tensor(
                retT[:, :, cs], ptr_t[:, 0:512].rearrange("p (b i) -> p b i", b=4), sgT[:, :, cs], ALU.mult),
                reads=[ptr_b, sgT_b], writes=[retT_b])
            if not (blk == NB - 1 and c == NT - 1):
                state_update(h, c)
        if blk < NB - 1:
            store(S_s[h], Sflat, reads=[St_b], writes=[S_sb[h]])
        store(retT_s[:, h * 4:(h + 1) * 4, :], retT, reads=[retT_b], writes=[retT_sb])

    if NPRE == 0:
        rec.op("dve", lambda e: e.memset(tails[0][0], 0.0), writes=[tails[0][1]])
    if NPRE:
        load(vecrep, nin_d, writes=[vecrep_b])
        wt_depth[0] = 3 + n_extra
    for pj in range(NPRE):
        norm_stage(pj, xp_d, cosp_d, sinp_d, prefix=True)
        prefix_heads(pj)
        if pj == NPRE - 1:
            rec.op("dve", lambda e: e.tensor_copy(tails[0][0], hT[:, :, TB - 16:TB]), reads=[hT_b], writes=[tails[0][1]])
    rec.barrier()
    wt_depth[0] = 3

    load(vecrep, nmem_d, writes=[vecrep_b])
    for mt in range(2):
        norm_tile_to_T(mem_d[mt * P:(mt + 1) * P, :], memnT, memnT_b, mt, vecrep, vecrep_b, xt, xt_b, hb, hb_b)
    for kt in range(D // WN):
        wt, wt_b = load_wt("w_mk", kt)
        for j in range(2):
            o, ob = proj_fm(wt, wt_b, j, memnT, memnT_b, MEM_LEN)
            evac_copy(stg, stg_b, o, ob)
            store(kmT_s[:, kt * 2 + j, :], stg, reads=[stg_b], writes=[kmT_sb])
    for vt in range(D // WN):
        wt, wt_b = load_wt("w_mv", vt)
        for mt in range(2):
            ps, psb = nb()
            o = ps[:, 0:WN]
            mm_group(o, psb, [(memnT[:, kc, mt * P:(mt + 1) * P], wt[:, kc, :]) for kc in range(KC)],
                     reads=[wt_b, memnT_b])
            evac_copy(stg, stg_b, o, psb)
            store(vm_s[mt][:, vt * WN:(vt + 1) * WN], stg, reads=[stg_b], writes=[vm_sb])
    rec.barrier()

    wt_depth[0] = 3 + own_extra
    for blk in range(NB):
        t0 = blk * TB
        hTt, hTt_b = tails[blk % 2]
        norm_stage(blk)

        for g in range(4):
            w = POOL_WINDOWS[g]
            for ut in range(GB * P // WN):
                wt, wt_b = load_wt("w_in", (cfg.OFF_U + g * GB * P) // WN + ut)
                for j in range(2):
                    ub_i = ut * 2 + j
                    o, ob = proj_fm(wt, wt_b, j, hT, hT_b, TB)
                    oh, ohb = proj_fm(wt, wt_b, j, hTt, hTt_b, 16)
                    ua, ua_b = ubufs[0]
                    rec.op("act", (lambda o=o: lambda e: e.copy(ua[:, 16:UW], o))(), reads=[ob], writes=[ua_b])
                    rec.op("act", (lambda oh=oh: lambda e: e.copy(ua[:, 0:16], oh))(), reads=[ohb], writes=[ua_b])
                    cur, cur_b = ua, ua_b
                    off, step, pi_ = 0, 1, 1
                    while step < w:
                        nx, nx_b = ubufs[pi_]
                        lo = off + step
                        rec.op("dve", (lambda nx=nx, cur=cur, lo=lo, step=step: lambda e: e.tensor_tensor(
                            nx[:, lo:UW], cur[:, lo:UW], cur[:, lo - step:UW - step], ALU.add))(),
                            reads=[cur_b], writes=[nx_b])
                        cur, cur_b = nx, nx_b
                        off = lo
                        step *= 2
                        pi_ = 3 - pi_
                    rec.op("dve", (lambda cur=cur, ub_i=ub_i: lambda e: e.scalar_tensor_tensor(
                        mixT[:, ub_i, :], cur[:, 16:UW], 1.0 / w, ua[:, 16:UW], ALU.mult, ALU.subtract))(),
                        reads=[cur_b, ua_b], writes=[mixT_b])
                    if blk == 0:
                        rec.op("dve", (lambda cur=cur: lambda e: e.tensor_tensor(
                            t16, cur[:, 16:32], rc[:, g * 16:(g + 1) * 16], ALU.mult))(),
                            reads=[cur_b, consts_b], writes=[t16_b])
                        rec.op("dve", (lambda ub_i=ub_i: lambda e: e.tensor_tensor(
                            mixT[:, ub_i, 0:16], t16, ua[:, 16:32], ALU.subtract))(),
                            reads=[t16_b, ua_b], writes=[mixT_b])
            for ot in range(GB * P // WN):
                wt, wt_b = load_wt("w_in", (cfg.OFF_G + g * GB * P) // WN + ot)
                for j in range(2):
                    o, ob = proj_fm(wt, wt_b, j, hT, hT_b, TB)
                    rec.op("act", (lambda o=o, j=j: lambda e: e.activation(sg2[:, j, :], o, AF.Silu))(),
                           reads=[ob], writes=[sg2_b])
                for j in range(2):
                    oblk = g * GB + ot * 2 + j
                    wg, wg_b = load_wg(oblk)
                    ps, psb = nb()
                    o = ps[:, 0:TB]
                    mm_group(o, psb, [(wg[:, kc, :], mixT[:, kc, :]) for kc in range(GB)], reads=[wg_b, mixT_b])
                    rec.op("dve", (lambda o=o, j=j, oblk=oblk: lambda e: e.scalar_tensor_tensor(
                        po[:, j, :], o, pscale[:, oblk:oblk + 1], sg2[:, j, :], ALU.mult, ALU.mult))(),
                        reads=[psb, sg2_b, consts_b], writes=[po_b])
                ob0 = g * GB + ot * 2
                store(poolT_s[:, ob0:ob0 + 2, :], po, reads=[po_b], writes=[poolT_sb])

        rec.barrier()
        for hm in range(4):
            load(kmTh, kmT_s[:, hm * MB:(hm + 1) * MB, :], reads=[kmT_sb], writes=[kmTh_b])
            load(vmh, vm_s[:, :, hm * MB * P:(hm + 1) * MB * P].rearrange("m p e -> p m e"),
                 reads=[vm_sb], writes=[vmh_b])
            for qt in range(MB * P // WN):
                wt, wt_b = load_wt("w_in", (cfg.OFF_QM + hm * MB * P) // WN + qt)
                for j in range(2):
                    o, ob = proj_fm(wt, wt_b, j, hT, hT_b, TB)
                    evac_copy(qmT[:, qt * 2 + j, :], qmT_b, o, ob)
            for c in range(NT):
                cs = slice(c * P, (c + 1) * P)
                ps, psb = nb()
                sc = ps[:, 0:MEM_LEN]
                mm_group(sc, psb, [(qmT[:, kc, cs], kmTh[:, kc, :]) for kc in range(MB)], reads=[qmT_b, kmTh_b])
                mx, mx_b = small[:, 4:5], small_b[4]
                sm, sm_b = small[:, 5:6], small_b[5]
                rec.op("dve", (lambda sc=sc: lambda e: e.reduce_max(mx, sc, AX.X))(), reads=[psb], writes=[mx_b])
                rec.op("dve", lambda e: e.tensor_scalar(mx, mx, -msc, None, ALU.mult), reads=[mx_b], writes=[mx_b])
                rec.op("act", (lambda sc=sc: lambda e: e.activation(pex, sc, AF.Exp, bias=mx, scale=msc, accum_out=sm))(),
                       reads=[psb, mx_b], writes=[pex_b, sm_b])
                rec.op("dve", lambda e: e.reciprocal(sm, sm), reads=[sm_b], writes=[sm_b])
                rec.op("dve", lambda e: e.tensor_scalar(pn, pex, sm, None, ALU.mult), reads=[pex_b, sm_b], writes=[pn_b])
                ptr_t, ptr_b = transposes([pn[:, m * P:(m + 1) * P] for m in range(2)], pn_b)
                rec.op("dve", (lambda cs=cs: lambda e: e.tensor_copy(
                    pT[:, :, cs], ptr_t[:, 0:256].rearrange("p (m i) -> p m i", m=2)))(),
                    reads=[ptr_b], writes=[pT_b])
            for dt_ in range(MB * P // WN):
                wt, wt_b = load_wt("w_in", (cfg.OFF_GM + hm * MB * P) // WN + dt_)
                for j in range(2):
                    o, ob = proj_fm(wt, wt_b, j, hT, hT_b, TB)
                    rec.op("act", (lambda o=o, j=j: lambda e: e.activation(sg2[:, j, :], o, AF.Silu))(),
                           reads=[ob], writes=[sg2_b])
                for j in range(2):
                    eb = dt_ * 2 + j
                    ps, psb = nb()
                    o = ps[:, 0:TB]
                    mm_group(o, psb, [(vmh[:, m, eb * P:(eb + 1) * P], pT[:, m, :]) for m in range(2)],
                             reads=[vmh_b, pT_b])
                    rec.op("dve", (lambda o=o, j=j: lambda e: e.tensor_tensor(po[:, j, :], o, sg2[:, j, :], ALU.mult))(),
                           reads=[psb, sg2_b], writes=[po_b])
                ob0 = hm * MB + dt_ * 2
                store(memT_s[:, ob0:ob0 + 2, :], po, reads=[po_b], writes=[memT_sb])

        for a in range(3):
            for t in range(D // WN):
                wt, wt_b = load_wt("w_in", (cfg.OFF_A + a * D) // WN + t)
                for j in range(2):
                    o, ob = proj_fm(wt, wt_b, j, hT, hT_b, TB)
                    rec.op("act", (lambda o=o, j=j: lambda e: e.activation(po[:, j, :], o, AF.Sigmoid))(),
                           reads=[ob], writes=[po_b])
                ob0 = a * KC + t * 2
                store(sig_s[:, ob0:ob0 + 2, :], po, reads=[po_b], writes=[sig_sb])
        for h in range(H):
            ret_pass2(blk, h)
        nxt, nxt_b = tails[(blk + 1) % 2]
        rec.op("pool", lambda e: e.tensor_copy(nxt, hT[:, :, TB - 16:TB]), reads=[hT_b], writes=[nxt_b])
        rec.barrier()

        pieces = [(poolT_s, poolT_sb, 0, "w_pp", 0), (retT_s, retT_sb, 0, "w_pr0", 1),
                  (retT_s, retT_sb, KC, "w_pr1", 1), (memT_s, memT_sb, 0, "w_pm", 2)]
        sg_i = 0
        for dh in range(2):
            for pi, (src, src_b, boff, wn, a) in enumerate(pieces):
                for q4 in range(4):
                    k0_, k1_ = q4 * KC // 4, (q4 + 1) * KC // 4
                    load(pin[:, k0_:k1_, :], src[:, boff + k0_:boff + k1_, :],
                         reads=[src_b], writes=[pin_b])
                for t in range(HB // 2):
                    tg = dh * (HB // 2) + t
                    wt, wt_b = load_wt(wn, tg)
                    for j in range(2):
                        ob_ = tg * 2 + j
                        lb = t * 2 + j
                        o, ob = proj_fm(wt, wt_b, j, pin, pin_b, TB)
                        sgt, sgt_b, sch = sgts[sg_i % 4]
                        sg_i += 1
                        rec.dma("act", sch, sgt, sig_s[:, a * KC + ob_, :], reads=[sig_sb], writes=[sgt_b])
                        if pi == 0:
                            rec.op("dve", (lambda o=o, lb=lb, sgt=sgt: lambda e: e.tensor_tensor(
                                acc[:, lb, :], o, sgt, ALU.mult))(), reads=[ob, sgt_b], writes=[acc_b])
                        else:
                            rec.op("dve", (lambda o=o, sgt=sgt: lambda e: e.tensor_tensor(tmpB, o, sgt, ALU.mult))(),
                                   reads=[ob, sgt_b], writes=[tmpB_b])
                            rec.op("pool", (lambda lb=lb: lambda e: e.tensor_tensor(
                                acc[:, lb, :], acc[:, lb, :], tmpB, ALU.add))(), reads=[tmpB_b, acc_b], writes=[acc_b])
            rec.op("act", (lambda dh=dh: lambda e: e.copy(
                mT[:, dh * HB:(dh + 1) * HB, :].rearrange("p b t -> p (b t)"),
                acc.rearrange("p b t -> p (b t)")))(), reads=[acc_b], writes=[mT_b])
        rec.barrier()

        wt_depth[0] = 3
        load(vecrep2, nf_d, writes=[vecrep2_b])
        for tt in range(NT):
            load(xo4[:, tt, :], x_d[t0 + tt * P:t0 + (tt + 1) * P, :], writes=[xo_b[tt]])
        for t in range(D // WN):
            wt, wt_b = load_wt("w_o", t)
            for tt in range(NT):
                ps, psb = nb()
                o = ps[:, 0:WN]
                mm_group(o, psb, [(mT[:, kc, tt * P:(tt + 1) * P], wt[:, kc, :]) for kc in range(KC)],
                         reads=[wt_b, mT_b])
                rec.op("dve", lambda e: e.tensor_tensor(
                    xo4[:, tt, t * WN:(t + 1) * WN], xo4[:, tt, t * WN:(t + 1) * WN], o, ALU.add),
                    reads=[psb, xo_b[tt]], writes=[xo_b[tt]])
        for tt in range(NT):
            ssq, ssq_b = small[:, 6:7], small_b[6]
            rs, rs_b = small[:, 7:8], small_b[7]
            rec.op("act", lambda e: e.activation(hb2, xo4[:, tt, :], AF.Square, accum_out=ssq),
                   reads=[xo_b[tt]], writes=[hb2_b, ssq_b])
            rstd_from_ssq(ssq, ssq_b, D, rs, rs_b)
            rec.op("dve", lambda e: e.scalar_tensor_tensor(xo4[:, tt, :], xo4[:, tt, :], rs, vecrep2, ALU.mult, ALU.mult),
                   reads=[xo_b[tt], rs_b, vecrep2_b], writes=[xo_b[tt]])
            store(y_d[t0 + tt * P:t0 + (tt + 1) * P, :], xo4[:, tt, :], reads=[xo_b[tt]], writes=[y_b])
        rec.barrier()
        wt_depth[0] = 3 + own_extra

    rec.final_wait("sp")

    with nc.Block() as block:
        @block.tensor
        def _(e):
            for f in rec.q["pe"]:
                f(e)

        @block.scalar
        def _(e):
            for f in rec.q["act"]:
                f(e)

        @block.vector
        def _(e):
            for f in rec.q["dve"]:
                f(e)

        @block.gpsimd
        def _(e):
            for f in rec.q["pool"]:
                f(e)

        @block.sync
        def _(e):
            for f in rec.q["sp"]:
                f(e)
    es.close()
    return nc


def tile_major(w, n):
    K, C = w.shape
    return np.ascontiguousarray(
        w.reshape(K // P, P, C // n, n).transpose(2, 1, 0, 3)).reshape(C // n, P, (K // P) * n)


def const_tables(cfg, core):
    S, H, SC = cfg.S, cfg.H, cfg.SC
    half = 128
    inv = 10000.0 ** (-np.arange(half, dtype=np.float64) / half)
    pos = np.arange(core * SC, (core + 1) * SC, dtype=np.float64)
    ang = inv[:, None] * pos[None, :]
    lg = np.log1p(-(2.0 ** (-5.0 - np.arange(H, dtype=np.float64))))
    i = np.arange(P, dtype=np.float64)
    diff = i[None, :] - i[:, None]
    maskT = np.where(diff[None] >= 0, np.exp(np.maximum(diff, 0)[None] * lg[:, None, None]), 0.0) / 16.0
    maskT = maskT.transpose(1, 0, 2).reshape(P, H * P)
    qdec = np.exp((i + 1.0)[None, :] * lg[:, None])
    qdec = np.broadcast_to(qdec.reshape(1, H * P), (P, H * P))
    kdec = (np.exp((P - 1.0 - i)[:, None] * lg[None, :]) / 16.0)
    cdec = np.broadcast_to(np.exp(P * lg)[None, :], (P, H))
    rc = np.zeros((P, 64))
    for g, w in enumerate(POOL_WINDOWS):
        cnt = np.minimum(np.arange(16) + 1, w) if core == 0 else np.full(16, w)
        rc[:, g * 16:(g + 1) * 16] = 1.0 / cnt[None, :]
    npre = max(cfg.NPRE, 1) * TB
    ppos = np.arange(core * SC - npre, core * SC, dtype=np.float64)
    pang = inv[:, None] * ppos[None, :]
    f = lambda a: np.ascontiguousarray(a, dtype=np.float32)
    return dict(cos_t=f(np.cos(ang)), sin_t=f(np.sin(ang)), maskT=f(maskT), qdec=f(qdec), kdec=f(kdec),
                cdec=f(cdec), rc=f(rc), cosp_t=f(np.cos(pang)), sinp_t=f(np.sin(pang)), ident=np.eye(P, dtype=np.float32).astype(ml_dtypes.bfloat16))


def make_inputs(cfg, x, mem, norm_in, norm_mem, w_in, w_pool_group, pool_scale, w_mem_k, w_mem_v,
                w_proj_pool, w_proj_ret, w_proj_mem, w_out, norm_f):
    D, NR, SC = cfg.D, cfg.NCORES, cfg.SC
    rep = lambda v: np.ascontiguousarray(np.broadcast_to(np.asarray(v, np.float32).reshape(1, D), (P, D)))
    common = {}
    common["mem"] = np.ascontiguousarray(np.asarray(mem, np.float32).reshape(MEM_LEN, D))
    common["norm_in_rep"] = rep(norm_in)
    common["norm_mem_rep"] = rep(norm_mem)
    common["norm_f_rep"] = rep(norm_f)
    common["pscale"] = np.ascontiguousarray(np.asarray(pool_scale, np.float32).reshape(cfg.KC, P).T)
    wt = {}
    wt["w_in"] = tile_major(np.asarray(w_in, np.float32).reshape(D, cfg.INW), WN)
    wpg = np.asarray(w_pool_group, np.float32).reshape(4, D // 4, D // 4)
    wt["w_pg"] = np.concatenate([tile_major(wpg[g], P) for g in range(4)], 0)
    wt["w_mk"] = tile_major(np.asarray(w_mem_k, np.float32).reshape(D, D), WN)
    wt["w_mv"] = tile_major(np.asarray(w_mem_v, np.float32).reshape(D, D), WN)
    wt["w_pp"] = tile_major(np.asarray(w_proj_pool, np.float32).reshape(D, D), WN)
    wpr = np.asarray(w_proj_ret, np.float32).reshape(2 * D, D)
    wt["w_pr0"] = tile_major(wpr[:D], WN)
    wt["w_pr1"] = tile_major(wpr[D:], WN)
    wt["w_pm"] = tile_major(np.asarray(w_proj_mem, np.float32).reshape(D, D), WN)
    wt["w_o"] = tile_major(np.asarray(w_out, np.float32).reshape(D, D), WN)
    xf = np.asarray(x, np.float32).reshape(cfg.S, D)
    maps = []
    for c in range(NR):
        m = dict(common)
        m.update(const_tables(cfg, c))
        m["x"] = np.ascontiguousarray(xf[c * SC:(c + 1) * SC])
        npre = max(cfg.NPRE, 1) * TB
        xp = np.zeros((npre, D), np.float32)
        if c > 0:
            n = min(npre, c * SC)
            xp[npre - n:] = xf[c * SC - n:c * SC]
        m["x_prev"] = xp
        m.update(wt)
        maps.append(m)
    return maps


def run(cfg, **inputs):
    nc = build(cfg)
    maps = make_inputs(cfg, **inputs)
    res = run_bass_kernel_spmd(nc, maps, core_ids=list(range(cfg.NCORES)))
    y = np.concatenate([res.results[c]["y"] for c in range(cfg.NCORES)], axis=0)
    return y.reshape(1, cfg.S, cfg.D).astype(np.float32)


def kernel(**inputs):
    cfg = Cfg(4096, 8192)
    return run(cfg, **inputs)
```

```python
import numpy as np
import ml_dtypes
import concourse.bass as bass
import concourse.mybir as mybir
from concourse.bass_utils import run_bass_kernel_spmd

F32 = mybir.dt.float32
BF16 = mybir.dt.bfloat16
ALU = mybir.AluOpType
AF = mybir.ActivationFunctionType
AX = mybir.AxisListType
P = 128
TB = 512
NT = TB // P
WN = 256
EPS = 1e-6
MEM_LEN = 256
POOL_WINDOWS = (2, 4, 8, 16)


class Cfg:
    def __init__(self, D, S, ncores=8):
        self.D, self.S = D, S
        self.NCORES = ncores
        self.SC = S // ncores
        self.KC = D // P
        self.H = D // 256
        self.GB = D // 4 // P
        self.MB = D // 4 // P
        self.NB = self.SC // TB
        self.NPRE = (S - self.SC) // TB
        D_ = D
        self.OFF_U, self.OFF_G = 0, D_
        self.OFF_Q, self.OFF_K = 2 * D_, 3 * D_
        self.OFF_V, self.OFF_GR = 4 * D_, 6 * D_
        self.OFF_QM, self.OFF_GM = 8 * D_, 9 * D_
        self.OFF_A = 10 * D_
        self.INW = 13 * D_


class Tk:
    __slots__ = ("key", "sem", "val")

    def __init__(self, key, sem, val):
        self.key, self.sem, self.val = key, sem, val


class Buf:
    def __init__(self, name=""):
        self.name = name
        self.w = None
        self.r = {}


class Chan:
    def __init__(self, key, sem, unit=16):
        self.key, self.sem, self.count, self.unit = key, sem, 0, unit
        self.in_barrier = True


ENGS = ("pe", "act", "dve", "pool", "sp")


class _Proxy:
    def __getattr__(self, name):
        def f(*a, **k):
            return (name, a, k)
        return f


PROXY = _Proxy()


def _replay(c):
    return lambda e: getattr(e, c[0])(*c[1], **c[2])


class Rec:
    def __init__(self, esem):
        self.q = {e: [] for e in ENGS}
        self.cnt = {e: 0 for e in ENGS}
        self.waited = {e: {} for e in ENGS}
        self.esem = esem
        self.chans = []

    def chan(self, sem, unit=16):
        c = Chan("ch%d" % len(self.chans), sem, unit)
        self.chans.append(c)
        return c

    def _wait(self, eng, tk):
        if tk is None:
            return
        if eng == "pe" and tk.key == "pe":
            return
        w = self.waited[eng]
        if w.get(tk.key, 0) >= tk.val:
            return
        w[tk.key] = tk.val
        sem, val = tk.sem, tk.val
        self.q[eng].append(lambda e: e.wait_ge(sem, val))

    def _deps(self, eng, reads, writes):
        for b in reads:
            self._wait(eng, b.w)
        for b in writes:
            self._wait(eng, b.w)
            for tk in list(b.r.values()):
                self._wait(eng, tk)

    def _commit(self, tk, reads, writes):
        for b in reads:
            b.r[tk.key] = tk
        for b in writes:
            b.w = tk
            b.r = {}

    def op(self, eng, fns, reads=(), writes=()):
        if not isinstance(fns, (list, tuple)):
            fns = [fns]
        calls = [f(PROXY) for f in fns]
        self._deps(eng, reads, writes)
        self.cnt[eng] += 1
        sem = self.esem[eng]
        q = self.q[eng]
        for c in calls[:-1]:
            q.append(_replay(c))
        last = _replay(calls[-1])
        q.append(lambda e: last(e).then_inc(sem, 1))
        tk = Tk(eng, sem, self.cnt[eng])
        self._commit(tk, reads, writes)
        return tk

    def dma(self, qeng, ch, out_ap, in_ap, reads=(), writes=()):
        self._deps(qeng, reads, writes)
        if ch.count:
            self._wait(qeng, Tk(ch.key, ch.sem, 16 * ch.count))
        ch.count += 1
        sem = ch.sem
        self.q[qeng].append(lambda e: e.dma_start(out=out_ap, in_=in_ap).then_inc(sem, 16))
        tk = Tk(ch.key, sem, 16 * ch.count)
        self._commit(tk, reads, writes)
        return tk

    def coll(self, ch, in_ap, out_ap, ranks, reads=(), writes=()):
        self._deps("pool", reads, writes)
        if ch.count:
            self._wait("pool", Tk(ch.key, ch.sem, ch.count))
        ch.count += 1
        sem = ch.sem
        self.q["pool"].append(lambda e: e.collective_compute(
            "AllGather", ALU.bypass, replica_groups=[list(range(ranks))],
            ins=[in_ap], outs=[out_ap]).then_inc(sem, 1))
        tk = Tk(ch.key, sem, ch.count)
        self._commit(tk, reads, writes)
        return tk

    def barrier(self):
        for e in ENGS:
            for e2 in ("pe", "act", "dve", "pool"):
                if self.cnt[e2] and e2 != e:
                    self._wait(e, Tk(e2, self.esem[e2], self.cnt[e2]))
            for c in self.chans:
                if c.count and c.in_barrier:
                    self._wait(e, Tk(c.key, c.sem, c.unit * c.count))

    def final_wait(self, eng):
        for e2 in ("pe", "act", "dve", "pool"):
            if self.cnt[e2] and e2 != eng:
                self._wait(eng, Tk(e2, self.esem[e2], self.cnt[e2]))
        for c in self.chans:
            if c.count:
                self._wait(eng, Tk(c.key, c.sem, c.unit * c.count))


def build(cfg):
    D, S, KC, H, GB, MB, NB = cfg.D, cfg.SC, cfg.KC, cfg.H, cfg.GB, cfg.MB, cfg.NB
    NR = cfg.NCORES
    NPRE = cfg.NPRE
    nc = bass.Bass("TRN2", target_bir_lowering=False)

    def din(name, shape, dt=F32):
        return nc.dram_tensor(name, list(shape), dt, kind="ExternalInput").ap()

    def dscr(name, shape, dt=BF16):
        return nc.dram_tensor(name, list(shape), dt).ap()

    x_d = din("x", [S, D])
    mem_d = din("mem", [MEM_LEN, D])
    y_d = nc.dram_tensor("y", [S, D], F32, kind="ExternalOutput").ap()
    nin_d = din("norm_in_rep", [P, D])
    nmem_d = din("norm_mem_rep", [P, D])
    nf_d = din("norm_f_rep", [P, D])
    pscale_d = din("pscale", [P, KC])
    cos_d = din("cos_t", [P, S])
    sin_d = din("sin_t", [P, S])
    mask_d = din("maskT", [P, H * P])
    qdec_d = din("qdec", [P, H * P])
    kdec_d = din("kdec", [P, H])
    cdec_d = din("cdec", [P, H])
    rc_d = din("rc", [P, 64])
    ident_d = din("ident", [P, P], BF16)
    xp_d = din("x_prev", [max(NPRE, 1) * TB, D])
    cosp_d = din("cosp_t", [P, max(NPRE, 1) * TB])
    sinp_d = din("sinp_t", [P, max(NPRE, 1) * TB])

    wspecs = {
        "w_in": (cfg.INW // WN, KC, WN),
        "w_pg": (4 * GB, GB, P),
        "w_mk": (D // WN, KC, WN),
        "w_mv": (D // WN, KC, WN),
        "w_pp": (D // WN, KC, WN),
        "w_pr0": (D // WN, KC, WN),
        "w_pr1": (D // WN, KC, WN),
        "w_pm": (D // WN, KC, WN),
        "w_o": (D // WN, KC, WN),
    }
    wf, wb = {}, {}
    for nm, (ntl, kc, n) in wspecs.items():
        wf[nm] = din(nm, [ntl, P, kc * n])
        wb[nm] = [dscr("%s_b%d" % (nm, i), [min(16, ntl - i), P, kc * n]) for i in range(0, ntl, 16)]

    kmT_s = dscr("kmT_s", [P, KC, MEM_LEN])
    vm_s = dscr("vm_s", [2, P, D])
    S_s = dscr("S_s", [H, P, 2 * 512], F32)

    retT_s = dscr("retT_s", [P, 2 * KC, TB])
    poolT_s = dscr("poolT_s", [P, KC, TB])
    memT_s = dscr("memT_s", [P, KC, TB])
    sig_s = dscr("sig_s", [P, 3 * KC, TB])

    from contextlib import ExitStack
    es = ExitStack()
    AW = 53200
    arena = es.enter_context(nc.sbuf_tensor("arena", [P, AW], F32))
    NBK = 6
    banks = [es.enter_context(nc.psum_tensor("pb%d" % i, [P, 512], F32)) for i in range(NBK)]
    ptrs = [(es.enter_context(nc.psum_tensor("ptr%d" % i, [P, 1024], BF16)), Buf("ptr%d" % i)) for i in range(2)]
    ptr_i = [0]
    sem_names = ["s_pe", "s_act", "s_dve", "s_pool"]
    sems = [es.enter_context(nc.semaphore(n)) for n in sem_names]
    rec = Rec({"pe": sems[0], "act": sems[1], "dve": sems[2], "pool": sems[3]})

    def newchan(name):
        return rec.chan(es.enter_context(nc.semaphore(name)))

    class Carve:
        def __init__(self, base):
            self.off = base

        def f32(self, n):
            a = arena[:, self.off:self.off + n]
            self.off += n
            return a

        def bf16(self, n):
            w = (n + 1) // 2
            a = arena[:, self.off:self.off + w].bitcast(BF16)
            self.off += w
            return a

    NSPLIT = 2
    cv = Carve(0)
    maskT = cv.f32(H * P)
    qdec = cv.f32(H * P)
    kdec = cv.f32(H)
    cdec = cv.f32(H)
    rc = cv.f32(64)
    pscale = cv.f32(KC)
    ident = cv.bf16(P)
    consts_b = Buf("consts")
    wts = [(cv.bf16(KC * WN), Buf("wt%d" % i), [newchan("c_wt%d_%d" % (i, j)) for j in range(NSPLIT)]) for i in range(3)]
    wgs = [(cv.bf16(GB * P), Buf("wg%d" % i), newchan("c_wg%d" % i)) for i in range(4)]
    sgts = [(cv.bf16(TB), Buf("sgt%d" % i), newchan("c_sg%d" % i)) for i in range(4)]
    tmpB = cv.f32(TB)
    tmpB_b = Buf("tmpB")
    small = cv.f32(32)
    tails = [(cv.bf16(KC * 16).rearrange("p (k t) -> p k t", k=KC), Buf("tail%d" % i)) for i in range(2)]
    small_b = [Buf("sm%d" % i) for i in range(8)]
    UNION = cv.off

    cA = Carve(UNION)
    hT = cA.bf16(KC * TB).rearrange("p (k t) -> p k t", k=KC)
    hT_b = Buf("hT")
    WREG = cA.off
    cP = Carve(WREG)
    css = [(cP.f32(TB), cP.f32(TB), Buf("cs%d" % i)) for i in range(2)]
    t1 = cP.f32(TB); t1_b = Buf("t1")
    t2 = cP.f32(TB); t2_b = Buf("t2")
    kTs = [(cP.bf16(2 * TB).rearrange("p (h t) -> p h t", h=2), Buf("kT%d" % i)) for i in range(2)]
    kT, kT_b = kTs[0]
    kd = cP.bf16(NT * 256).rearrange("p (c d) -> p c d", c=NT); kd_b = Buf("kd")
    vv = cP.bf16(NT * 512).rearrange("p (c e) -> p c e", c=NT); vv_b = Buf("v")
    vTs = [(cP.bf16(4 * TB).rearrange("p (b t) -> p b t", b=4), Buf("vT%d" % i)) for i in range(2)]
    vT, vT_b = vTs[0]
    Sts = [(cP.f32(1024).rearrange("p (h e) -> p h e", h=2), Buf("S%d" % i)) for i in range(2)]
    NREG = cP.off
    cN = Carve(NREG)
    xt = cN.f32(D); xt_b = Buf("xt")
    hb = cN.bf16(D); hb_b = Buf("hb")
    vecrep = cN.f32(D); vecrep_b = Buf("vecrep")
    XW0 = cN.off
    memnT = cN.bf16(KC * MEM_LEN).rearrange("p (k t) -> p k t", k=KC); memnT_b = Buf("memnT")
    stg = cN.bf16(WN); stg_b = Buf("stg")
    cX = Carve(XW0)
    n_extra = 0
    while cX.off + KC * WN // 2 <= AW and n_extra < 3:
        wts.append((cX.bf16(KC * WN), Buf("wtx%d" % n_extra), [newchan("c_wtx%d_%d" % (n_extra, j)) for j in range(NSPLIT)]))
        n_extra += 1
    wt_depth = [3]
    cM = Carve(NREG)
    qT = cM.bf16(2 * TB).rearrange("p (h t) -> p h t", h=2); qT_b = Buf("qT")
    qdT = cM.bf16(2 * TB).rearrange("p (h t) -> p h t", h=2); qdT_b = Buf("qdT")
    sgT = cM.bf16(4 * TB).rearrange("p (b t) -> p b t", b=4); sgT_b = Buf("sgT")
    Sb = cM.bf16(1024).rearrange("p (h e) -> p h e", h=2); Sb_b = Buf("Sb")
    PT = cM.bf16(P); PT_b = Buf("PT")
    retn = cM.bf16(512); retn_b = Buf("retn")
    retT = cM.bf16(4 * TB).rearrange("p (b t) -> p b t", b=4); retT_b = Buf("retT")
    junk = cM.bf16(512); junk_b = Buf("junk")

    UW = 16 + TB
    sg2 = cM.bf16(2 * TB).rearrange("p (j t) -> p j t", j=2); sg2_b = Buf("sg2")
    po = cM.bf16(2 * TB).rearrange("p (j t) -> p j t", j=2); po_b = Buf("po")
    cPl = Carve(cM.off)
    ubufs = [(cPl.f32(UW), Buf("u%d" % i)) for i in range(3)]
    mixT = cPl.bf16(GB * TB).rearrange("p (k t) -> p k t", k=GB); mixT_b = Buf("mixT")
    t16 = cPl.f32(16); t16_b = Buf("t16")
    cMm = Carve(cM.off)
    kmTh = cMm.bf16(MB * MEM_LEN).rearrange("p (k m) -> p k m", k=MB); kmTh_b = Buf("kmTh")
    vmh = cMm.bf16(2 * MB * P).rearrange("p (m e) -> p m e", m=2); vmh_b = Buf("vmh")
    qmT = cMm.bf16(MB * TB).rearrange("p (k t) -> p k t", k=MB); qmT_b = Buf("qmT")
    pex = cMm.f32(MEM_LEN); pex_b = Buf("pex")
    pn = cMm.bf16(MEM_LEN); pn_b = Buf("pn")
    pT = cMm.bf16(2 * TB).rearrange("p (m t) -> p m t", m=2); pT_b = Buf("pT")
    cM.off = max(cPl.off, cMm.off)
    own_extra = 1 if (n_extra >= 1 and cM.off <= XW0) else 0
    cB = Carve(UNION)
    HB = KC // 2
    acc = cB.f32(HB * TB).rearrange("p (b t) -> p b t", b=HB); acc_b = Buf("acc")
    pin = cB.bf16(KC * TB).rearrange("p (k t) -> p k t", k=KC); pin_b = Buf("pin")
    mT = cB.bf16(KC * TB).rearrange("p (k t) -> p k t", k=KC); mT_b = Buf("mT")
    cF = Carve(UNION)
    xo4 = cF.f32(NT * D).rearrange("p (t d) -> p t d", t=NT)
    xo_b = [Buf("xo%d" % i) for i in range(NT)]
    assert cF.off <= UNION + (HB * TB + KC * TB // 2), "final stage must not overlap mT"
    cF = Carve(cB.off)
    hb2 = cF.bf16(D); hb2_b = Buf("hb2")
    vecrep2 = cF.f32(D); vecrep2_b = Buf("vecrep2")
    assert max(cA.off, cP.off, cN.off, cM.off, cB.off, cF.off) <= AW, (cA.off, cP.off, cN.off, cM.off, cB.off, cF.off)

    bank_b = [Buf("bank%d" % i) for i in range(NBK)]
    bank_i = [0]

    def nb():
        i = bank_i[0] % NBK
        bank_i[0] += 1
        return banks[i], bank_b[i]

    ch_ld = [newchan("c_ld%d" % i) for i in range(4)]
    ch_st = [newchan("c_st%d" % i) for i in range(4)]
    ch_cast = [newchan("c_cast%d" % i) for i in range(4)]
    ch_sst = [newchan("c_sst%d" % i) for i in range(2)]
    ch_xq = newchan("c_xq")
    ld_i, st_i = [0], [0]

    def load(out_ap, in_ap, reads=(), writes=()):
        c = ch_ld[ld_i[0] % 4]; ld_i[0] += 1
        return rec.dma("sp", c, out_ap, in_ap, reads, writes)

    def store(out_ap, in_ap, reads=(), writes=()):
        c = ch_st[st_i[0] % 4]; st_i[0] += 1
        return rec.dma("pool", c, out_ap, in_ap, reads, writes)

    wb_b = {nm: [Buf("%s_%d" % (nm, t)) for t in range(wspecs[nm][0])] for nm in wspecs}
    for c_ in ch_cast:
        c_.in_barrier = False
    kmT_sb, vm_sb, S_sb = Buf("kmT_s"), Buf("vm_s"), [Buf("S_s%d" % h) for h in range(H)]

    retT_sb, poolT_sb, memT_sb, sig_sb = Buf("retT_s"), Buf("poolT_s"), Buf("memT_s"), Buf("sig_s")
    y_b = Buf("y")

    for (ap, src) in ((maskT, mask_d), (qdec, qdec_d), (kdec, kdec_d), (cdec, cdec_d), (rc, rc_d),
                      (pscale, pscale_d), (ident, ident_d)):
        load(ap, src, writes=[consts_b])
    ci = 0
    NCAST = 1
    order = []
    for h in range(H):
        order.append(("w_in", (cfg.OFF_K + h * 256) // WN))
        order += [("w_in", (cfg.OFF_V + h * 512) // WN + vt) for vt in range(2)]
    seen = set(order)
    for nm in ("w_mk", "w_mv", "w_pg", "w_in", "w_pp", "w_pr0", "w_pr1", "w_pm", "w_o"):
        for tl in range(wspecs[nm][0]):
            if (nm, tl) not in seen:
                order.append((nm, tl))
    for nm, tl in order:
        row = wspecs[nm][1] * wspecs[nm][2]
        cw = min(row, 4096)
        for c0 in range(0, row, cw):
            rec.dma("pool", ch_cast[ci % NCAST], wb[nm][tl // 16][tl % 16][:, c0:c0 + cw], wf[nm][tl][:, c0:c0 + cw],
                    writes=[wb_b[nm][tl]])
            ci += 1

    wt_i = [0]

    def _load_tile(nm, tidx, ap, b, chs):
        row = wspecs[nm][1] * wspecs[nm][2]
        src = wb[nm][tidx // 16][tidx % 16]
        hw = row // NSPLIT if row >= 2048 else row
        for i, c0 in enumerate(range(0, row, hw)):
            rec.dma("sp", chs[i % len(chs)], ap[:, c0:c0 + hw], src[:, c0:c0 + hw], reads=[wb_b[nm][tidx]], writes=[b])

    def load_wt(nm, tidx):
        ap, b, chs = wts[wt_i[0] % wt_depth[0]]
        wt_i[0] += 1
        _load_tile(nm, tidx, ap, b, chs)
        return ap.rearrange("p (k n) -> p k n", n=WN), b

    wg_i = [0]

    def load_wg(tidx):
        ap, b, ch = wgs[wg_i[0] % 4]
        wg_i[0] += 1
        src = wb["w_pg"][tidx // 16][tidx % 16]
        rec.dma("act", ch, ap, src, reads=[wb_b["w_pg"][tidx]], writes=[b])
        return ap.rearrange("p (k n) -> p k n", n=P), b

    def mm_group(out_ap, out_b, pairs, reads):
        n = len(pairs)
        fns = []
        for i, (l, r) in enumerate(pairs):
            fns.append((lambda l=l, r=r, i=i: (lambda e: e.matmul(out_ap, l, r, start=(i == 0), stop=(i == n - 1))))())
        return rec.op("pe", fns, reads=reads, writes=[out_b])

    def proj_fm(wt, wt_b, j, rhs3, rhs_b, ncols, kcn=None):
        kcn = kcn or KC
        ps, psb = nb()
        o = ps[:, 0:ncols]
        mm_group(o, psb, [(wt[:, kc, j * P:(j + 1) * P], rhs3[:, kc, 0:ncols]) for kc in range(kcn)],
                 reads=[wt_b, rhs_b])
        return o, psb

    def transposes(srcs, src_b):
        ptr_t, ptr_b = ptrs[ptr_i[0] % 2]
        ptr_i[0] += 1
        fns = []
        for i, s_ in enumerate(srcs):
            fns.append((lambda s_=s_, i=i: (lambda e: e.transpose(ptr_t[:, i * P:(i + 1) * P], s_, ident)))())
        rec.op("pe", fns, reads=[src_b, consts_b], writes=[ptr_b])
        return ptr_t, ptr_b

    def rstd_from_ssq(ssq_ap, ssq_b, n, out_ap, out_b):
        rec.op("dve", lambda e: e.tensor_scalar(out_ap, ssq_ap, 1.0 / n, EPS, ALU.mult, ALU.add),
               reads=[ssq_b], writes=[out_b])
        rec.op("act", lambda e: e.activation(out_ap, out_ap, AF.Sqrt), reads=[out_b], writes=[out_b])
        rec.op("dve", lambda e: e.reciprocal(out_ap, out_ap), reads=[out_b], writes=[out_b])

    alt = [0]

    def evac_copy(out_ap, out_b, in_ap, in_b):
        alt[0] += 1
        if alt[0] % 2:
            rec.op("act", lambda e: e.copy(out_ap, in_ap), reads=[in_b], writes=[out_b])
        else:
            rec.op("dve", lambda e: e.tensor_copy(out_ap, in_ap), reads=[in_b], writes=[out_b])

    def norm_tile_to_T(src_rows, dstT, dstT_b, tt, vrep, vrep_b, xt_, xt_b_, hb_, hb_b_, act_q=False):
        if act_q:
            rec.dma("act", ch_xq, xt_, src_rows, writes=[xt_b_])
        else:
            load(xt_, src_rows, writes=[xt_b_])
        ssq, ssq_b = small[:, 0:1], small_b[0]
        rs, rs_b = small[:, 1:2], small_b[1]
        rec.op("act", lambda e: e.activation(hb_, xt_, AF.Square, accum_out=ssq),
               reads=[xt_b_], writes=[hb_b_, ssq_b])
        rstd_from_ssq(ssq, ssq_b, D, rs, rs_b)
        rec.op("dve", lambda e: e.scalar_tensor_tensor(hb_, xt_, rs, vrep, ALU.mult, ALU.mult),
               reads=[xt_b_, rs_b, vrep_b], writes=[hb_b_])
        for k0 in range(0, KC, 4):
            ptr_t, ptr_b = transposes([hb_[:, (k0 + i) * P:(k0 + i + 1) * P] for i in range(4)], hb_b_)
            evac_copy(dstT[:, k0:k0 + 4, tt * P:(tt + 1) * P],
                      dstT_b, ptr_t[:, 0:4 * P].rearrange("p (k t) -> p k t", k=4), ptr_b)

    rec.barrier()

    msc = float((MB * P) ** -0.5)


    cs_cur = [None]
    cs_i = [0]

    def norm_stage(blk, xsrc=None, csrc=None, ssrc=None, prefix=False):
        xsrc = x_d if xsrc is None else xsrc
        csrc = cos_d if csrc is None else csrc
        ssrc = sin_d if ssrc is None else ssrc
        t0 = blk * TB
        if not prefix:
            load(vecrep, nin_d, writes=[vecrep_b])
        for tt in range(NT):
            norm_tile_to_T(xsrc[t0 + tt * P:t0 + (tt + 1) * P, :], hT, hT_b, tt, vecrep, vecrep_b, xt, xt_b, hb, hb_b,
                           act_q=prefix)
        if not prefix:
            rec.barrier()
        cosb_, sinb_, csb_ = css[cs_i[0] % 2]
        cs_i[0] += 1
        load(cosb_, csrc[:, t0:t0 + TB], writes=[csb_])
        load(sinb_, ssrc[:, t0:t0 + TB], writes=[csb_])
        cs_cur[0] = (cosb_, sinb_, csb_)

    def rope_pair(tile_idx, dst, dst_b):
        cosb, sinb, cs_b = cs_cur[0]
        wt, wt_b = load_wt("w_in", tile_idx)
        A, Ab = proj_fm(wt, wt_b, 0, hT, hT_b, TB)
        Bq, Bb = proj_fm(wt, wt_b, 1, hT, hT_b, TB)
        rec.op("dve", lambda e: e.tensor_tensor(t1, A, cosb, ALU.mult), reads=[Ab, cs_b], writes=[t1_b])
        rec.op("dve", lambda e: e.tensor_tensor(t2, Bq, sinb, ALU.mult), reads=[Bb, cs_b], writes=[t2_b])
        rec.op("dve", lambda e: e.tensor_tensor(dst[:, 0, :], t1, t2, ALU.subtract),
               reads=[t1_b, t2_b], writes=[dst_b])
        rec.op("dve", lambda e: e.tensor_tensor(t1, Bq, cosb, ALU.mult), reads=[Bb, cs_b], writes=[t1_b])
        rec.op("dve", lambda e: e.tensor_tensor(t2, A, sinb, ALU.mult), reads=[Ab, cs_b], writes=[t2_b])
        rec.op("dve", lambda e: e.tensor_tensor(dst[:, 1, :], t1, t2, ALU.add),
               reads=[t1_b, t2_b], writes=[dst_b])

    kT_f = kT.rearrange("p h t -> p (h t)")
    kd_f = kd.rearrange("p c d -> p (c d)")
    vv_f = vv.rearrange("p c e -> p (c e)")

    def state_update(h, c):
        St, St_b = Sts[h % 2]
        for hf in range(2):
            p2, p2b = nb()
            o2 = p2[:, 0:512]
            mm_group(o2, p2b, [(kd[:, c, hf * P:(hf + 1) * P], vv[:, c, :])], reads=[kd_b, vv_b])
            rec.op("dve", lambda e: e.scalar_tensor_tensor(
                St[:, hf, :], St[:, hf, :], cdec[:, h:h + 1], o2, ALU.mult, ALU.add),
                reads=[St_b, p2b, consts_b], writes=[St_b])

    def kv_proj(h, kTx, kTx_b, vTx, vTx_b):
        rope_pair((cfg.OFF_K + h * 256) // WN, kTx, kTx_b)
        for vt in range(2):
            wt, wt_b = load_wt("w_in", (cfg.OFF_V + h * 512) // WN + vt)
            for j in range(2):
                o, ob = proj_fm(wt, wt_b, j, hT, hT_b, TB)
                evac_copy(vTx[:, vt * 2 + j, :], vTx_b, o, ob)

    def kv_post(h, kTx, kTx_b, vTx, vTx_b):
        for c in range(NT):
            ptr_t, ptr_b = transposes([kTx[:, hf, c * P:(c + 1) * P] for hf in range(2)], kTx_b)
            rec.op("dve", lambda e: e.tensor_scalar(kd[:, c, :], ptr_t[:, 0:256], kdec[:, h:h + 1], None, ALU.mult),
                   reads=[ptr_b, consts_b], writes=[kd_b])
        for c in range(NT):
            ptr_t, ptr_b = transposes([vTx[:, eb, c * P:(c + 1) * P] for eb in range(4)], vTx_b)
            evac_copy(vv[:, c, :], vv_b, ptr_t[:, 0:512], ptr_b)

    def kv_compute(h):
        kv_proj(h, kT, kT_b, vT, vT_b)
        kv_post(h, kT, kT_b, vT, vT_b)

    def kv_prefix_post(pj, h):
        kTx, kTx_b = kTs[h % 2]
        vTx, vTx_b = vTs[h % 2]
        kv_post(h, kTx, kTx_b, vTx, vTx_b)
        St, St_b = Sts[h % 2]
        Sflat = St.rearrange("p h e -> p (h e)")
        if pj == 0:
            rec.op("dve", lambda e: e.memset(Sflat, 0.0), writes=[St_b])
        else:
            load(Sflat, S_s[h], reads=[S_sb[h]], writes=[St_b])
        for c in range(NT):
            state_update(h, c)
        rec.dma("act", ch_sst[h % 2], S_s[h], Sflat, reads=[St_b], writes=[S_sb[h]])

    def prefix_heads(pj):
        for h in range(H):
            kv_proj(h, kTs[h % 2][0], kTs[h % 2][1], vTs[h % 2][0], vTs[h % 2][1])
            if h >= 1:
                kv_prefix_post(pj, h - 1)
        kv_prefix_post(pj, H - 1)

    def ret_pass2(blk, h):
        St, St_b = Sts[h % 2]
        Sflat = St.rearrange("p h e -> p (h e)")
        kv_compute(h)
        rope_pair((cfg.OFF_Q + h * 256) // WN, qT, qT_b)
        qdv = qdec[:, h * P:(h + 1) * P].unsqueeze(1).to_broadcast([P, NT, P])
        for hf in range(2):
            rec.op("dve", lambda e: e.tensor_tensor(
                qdT[:, hf, :].rearrange("p (c i) -> p c i", c=NT),
                qT[:, hf, :].rearrange("p (c i) -> p c i", c=NT), qdv, ALU.mult),
                reads=[qT_b, consts_b], writes=[qdT_b])
        for gt in range(2):
            wt, wt_b = load_wt("w_in", (cfg.OFF_GR + h * 512) // WN + gt)
            for j in range(2):
                o, ob = proj_fm(wt, wt_b, j, hT, hT_b, TB)
                rec.op("act", lambda e: e.activation(sgT[:, gt * 2 + j, :], o, AF.Silu), reads=[ob], writes=[sgT_b])
        if NPRE == 0 and blk == 0:
            rec.op("pool", lambda e: e.memset(Sflat, 0.0), writes=[St_b])
        else:
            load(Sflat, S_s[h], reads=[S_sb[h]], writes=[St_b])
        for c in range(NT):
            cs = slice(c * P, (c + 1) * P)
            ps, psb = nb()
            sc = ps[:, 0:P]
            mm_group(sc, psb, [(kT[:, hf, cs], qT[:, hf, cs]) for hf in range(2)], reads=[kT_b, qT_b])
            rec.op("dve", lambda e: e.tensor_tensor(PT, sc, maskT[:, h * P:(h + 1) * P], ALU.mult),
                   reads=[psb, consts_b], writes=[PT_b])
            rec.op("act", lambda e: e.copy(Sb.rearrange("p h e -> p (h e)"), Sflat), reads=[St_b], writes=[Sb_b])
            po_, pob = nb()
            o = po_[:, 0:512]
            mm_group(o, pob, [(PT, vv[:, c, :])] + [(qdT[:, hf, cs], Sb[:, hf, :]) for hf in range(2)],
                     reads=[PT_b, vv_b, qdT_b, Sb_b])
            ssq, ssq_b = small[:, 2:3], small_b[2]
            rs, rs_b = small[:, 3:4], small_b[3]
            rec.op("act", lambda e: e.activation(junk, o, AF.Square, accum_out=ssq),
                   reads=[pob], writes=[junk_b, ssq_b])
            rstd_from_ssq(ssq, ssq_b, 512, rs, rs_b)
            rec.op("dve", lambda e: e.tensor_scalar(retn, o, rs, None, ALU.mult), reads=[pob, rs_b], writes=[retn_b])
            ptr_t, ptr_b = transposes([retn[:, eb * P:(eb + 1) * P] for eb in range(4)], retn_b)
            rec.op("dve", lambda e: e.tensor_tensor(
                retT[:, :, cs], ptr_t[:, 0:512].rearrange("p (b i) -> p b i", b=4), sgT[:, :, cs], ALU.mult),
                reads=[ptr_b, sgT_b], writes=[retT_b])
            if not (blk == NB - 1 and c == NT - 1):
                state_update(h, c)
        if blk < NB - 1:
            store(S_s[h], Sflat, reads=[St_b], writes=[S_sb[h]])
        store(retT_s[:, h * 4:(h + 1) * 4, :], retT, reads=[retT_b], writes=[retT_sb])

    if NPRE == 0:
        rec.op("dve", lambda e: e.memset(tails[0][0], 0.0), writes=[tails[0][1]])
    if NPRE:
        load(vecrep, nin_d, writes=[vecrep_b])
        wt_depth[0] = 3 + n_extra
    for pj in range(NPRE):
        norm_stage(pj, xp_d, cosp_d, sinp_d, prefix=True)
        prefix_heads(pj)
        if pj == NPRE - 1:
            rec.op("dve", lambda e: e.tensor_copy(tails[0][0], hT[:, :, TB - 16:TB]), reads=[hT_b], writes=[tails[0][1]])
    rec.barrier()
    wt_depth[0] = 3

    load(vecrep, nmem_d, writes=[vecrep_b])
    for mt in range(2):
        norm_tile_to_T(mem_d[mt * P:(mt + 1) * P, :], memnT, memnT_b, mt, vecrep, vecrep_b, xt, xt_b, hb, hb_b)
    for kt in range(D // WN):
        wt, wt_b = load_wt("w_mk", kt)
        for j in range(2):
            o, ob = proj_fm(wt, wt_b, j, memnT, memnT_b, MEM_LEN)
            evac_copy(stg, stg_b, o, ob)
            store(kmT_s[:, kt * 2 + j, :], stg, reads=[stg_b], writes=[kmT_sb])
    for vt in range(D // WN):
        wt, wt_b = load_wt("w_mv", vt)
        for mt in range(2):
            ps, psb = nb()
            o = ps[:, 0:WN]
            mm_group(o, psb, [(memnT[:, kc, mt * P:(mt + 1) * P], wt[:, kc, :]) for kc in range(KC)],
                     reads=[wt_b, memnT_b])
            evac_copy(stg, stg_b, o, psb)
            store(vm_s[mt][:, vt * WN:(vt + 1) * WN], stg, reads=[stg_b], writes=[vm_sb])
    rec.barrier()

    wt_depth[0] = 3 + own_extra
    for blk in range(NB):
        t0 = blk * TB
        hTt, hTt_b = tails[blk % 2]
        norm_stage(blk)

        for g in range(4):
            w = POOL_WINDOWS[g]
            for ut in range(GB * P // WN):
                wt, wt_b = load_wt("w_in", (cfg.OFF_U + g * GB * P) // WN + ut)
                for j in range(2):
                    ub_i = ut * 2 + j
                    o, ob = proj_fm(wt, wt_b, j, hT, hT_b, TB)
                    oh, ohb = proj_fm(wt, wt_b, j, hTt, hTt_b, 16)
                    ua, ua_b = ubufs[0]
                    rec.op("act", (lambda o=o: lambda e: e.copy(ua[:, 16:UW], o))(), reads=[ob], writes=[ua_b])
                    rec.op("act", (lambda oh=oh: lambda e: e.copy(ua[:, 0:16], oh))(), reads=[ohb], writes=[ua_b])
                    cur, cur_b = ua, ua_b
                    off, step, pi_ = 0, 1, 1
                    while step < w:
                        nx, nx_b = ubufs[pi_]
                        lo = off + step
                        rec.op("dve", (lambda nx=nx, cur=cur, lo=lo, step=step: lambda e: e.tensor_tensor(
                            nx[:, lo:UW], cur[:, lo:UW], cur[:, lo - step:UW - step], ALU.add))(),
                            reads=[cur_b], writes=[nx_b])
                        cur, cur_b = nx, nx_b
                        off = lo
                        step *= 2
                        pi_ = 3 - pi_
                    rec.op("dve", (lambda cur=cur, ub_i=ub_i: lambda e: e.scalar_tensor_tensor(
                        mixT[:, ub_i, :], cur[:, 16:UW], 1.0 / w, ua[:, 16:UW], ALU.mult, ALU.subtract))(),
                        reads=[cur_b, ua_b], writes=[mixT_b])
                    if blk == 0:
                        rec.op("dve", (lambda cur=cur: lambda e: e.tensor_tensor(
                            t16, cur[:, 16:32], rc[:, g * 16:(g + 1) * 16], ALU.mult))(),
                            reads=[cur_b, consts_b], writes=[t16_b])
                        rec.op("dve", (lambda ub_i=ub_i: lambda e: e.tensor_tensor(
                            mixT[:, ub_i, 0:16], t16, ua[:, 16:32], ALU.subtract))(),
                            reads=[t16_b, ua_b], writes=[mixT_b])
            for ot in range(GB * P // WN):
                wt, wt_b = load_wt("w_in", (cfg.OFF_G + g * GB * P) // WN + ot)
                for j in range(2):
                    o, ob = proj_fm(wt, wt_b, j, hT, hT_b, TB)
                    rec.op("act", (lambda o=o, j=j: lambda e: e.activation(sg2[:, j, :], o, AF.Silu))(),
                           reads=[ob], writes=[sg2_b])
                for j in range(2):
                    oblk = g * GB + ot * 2 + j
                    wg, wg_b = load_wg(oblk)
                    ps, psb = nb()
                    o = ps[:, 0:TB]
                    mm_group(o, psb, [(wg[:, kc, :], mixT[:, kc, :]) for kc in range(GB)], reads=[wg_b, mixT_b])
                    rec.op("dve", (lambda o=o, j=j, oblk=oblk: lambda e: e.scalar_tensor_tensor(
                        po[:, j, :], o, pscale[:, oblk:oblk + 1], sg2[:, j, :], ALU.mult, ALU.mult))(),
                        reads=[psb, sg2_b, consts_b], writes=[po_b])
                ob0 = g * GB + ot * 2
                store(poolT_s[:, ob0:ob0 + 2, :], po, reads=[po_b], writes=[poolT_sb])

        rec.barrier()
        for hm in range(4):
            load(kmTh, kmT_s[:, hm * MB:(hm + 1) * MB, :], reads=[kmT_sb], writes=[kmTh_b])
            load(vmh, vm_s[:, :, hm * MB * P:(hm + 1) * MB * P].rearrange("m p e -> p m e"),
                 reads=[vm_sb], writes=[vmh_b])
            for qt in range(MB * P // WN):
                wt, wt_b = load_wt("w_in", (cfg.OFF_QM + hm * MB * P) // WN + qt)
                for j in range(2):
                    o, ob = proj_fm(wt, wt_b, j, hT, hT_b, TB)
                    evac_copy(qmT[:, qt * 2 + j, :], qmT_b, o, ob)
            for c in range(NT):
                cs = slice(c * P, (c + 1) * P)
                ps, psb = nb()
                sc = ps[:, 0:MEM_LEN]
                mm_group(sc, psb, [(qmT[:, kc, cs], kmTh[:, kc, :]) for kc in range(MB)], reads=[qmT_b, kmTh_b])
                mx, mx_b = small[:, 4:5], small_b[4]
                sm, sm_b = small[:, 5:6], small_b[5]
                rec.op("dve", (lambda sc=sc: lambda e: e.reduce_max(mx, sc, AX.X))(), reads=[psb], writes=[mx_b])
                rec.op("dve", lambda e: e.tensor_scalar(mx, mx, -msc, None, ALU.mult), reads=[mx_b], writes=[mx_b])
                rec.op("act", (lambda sc=sc: lambda e: e.activation(pex, sc, AF.Exp, bias=mx, scale=msc, accum_out=sm))(),
                       reads=[psb, mx_b], writes=[pex_b, sm_b])
                rec.op("dve", lambda e: e.reciprocal(sm, sm), reads=[sm_b], writes=[sm_b])
                rec.op("dve", lambda e: e.tensor_scalar(pn, pex, sm, None, ALU.mult), reads=[pex_b, sm_b], writes=[pn_b])
                ptr_t, ptr_b = transposes([pn[:, m * P:(m + 1) * P] for m in range(2)], pn_b)
                rec.op("dve", (lambda cs=cs: lambda e: e.tensor_copy(
                    pT[:, :, cs], ptr_t[:, 0:256].rearrange("p (m i) -> p m i", m=2)))(),
                    reads=[ptr_b], writes=[pT_b])
            for dt_ in range(MB * P // WN):
                wt, wt_b = load_wt("w_in", (cfg.OFF_GM + hm * MB * P) // WN + dt_)
                for j in range(2):
                    o, ob = proj_fm(wt, wt_b, j, hT, hT_b, TB)
                    rec.op("act", (lambda o=o, j=j: lambda e: e.activation(sg2[:, j, :], o, AF.Silu))(),
                           reads=[ob], writes=[sg2_b])
                for j in range(2):
                    eb = dt_ * 2 + j
                    ps, psb = nb()
                    o = ps[:, 0:TB]
                    mm_group(o, psb, [(vmh[:, m, eb * P:(eb + 1) * P], pT[:, m, :]) for m in range(2)],
                             reads=[vmh_b, pT_b])
                    rec.op("dve", (lambda o=o, j=j: lambda e: e.tensor_tensor(po[:, j, :], o, sg2[:, j, :], ALU.mult))(),
                           reads=[psb, sg2_b], writes=[po_b])
                ob0 = hm * MB + dt_ * 2
                store(memT_s[:, ob0:ob0 + 2, :], po, reads=[po_b], writes=[memT_sb])

        for a in range(3):
            for t in range(D // WN):
                wt, wt_b = load_wt("w_in", (cfg.OFF_A + a * D) // WN + t)
                for j in range(2):
                    o, ob = proj_fm(wt, wt_b, j, hT, hT_b, TB)
                    rec.op("act", (lambda o=o, j=j: lambda e: e.activation(po[:, j, :], o, AF.Sigmoid))(),
                           reads=[ob], writes=[po_b])
                ob0 = a * KC + t * 2
                store(sig_s[:, ob0:ob0 + 2, :], po, reads=[po_b], writes=[sig_sb])
        for h in range(H):
            ret_pass2(blk, h)
        nxt, nxt_b = tails[(blk + 1) % 2]
        rec.op("pool", lambda e: e.tensor_copy(nxt, hT[:, :, TB - 16:TB]), reads=[hT_b], writes=[nxt_b])
        rec.barrier()

        pieces = [(poolT_s, poolT_sb, 0, "w_pp", 0), (retT_s, retT_sb, 0, "w_pr0", 1),
                  (retT_s, retT_sb, KC, "w_pr1", 1), (memT_s, memT_sb, 0, "w_pm", 2)]
        sg_i = 0
        for dh in range(2):
            for pi, (src, src_b, boff, wn, a) in enumerate(pieces):
                for q4 in range(4):
                    k0_, k1_ = q4 * KC // 4, (q4 + 1) * KC // 4
                    load(pin[:, k0_:k1_, :], src[:, boff + k0_:boff + k1_, :],
                         reads=[src_b], writes=[pin_b])
                for t in range(HB // 2):
                    tg = dh * (HB // 2) + t
                    wt, wt_b = load_wt(wn, tg)
                    for j in range(2):
                        ob_ = tg * 2 + j
                        lb = t * 2 + j
                        o, ob = proj_fm(wt, wt_b, j, pin, pin_b, TB)
                        sgt, sgt_b, sch = sgts[sg_i % 4]
                        sg_i += 1
                        rec.dma("act", sch, sgt, sig_s[:, a * KC + ob_, :], reads=[sig_sb], writes=[sgt_b])
                        if pi == 0:
                            rec.op("dve", (lambda o=o, lb=lb, sgt=sgt: lambda e: e.tensor_tensor(
                                acc[:, lb, :], o, sgt, ALU.mult))(), reads=[ob, sgt_b], writes=[acc_b])
                        else:
                            rec.op("dve", (lambda o=o, sgt=sgt: lambda e: e.tensor_tensor(tmpB, o, sgt, ALU.mult))(),
                                   reads=[ob, sgt_b], writes=[tmpB_b])
                            rec.op("pool", (lambda lb=lb: lambda e: e.tensor_tensor(
                                acc[:, lb, :], acc[:, lb, :], tmpB, ALU.add))(), reads=[tmpB_b, acc_b], writes=[acc_b])
            rec.op("act", (lambda dh=dh: lambda e: e.copy(
                mT[:, dh * HB:(dh + 1) * HB, :].rearrange("p b t -> p (b t)"),
                acc.rearrange("p b t -> p (b t)")))(), reads=[acc_b], writes=[mT_b])
        rec.barrier()

        wt_depth[0] = 3
        load(vecrep2, nf_d, writes=[vecrep2_b])
        for tt in range(NT):
            load(xo4[:, tt, :], x_d[t0 + tt * P:t0 + (tt + 1) * P, :], writes=[xo_b[tt]])
        for t in range(D // WN):
            wt, wt_b = load_wt("w_o", t)
            for tt in range(NT):
                ps, psb = nb()
                o = ps[:, 0:WN]
                mm_group(o, psb, [(mT[:, kc, tt * P:(tt + 1) * P], wt[:, kc, :]) for kc in range(KC)],
                         reads=[wt_b, mT_b])
                rec.op("dve", lambda e: e.tensor_tensor(
                    xo4[:, tt, t * WN:(t + 1) * WN], xo4[:, tt, t * WN:(t + 1) * WN], o, ALU.add),
                    reads=[psb, xo_b[tt]], writes=[xo_b[tt]])
        for tt in range(NT):
            ssq, ssq_b = small[:, 6:7], small_b[6]
            rs, rs_b = small[:, 7:8], small_b[7]
            rec.op("act", lambda e: e.activation(hb2, xo4[:, tt, :], AF.Square, accum_out=ssq),
                   reads=[xo_b[tt]], writes=[hb2_b, ssq_b])
            rstd_from_ssq(ssq, ssq_b, D, rs, rs_b)
            rec.op("dve", lambda e: e.scalar_tensor_tensor(xo4[:, tt, :], xo4[:, tt, :], rs, vecrep2, ALU.mult, ALU.mult),
                   reads=[xo_b[tt], rs_b, vecrep2_b], writes=[xo_b[tt]])
            store(y_d[t0 + tt * P:t0 + (tt + 1) * P, :], xo4[:, tt, :], reads=[xo_b[tt]], writes=[y_b])
        rec.barrier()
        wt_depth[0] = 3 + own_extra

    rec.final_wait("sp")

    with nc.Block() as block:
        @block.tensor
        def _(e):
            for f in rec.q["pe"]:
                f(e)

        @block.scalar
        def _(e):
            for f in rec.q["act"]:
                f(e)

        @block.vector
        def _(e):
            for f in rec.q["dve"]:
                f(e)

        @block.gpsimd
        def _(e):
            for f in rec.q["pool"]:
                f(e)

        @block.sync
        def _(e):
            for f in rec.q["sp"]:
                f(e)
    es.close()
    return nc


def tile_major(w, n):
    K, C = w.shape
    return np.ascontiguousarray(
        w.reshape(K // P, P, C // n, n).transpose(2, 1, 0, 3)).reshape(C // n, P, (K // P) * n)


def const_tables(cfg, core):
    S, H, SC = cfg.S, cfg.H, cfg.SC
    half = 128
    inv = 10000.0 ** (-np.arange(half, dtype=np.float64) / half)
    pos = np.arange(core * SC, (core + 1) * SC, dtype=np.float64)
    ang = inv[:, None] * pos[None, :]
    lg = np.log1p(-(2.0 ** (-5.0 - np.arange(H, dtype=np.float64))))
    i = np.arange(P, dtype=np.float64)
    diff = i[None, :] - i[:, None]
    maskT = np.where(diff[None] >= 0, np.exp(np.maximum(diff, 0)[None] * lg[:, None, None]), 0.0) / 16.0
    maskT = maskT.transpose(1, 0, 2).reshape(P, H * P)
    qdec = np.exp((i + 1.0)[None, :] * lg[:, None])
    qdec = np.broadcast_to(qdec.reshape(1, H * P), (P, H * P))
    kdec = (np.exp((P - 1.0 - i)[:, None] * lg[None, :]) / 16.0)
    cdec = np.broadcast_to(np.exp(P * lg)[None, :], (P, H))
    rc = np.zeros((P, 64))
    for g, w in enumerate(POOL_WINDOWS):
        cnt = np.minimum(np.arange(16) + 1, w) if core == 0 else np.full(16, w)
        rc[:, g * 16:(g + 1) * 16] = 1.0 / cnt[None, :]
    npre = max(cfg.NPRE, 1) * TB
    ppos = np.arange(core * SC - npre, core * SC, dtype=np.float64)
    pang = inv[:, None] * ppos[None, :]
    f = lambda a: np.ascontiguousarray(a, dtype=np.float32)
    return dict(cos_t=f(np.cos(ang)), sin_t=f(np.sin(ang)), maskT=f(maskT), qdec=f(qdec), kdec=f(kdec),
                cdec=f(cdec), rc=f(rc), cosp_t=f(np.cos(pang)), sinp_t=f(np.sin(pang)), ident=np.eye(P, dtype=np.float32).astype(ml_dtypes.bfloat16))


def make_inputs(cfg, x, mem, norm_in, norm_mem, w_in, w_pool_group, pool_scale, w_mem_k, w_mem_v,
                w_proj_pool, w_proj_ret, w_proj_mem, w_out, norm_f):
    D, NR, SC = cfg.D, cfg.NCORES, cfg.SC
    rep = lambda v: np.ascontiguousarray(np.broadcast_to(np.asarray(v, np.float32).reshape(1, D), (P, D)))
    common = {}
    common["mem"] = np.ascontiguousarray(np.asarray(mem, np.float32).reshape(MEM_LEN, D))
    common["norm_in_rep"] = rep(norm_in)
    common["norm_mem_rep"] = rep(norm_mem)
    common["norm_f_rep"] = rep(norm_f)
    common["pscale"] = np.ascontiguousarray(np.asarray(pool_scale, np.float32).reshape(cfg.KC, P).T)
    wt = {}
    wt["w_in"] = tile_major(np.asarray(w_in, np.float32).reshape(D, cfg.INW), WN)
    wpg = np.asarray(w_pool_group, np.float32).reshape(4, D // 4, D // 4)
    wt["w_pg"] = np.concatenate([tile_major(wpg[g], P) for g in range(4)], 0)
    wt["w_mk"] = tile_major(np.asarray(w_mem_k, np.float32).reshape(D, D), WN)
    wt["w_mv"] = tile_major(np.asarray(w_mem_v, np.float32).reshape(D, D), WN)
    wt["w_pp"] = tile_major(np.asarray(w_proj_pool, np.float32).reshape(D, D), WN)
    wpr = np.asarray(w_proj_ret, np.float32).reshape(2 * D, D)
    wt["w_pr0"] = tile_major(wpr[:D], WN)
    wt["w_pr1"] = tile_major(wpr[D:], WN)
    wt["w_pm"] = tile_major(np.asarray(w_proj_mem, np.float32).reshape(D, D), WN)
    wt["w_o"] = tile_major(np.asarray(w_out, np.float32).reshape(D, D), WN)
    xf = np.asarray(x, np.float32).reshape(cfg.S, D)
    maps = []
    for c in range(NR):
        m = dict(common)
        m.update(const_tables(cfg, c))
        m["x"] = np.ascontiguousarray(xf[c * SC:(c + 1) * SC])
        npre = max(cfg.NPRE, 1) * TB
        xp = np.zeros((npre, D), np.float32)
        if c > 0:
            n = min(npre, c * SC)
            xp[npre - n:] = xf[c * SC - n:c * SC]
        m["x_prev"] = xp
        m.update(wt)
        maps.append(m)
    return maps


def run(cfg, **inputs):
    nc = build(cfg)
    maps = make_inputs(cfg, **inputs)
    res = run_bass_kernel_spmd(nc, maps, core_ids=list(range(cfg.NCORES)))
    y = np.concatenate([res.results[c]["y"] for c in range(cfg.NCORES)], axis=0)
    return y.reshape(1, cfg.S, cfg.D).astype(np.float32)


def kernel(**inputs):
    cfg = Cfg(4096, 8192)
    return run(cfg, **inputs)
```
